# Optimizing a Trainium2 kernel written in Bass

```python
import math
import jax, jax.numpy as jnp
from jax import lax
import numpy as np

D_MODEL = 2048
BATCH = 2
SEQ = 16384
DEPTH = 2

N_MIXERS = 2
N_NSA_LAYERS = (DEPTH + N_MIXERS - 1) // N_MIXERS
N_MLA_LAYERS = DEPTH // N_MIXERS
D_FF = 5632
RMS_EPS = 1e-6
ROPE_THETA = 10000.0
NEG_INF = -1e30
FORCE_BONUS = 1e9

NSA_HEADS = 16
NSA_HEAD_DIM = 128
NSA_KV_GROUPS = 4
NSA_HEADS_PER_GROUP = NSA_HEADS // NSA_KV_GROUPS
NSA_N_BRANCHES = 3
NSA_CMP_LEN = 32
NSA_CMP_STRIDE = 16
NSA_CMP_HIDDEN = 256
NSA_SEL_BLOCK = 64
NSA_N_SELECT = 16
NSA_WINDOW = 512
NSA_Q_BLOCK = 64
NSA_Q_COLS = NSA_HEADS * NSA_HEAD_DIM
NSA_KV_COLS = NSA_N_BRANCHES * 2 * NSA_KV_GROUPS * NSA_HEAD_DIM
NSA_GATE_COLS = NSA_N_BRANCHES * NSA_HEADS
NSA_IN_COLS = NSA_Q_COLS + NSA_KV_COLS + NSA_GATE_COLS

MLA_HEADS = 16
MLA_Q_RANK = 512
MLA_KV_RANK = 512
MLA_NOPE_DIM = 128
MLA_ROPE_DIM = 64
MLA_V_DIM = 128
MLA_QK_DIM = MLA_NOPE_DIM + MLA_ROPE_DIM
MLA_IN_COLS = MLA_Q_RANK + MLA_KV_RANK + MLA_ROPE_DIM
MLA_Q_BLOCK = 128

kernel_name = 'hybrid_nsa_mla_macaron'


def rms_norm(x, w):
    x32 = x.astype(jnp.float32)
    y = x32 * lax.rsqrt(jnp.mean(x32 * x32, axis=-1, keepdims=True) + RMS_EPS)
    return (y * w.astype(jnp.float32)).astype(x.dtype)


def swiglu(h, w_in, w_out):
    g, u = jnp.split(h @ w_in, 2, axis=-1)
    return (jax.nn.silu(g) * u) @ w_out


def rope_tables(pos, dim):
    inv = 1.0 / (ROPE_THETA ** (jnp.arange(0, dim, 2, dtype=jnp.float32) / dim))
    ang = pos.astype(jnp.float32)[..., None] * inv
    return jnp.cos(ang), jnp.sin(ang)


def apply_rope(x, cos, sin):
    c = cos[:, :, None, :]
    s = sin[:, :, None, :]
    x1, x2 = jnp.split(x.astype(jnp.float32), 2, axis=-1)
    return jnp.concatenate([x1 * c - x2 * s, x2 * c + x1 * s], axis=-1).astype(x.dtype)


def nsa_mixer(h, positions, w_in, cmp_pe, cmp_w1, cmp_w2, w_out):
    B, S, _ = h.shape
    H, G, R, dk = NSA_HEADS, NSA_KV_GROUPS, NSA_HEADS_PER_GROUP, NSA_HEAD_DIM
    QB = NSA_Q_BLOCK
    dt = h.dtype
    scale = dk ** -0.5
    proj = h @ w_in
    q = proj[..., :NSA_Q_COLS].reshape(B, S, H, dk)
    kv = proj[..., NSA_Q_COLS:NSA_Q_COLS + NSA_KV_COLS].reshape(B, S, NSA_N_BRANCHES, 2, G, dk)
    gates = jax.nn.sigmoid(proj[..., NSA_Q_COLS + NSA_KV_COLS:].astype(jnp.float32)).reshape(B, S, G, R, NSA_N_BRANCHES)
    cos, sin = rope_tables(positions, dk)
    q = apply_rope(q, cos, sin).reshape(B, S, G, R, dk)
    k_cmp_tok, v_cmp_tok = kv[:, :, 0, 0], kv[:, :, 0, 1]
    k_slc, v_slc = apply_rope(kv[:, :, 1, 0], cos, sin), kv[:, :, 1, 1]
    k_win, v_win = apply_rope(kv[:, :, 2, 0], cos, sin), kv[:, :, 2, 1]

    n_cmp = (S - NSA_CMP_LEN) // NSA_CMP_STRIDE + 1
    blk_start = jnp.arange(n_cmp) * NSA_CMP_STRIDE
    cmp_idx = blk_start[:, None] + jnp.arange(NSA_CMP_LEN)[None, :]
    cmp_end = blk_start + NSA_CMP_LEN - 1

    def compress(tok, pe, w1, w2):
        blocks = tok[:, cmp_idx] + pe[None, None, :, None, :]
        blocks = blocks.transpose(0, 1, 3, 2, 4).reshape(B, n_cmp, G, NSA_CMP_LEN * dk)
        return jax.nn.gelu(blocks @ w1) @ w2

    k_cmp = compress(k_cmp_tok, cmp_pe[0], cmp_w1[0], cmp_w2[0])
    v_cmp = compress(v_cmp_tok, cmp_pe[1], cmp_w1[1], cmp_w2[1])
    cos_c, sin_c = rope_tables(positions[:, cmp_end], dk)
    k_cmp = apply_rope(k_cmp, cos_c, sin_c)

    n_sel_blk = S // NSA_SEL_BLOCK
    n_top = min(NSA_N_SELECT, n_sel_blk)
    sel_start = jnp.arange(n_sel_blk) * NSA_SEL_BLOCK
    overlap = ((blk_start[:, None] <= sel_start[None, :] + NSA_SEL_BLOCK - 1)
               & (cmp_end[:, None] >= sel_start[None, :])).astype(jnp.float32)
    k_slc_blk = k_slc.reshape(B, n_sel_blk, NSA_SEL_BLOCK, G, dk).transpose(0, 3, 1, 2, 4)
    v_slc_blk = v_slc.reshape(B, n_sel_blk, NSA_SEL_BLOCK, G, dk).transpose(0, 3, 1, 2, 4)
    b_ix = jnp.arange(B)[:, None, None, None]
    g_ix = jnp.arange(G)[None, None, :, None]
    blk_off = jnp.arange(NSA_SEL_BLOCK)
    sel_ids = jnp.arange(n_sel_blk)

    pad = ((0, 0), (NSA_WINDOW, 0), (0, 0), (0, 0))
    k_win_p = jnp.pad(k_win, pad)
    v_win_p = jnp.pad(v_win, pad)
    win_len = NSA_WINDOW + QB
    win_off = jnp.arange(win_len) - NSA_WINDOW

    def block(start):
        t = start + jnp.arange(QB)
        qb = lax.dynamic_slice_in_dim(q, start, QB, axis=1)
        gb = lax.dynamic_slice_in_dim(gates, start, QB, axis=1)
        s_c = jnp.einsum('bqgrd,bngd->bqgrn', qb, k_cmp).astype(jnp.float32) * scale
        vc = (cmp_end[None, :] <= t[:, None])[None, :, None, None, :]
        p_c = jax.nn.softmax(jnp.where(vc, s_c, NEG_INF), axis=-1) * vc
        o_c = jnp.einsum('bqgrn,bngd->bqgrd', p_c.astype(dt), v_cmp)
        imp = jnp.einsum('bqgrn,nj->bqgj', p_c, overlap)
        cur = (t // NSA_SEL_BLOCK)[:, None]
        j = sel_ids[None, :]
        forced = ((j == 0) | (j == cur) | (j == cur - 1)).astype(jnp.float32)
        future = j > cur
        imp = jnp.where(future[None, :, None, :], NEG_INF, imp + FORCE_BONUS * forced[None, :, None, :])
        top_val, top_idx = lax.top_k(imp, n_top)
        ks = k_slc_blk[b_ix, g_ix, top_idx]
        vs = v_slc_blk[b_ix, g_ix, top_idx]
        s_s = jnp.einsum('bqgrd,bqgnkd->bqgrnk', qb, ks).astype(jnp.float32) * scale
        tok = top_idx[..., None] * NSA_SEL_BLOCK + blk_off
        m_s = (top_val > 0.5 * NEG_INF)[..., None] & (tok <= t[None, :, None, None, None])
        s_s = jnp.where(m_s[:, :, :, None], s_s, NEG_INF).reshape(B, QB, G, R, n_top * NSA_SEL_BLOCK)
        p_s = jax.nn.softmax(s_s, axis=-1).reshape(B, QB, G, R, n_top, NSA_SEL_BLOCK)
        o_s = jnp.einsum('bqgrnk,bqgnkd->bqgrd', p_s.astype(dt), vs)
        kw = lax.dynamic_slice_in_dim(k_win_p, start, win_len, axis=1)
        vw = lax.dynamic_slice_in_dim(v_win_p, start, win_len, axis=1)
        kpos = start + win_off
        m_w = (kpos[None, :] <= t[:, None]) & (t[:, None] - kpos[None, :] < NSA_WINDOW) & (kpos[None, :] >= 0)
        s_w = jnp.einsum('bqgrd,bkgd->bqgrk', qb, kw).astype(jnp.float32) * scale
        p_w = jax.nn.softmax(jnp.where(m_w[None, :, None, None, :], s_w, NEG_INF), axis=-1)
        o_w = jnp.einsum('bqgrk,bkgd->bqgrd', p_w.astype(dt), vw)
        o = gb[..., 0:1] * o_c + gb[..., 1:2] * o_s + gb[..., 2:3] * o_w
        return o.astype(dt).reshape(B, QB, H * dk)

    starts = jnp.arange(S // QB) * QB
    out = lax.map(block, starts).transpose(1, 0, 2, 3).reshape(B, S, H * dk)
    return out @ w_out


def mla_mixer(h, positions, w_in, q_norm_w, kv_norm_w, w_uq, w_ukv, w_out):
    B, S, _ = h.shape
    H, QB = MLA_HEADS, MLA_Q_BLOCK
    dt = h.dtype
    proj = h @ w_in
    c_q, c_kv, k_r = jnp.split(proj, [MLA_Q_RANK, MLA_Q_RANK + MLA_KV_RANK], axis=-1)
    q = (rms_norm(c_q, q_norm_w) @ w_uq).reshape(B, S, H, MLA_QK_DIM)
    kv = (rms_norm(c_kv, kv_norm_w) @ w_ukv).reshape(B, S, H, MLA_NOPE_DIM + MLA_V_DIM)
    cos, sin = rope_tables(positions, MLA_ROPE_DIM)
    q_nope, q_rope = jnp.split(q, [MLA_NOPE_DIM], axis=-1)
    q_rope = apply_rope(q_rope, cos, sin)
    k_nope, v = jnp.split(kv, [MLA_NOPE_DIM], axis=-1)
    k_rope = apply_rope(k_r[:, :, None, :], cos, sin)[:, :, 0]
    scale = MLA_QK_DIM ** -0.5
    k_pos = jnp.arange(S)

    def block(start):
        t = start + jnp.arange(QB)
        qn = lax.dynamic_slice_in_dim(q_nope, start, QB, axis=1)
        qr = lax.dynamic_slice_in_dim(q_rope, start, QB, axis=1)
        s = (jnp.einsum('bqhd,bkhd->bhqk', qn, k_nope)
             + jnp.einsum('bqhd,bkd->bhqk', qr, k_rope)).astype(jnp.float32) * scale
        mask = k_pos[None, :] <= t[:, None]
        p = jax.nn.softmax(jnp.where(mask[None, None], s, NEG_INF), axis=-1)
        return jnp.einsum('bhqk,bkhd->bqhd', p.astype(dt), v).reshape(B, QB, H * MLA_V_DIM)

    starts = jnp.arange(S // QB) * QB
    out = lax.map(block, starts).transpose(1, 0, 2, 3).reshape(B, S, H * MLA_V_DIM)
    return out @ w_out


def setup_inputs(seed: int = 0) -> dict:
    key = jax.random.key(seed)
    ks = jax.random.split(key, 18)

    def normal(k, shape, fan_in):
        return jax.random.normal(k, shape, jnp.float32) * (fan_in ** -0.5)

    def gain(k, shape):
        return 1.0 + 0.05 * jax.random.normal(k, shape, jnp.float32)

    return {
        'x': jax.random.normal(ks[0], (BATCH, SEQ, D_MODEL), jnp.float32),
        'positions': jnp.tile(jnp.arange(SEQ, dtype=jnp.int32)[None, :], (BATCH, 1)),
        'ffn_norm_w': gain(ks[1], (DEPTH, 2, D_MODEL)),
        'ffn_w_in': normal(ks[2], (DEPTH, 2, D_MODEL, 2 * D_FF), D_MODEL),
        'ffn_w_out': normal(ks[3], (DEPTH, 2, D_FF, D_MODEL), D_FF),
        'mix_norm_w': gain(ks[4], (DEPTH, D_MODEL)),
        'nsa_w_in': normal(ks[5], (N_NSA_LAYERS, D_MODEL, NSA_IN_COLS), D_MODEL),
        'nsa_cmp_pe': 0.1 * jax.random.normal(ks[6], (N_NSA_LAYERS, 2, NSA_CMP_LEN, NSA_HEAD_DIM), jnp.float32),
        'nsa_cmp_w1': normal(ks[7], (N_NSA_LAYERS, 2, NSA_CMP_LEN * NSA_HEAD_DIM, NSA_CMP_HIDDEN), NSA_CMP_LEN * NSA_HEAD_DIM),
        'nsa_cmp_w2': normal(ks[8], (N_NSA_LAYERS, 2, NSA_CMP_HIDDEN, NSA_HEAD_DIM), NSA_CMP_HIDDEN),
        'nsa_w_out': normal(ks[9], (N_NSA_LAYERS, NSA_HEADS * NSA_HEAD_DIM, D_MODEL), NSA_HEADS * NSA_HEAD_DIM),
        'mla_w_in': normal(ks[10], (N_MLA_LAYERS, D_MODEL, MLA_IN_COLS), D_MODEL),
        'mla_q_norm_w': gain(ks[11], (N_MLA_LAYERS, MLA_Q_RANK)),
        'mla_kv_norm_w': gain(ks[12], (N_MLA_LAYERS, MLA_KV_RANK)),
        'mla_w_uq': normal(ks[13], (N_MLA_LAYERS, MLA_Q_RANK, MLA_HEADS * MLA_QK_DIM), MLA_Q_RANK),
        'mla_w_ukv': normal(ks[14], (N_MLA_LAYERS, MLA_KV_RANK, MLA_HEADS * (MLA_NOPE_DIM + MLA_V_DIM)), MLA_KV_RANK),
        'mla_w_out': normal(ks[15], (N_MLA_LAYERS, MLA_HEADS * MLA_V_DIM, D_MODEL), MLA_HEADS * MLA_V_DIM),
        'final_norm_w': gain(ks[16], (D_MODEL,)),
    }


def reference(x, positions, ffn_norm_w, ffn_w_in, ffn_w_out, mix_norm_w,
              nsa_w_in, nsa_cmp_pe, nsa_cmp_w1, nsa_cmp_w2, nsa_w_out,
              mla_w_in, mla_q_norm_w, mla_kv_norm_w, mla_w_uq, mla_w_ukv, mla_w_out,
              final_norm_w):
    for i in range(DEPTH):
        x = x + 0.5 * swiglu(rms_norm(x, ffn_norm_w[i, 0]), ffn_w_in[i, 0], ffn_w_out[i, 0])
        h = rms_norm(x, mix_norm_w[i])
        j = i // N_MIXERS
        if i % N_MIXERS == 0:
            x = x + nsa_mixer(h, positions, nsa_w_in[j], nsa_cmp_pe[j], nsa_cmp_w1[j], nsa_cmp_w2[j], nsa_w_out[j])
        else:
            x = x + mla_mixer(h, positions, mla_w_in[j], mla_q_norm_w[j], mla_kv_norm_w[j],
                              mla_w_uq[j], mla_w_ukv[j], mla_w_out[j])
        x = x + 0.5 * swiglu(rms_norm(x, ffn_norm_w[i, 1]), ffn_w_in[i, 1], ffn_w_out[i, 1])
    return rms_norm(x, final_norm_w)
```

```python
import numpy as np
import ml_dtypes
from contextlib import ExitStack
import concourse.bass as bass
import concourse.mybir as mybir
from concourse.bass_utils import run_bass_kernel_spmd

F32 = mybir.dt.float32
BF16 = mybir.dt.bfloat16
I32 = mybir.dt.int32
AF = mybir.ActivationFunctionType
ALU = mybir.AluOpType
AX = mybir.AxisListType


class Buf:
    def __init__(self, t, name, dma_sem=None):
        self.t = t
        self.name = name
        self.w = None
        self.r = []
        self.dma_ent = dma_sem

    def __getitem__(self, idx):
        return self.t[idx]


CC_INC = 1


class Ctx:
    ENG = ("pe", "dve", "act", "pool", "sp")

    def __init__(self, nc, stack):
        self.nc = nc
        self.stack = stack
        self.root_stack = stack
        self.streams = {e: [] for e in self.ENG}
        self.sem = {}
        self.count = {e: 0 for e in self.ENG}
        self.seen = {e: {} for e in self.ENG}
        self.pending = {e: False for e in self.ENG}
        for e in self.ENG:
            self.sem[e] = stack.enter_context(nc.semaphore("sem_" + e))
        self.n_dma_sems = 0
        self.uid = 0
        self.free_dsems = []
        self.scope_dsems = []

    def _name(self, name):
        self.uid += 1
        return "%s_%d" % (name, self.uid)

    def new_dma_sem(self):
        if self.free_dsems:
            ent = self.free_dsems.pop()
        else:
            self.n_dma_sems += 1
            sem = self.root_stack.enter_context(self.nc.semaphore("dsem_%d" % self.n_dma_sems))
            ent = [sem, 0, "d%d" % self.n_dma_sems]
        self.scope_dsems.append(ent)
        return ent

    def barrier(self):
        for e in self.ENG:
            assert not self.pending[e], "engine %s has un-signalled trailing ops" % e
        for e in self.ENG:
            for o in self.ENG:
                if o != e and self.count[o] > 0:
                    self._wait(e, (o, self.sem[o], self.count[o]))
            for ent in self.scope_dsems:
                if ent[1] > 0:
                    self._wait(e, (ent[2], ent[0], ent[1]))

    def scope(self):
        cx = self

        class _Scope:
            def __enter__(s):
                s.saved_stack = cx.stack
                s.saved_dsems = cx.scope_dsems
                s.es = ExitStack()
                s.es.__enter__()
                cx.stack = s.es
                cx.scope_dsems = []
                return s

            def __exit__(s, *a):
                cx.barrier()
                cx.free_dsems.extend(cx.scope_dsems)
                cx.scope_dsems = s.saved_dsems
                cx.stack = s.saved_stack
                s.es.__exit__(None, None, None)
                return False

        return _Scope()

    def sbuf(self, name, shape, dt, dma=False):
        t = self.stack.enter_context(self.nc.sbuf_tensor(self._name(name), list(shape), dt))
        return Buf(t, name, self.new_dma_sem() if dma else None)

    def psum(self, name, shape, dt):
        t = self.stack.enter_context(self.nc.psum_tensor(self._name(name), list(shape), dt))
        return Buf(t, name)

    def dram_buf(self, ap, name):
        return Buf(ap, name, self.new_dma_sem())

    def _wait(self, eng, tok, force=False):
        if tok is None:
            return
        sem_key, sem, val = tok
        if sem_key == eng and not force:
            return
        if sem_key == eng and val > self.count[eng]:
            return
        if self.seen[eng].get(sem_key, 0) >= val:
            return
        self.seen[eng][sem_key] = val
        self.streams[eng].append(("wait", sem, val))

    def op(self, eng, fn, reads=(), writes=(), inc=True, scalars=()):
        for b in scalars:
            self._wait(eng, b.w, force=True)
        reads = list(reads) + list(scalars)
        for b in reads:
            self._wait(eng, b.w)
        for b in writes:
            self._wait(eng, b.w)
            for t in b.r:
                self._wait(eng, t)
        if inc:
            self.count[eng] += 1
            tok = (eng, self.sem[eng], self.count[eng])
            self.pending[eng] = False
        else:
            tok = (eng, self.sem[eng], self.count[eng] + 1)
            self.pending[eng] = True
        self.streams[eng].append(("op", fn, self.sem[eng] if inc else None))
        for b in writes:
            b.w = tok
            b.r = []
        for b in reads:
            if b in writes:
                continue
            b.r = [t for t in b.r if t[0] != eng] + [tok]

    def dma(self, q, fn, src, dst, n=1):
        if src is not None:
            self._wait(q, src.w)
        if dst is not None:
            self._wait(q, dst.w)
            for t in dst.r:
                self._wait(q, t)
        holder = dst if (dst is not None and dst.dma_ent is not None) else src
        assert holder is not None and holder.dma_ent is not None, "dma needs a Buf with dma_sem"
        ent = holder.dma_ent
        ent[1] += 16 * n
        tok = (ent[2], ent[0], ent[1])
        self.streams[q].append(("dma", fn, ent[0]))
        if dst is not None:
            dst.w = tok
            dst.r = []
        if src is not None:
            src.r = src.r + [tok]

    def cc(self, fn, src, dst):
        q = "pool"
        self._wait(q, src.w)
        self._wait(q, dst.w)
        for t in dst.r:
            self._wait(q, t)
        ent = dst.dma_ent
        ent[1] += CC_INC
        tok = (ent[2], ent[0], ent[1])
        self.streams[q].append(("cc", fn, ent[0]))
        dst.w = tok
        dst.r = []
        src.r = src.r + [tok]

    def load_rank(self, eng, rank_ap):
        holder = {}

        def fn(e):
            reg = e.alloc_register("rank_reg_%s" % eng)
            e.reg_load(reg, rank_ap[0:1, 0:1])
            holder["v"] = e.snap(reg, min_val=0, max_val=3)
            return None

        self.streams[eng].append(("raw", fn))
        return lambda: holder["v"]

    def wait_all(self, eng, bufs):
        for b in bufs:
            self._wait(eng, b.w)
            for t in b.r:
                self._wait(eng, t)

    def emit(self):
        nc = self.nc
        hmap = {"pe": "tensor", "dve": "vector", "act": "scalar", "pool": "gpsimd", "sp": "sync"}
        with nc.Block() as block:
            for ename in self.ENG:
                stream = self.streams[ename]

                def body(e, stream=stream):
                    for item in stream:
                        if item[0] == "wait":
                            e.wait_ge(item[1], item[2])
                        elif item[0] == "op":
                            ins = item[1](e)
                            if item[2] is not None:
                                ins.then_inc(item[2], 1)
                        elif item[0] == "raw":
                            item[1](e)
                        elif item[0] == "cc":
                            item[1](e).then_inc(item[2], CC_INC)
                        else:
                            lst = item[1](e)
                            for ins in lst:
                                ins.then_inc(item[2], 16)

                getattr(block, hmap[ename])(body)


class Common:
    def __init__(self, cx, consts_ap):
        self.cx = cx
        nc = cx.nc
        self.ident_f = cx.sbuf("ident_f", [128, 128], F32, dma=True)
        self.ident = cx.sbuf("ident", [128, 128], BF16)
        cx.dma("sp", lambda e: [e.dma_start(out=self.ident_f[:, :], in_=consts_ap[:, 0:128])], None, self.ident_f)
        cx.op("dve", lambda e: e.tensor_copy(out=self.ident[:, :], in_=self.ident_f[:, :]), [self.ident_f], [self.ident])
        self.eps = cx.sbuf("eps", [128, 1], F32)
        cx.op("dve", lambda e: e.memset(self.eps[:, :], 1e-6), [], [self.eps])
        cx.eps_tile = self.eps


def eps_ap(cx):
    return cx.eps_tile[:, 0:1]


def sincos_reduce(cx, t, shift, shape):
    P, N = shape
    ki = cx.sbuf("sc_ki", [P, N], I32)
    kf = cx.sbuf("sc_kf", [P, N], F32)
    TWO_PI = 2.0 * np.pi
    C1 = 6.28125
    C2 = TWO_PI - C1
    if shift != 0.0:
        cx.op("dve", lambda e: e.tensor_scalar(out=t[:, :], in0=t[:, :], scalar1=float(shift), scalar2=None, op0=ALU.add), [t], [t])
    cx.op("dve", lambda e: e.tensor_scalar(out=ki[:, :], in0=t[:, :], scalar1=float(1.0 / TWO_PI), scalar2=None, op0=ALU.mult), [t], [ki])
    cx.op("dve", lambda e: e.tensor_copy(out=kf[:, :], in_=ki[:, :]), [ki], [kf])
    cx.op("dve", lambda e: e.scalar_tensor_tensor(out=t[:, :], in0=kf[:, :], scalar=-C1, in1=t[:, :], op0=ALU.mult, op1=ALU.add), [kf, t], [t])
    cx.op("dve", lambda e: e.scalar_tensor_tensor(out=t[:, :], in0=kf[:, :], scalar=-C2, in1=t[:, :], op0=ALU.mult, op1=ALU.add), [kf, t], [t])
    cx.op("dve", lambda e: e.tensor_scalar(out=kf[:, :], in0=t[:, :], scalar1=float(-np.pi), scalar2=None, op0=ALU.is_lt), [t], [kf])
    cx.op("dve", lambda e: e.scalar_tensor_tensor(out=t[:, :], in0=kf[:, :], scalar=TWO_PI, in1=t[:, :], op0=ALU.mult, op1=ALU.add), [kf, t], [t])
    cx.op("dve", lambda e: e.tensor_scalar(out=kf[:, :], in0=t[:, :], scalar1=float(np.pi), scalar2=None, op0=ALU.is_gt), [t], [kf])
    cx.op("dve", lambda e: e.scalar_tensor_tensor(out=t[:, :], in0=kf[:, :], scalar=-TWO_PI, in1=t[:, :], op0=ALU.mult, op1=ALU.add), [kf, t], [t])
    cx.op("dve", lambda e: e.tensor_scalar(out=t[:, :], in0=t[:, :], scalar1=float(np.pi), scalar2=float(-np.pi), op0=ALU.min, op1=ALU.max), [t], [t])


def rmsnorm_to_hT(cx, cm, x_tile, w_rep, hT, col0, D, eps, scratch):
    junk, ss, rstd, hbf = scratch["junk"], scratch["ss"], scratch["rstd"], scratch["hbf"]
    KC = D // 128
    cx.op("act", lambda e: e.activation(out=junk[:, :], in_=x_tile[:, 0:D], func=AF.Square, accum_out=ss[:, 0:1]),
          [x_tile], [junk, ss])
    cx.op("act", lambda e: e.activation(out=rstd[:, 0:1], in_=ss[:, 0:1], func=AF.Sqrt, scale=1.0 / D, bias=eps_ap(cx)),
          [ss], [rstd], scalars=[cx.eps_tile])
    cx.op("dve", lambda e: e.reciprocal(out=rstd[:, 0:1], in_=rstd[:, 0:1]), [rstd], [rstd])
    cx.op("dve", lambda e: e.scalar_tensor_tensor(out=hbf[:, 0:D], in0=x_tile[:, 0:D], scalar=rstd[:, 0:1],
                                                   in1=w_rep[:, 0:D], op0=ALU.mult, op1=ALU.mult),
          [x_tile, w_rep], [hbf], scalars=[rstd])
    transpose_to_fm(cx, cm, hbf, hT, col0, KC, scratch)


def transpose_to_fm(cx, cm, src, dstT, col0, KC, scratch):
    tps = scratch["tp"]
    for g0 in range(0, KC, 4):
        g = min(4, KC - g0)
        tp = tps[scratch["tpi"][0] % len(tps)]
        scratch["tpi"][0] += 1
        for i in range(g):
            kc = g0 + i
            cx.op("pe", lambda e, kc=kc, i=i, tp=tp: e.transpose(out=tp[:, i * 128:(i + 1) * 128],
                                                                  in_=src[:, kc * 128:(kc + 1) * 128],
                                                                  identity=cm.ident[:, :]),
                  [src, cm.ident], [tp], inc=(i == g - 1))
        cx.op("act", lambda e, g0=g0, g=g, tp=tp: e.copy(
            out=dstT[:, g0:g0 + g, col0:col0 + 128],
            in_=tp[:, 0:g * 128].rearrange("p (a b) -> p a b", a=g)), [tp], [dstT])


def ffn_phase(cx, cm, x_in, x_out, w_rep_ap, w_in, w_out, NT, D, F, eps, T=1024):
    nc = cx.nc
    KC = D // 128
    FC = F // 128
    NB = T // 128
    assert NT % T == 0 and FC % 2 == 0 and D % 256 == 0
    SLABW = max(32 * 256, FC * 256)
    w_rep = cx.sbuf("w_rep", [128, D], F32, dma=True)
    cx.dma("sp", lambda e: [e.dma_start(out=w_rep[:, :], in_=w_rep_ap[:, :])], None, w_rep)
    xts = [cx.sbuf("xt", [128, D], F32, dma=True) for _ in range(1)]
    hbf_ = cx.sbuf("hbf", [128, D], BF16)
    scratch = dict(junk=hbf_, ss=cx.sbuf("ss", [128, 1], F32),
                   rstd=cx.sbuf("rstd", [128, 1], F32), hbf=hbf_,
                   tp=[cx.psum("tp", [128, 512], BF16) for _ in range(2)], tpi=[0])
    hT = cx.sbuf("hT", [128, KC, T], BF16)
    hid = cx.sbuf("hid", [128, FC, T], BF16)
    slabs = [cx.sbuf("slab", [128, FC * 256], BF16, dma=True) for _ in range(2)]
    assert FC * 256 >= 2 * KC * 256
    slab_i = [0]
    pg = [cx.psum("pg", [128, 512], F32) for _ in range(2)]
    pu = [cx.psum("pu", [128, 512], F32) for _ in range(2)]
    py = [cx.psum("py", [128, 512], F32) for _ in range(2)]
    sg = [cx.sbuf("sg", [128, 512], BF16) for _ in range(2)]
    xs = [cx.sbuf("xs", [128, 256], F32, dma=True) for _ in range(3)]
    ys = [cx.sbuf("ys", [128, 256], F32, dma=True) for _ in range(3)]
    cnt = [0, 0]

    def next_slab(big=False):
        n = 2
        s = slabs[slab_i[0] % n]
        slab_i[0] += 1
        return s

    for st in range(NT // T):
        t0 = st * T
        for b in range(NB):
            xt = xts[0]
            r0 = t0 + b * 128
            cx.dma("sp", lambda e, xt=xt, r0=r0: [e.dma_start(out=xt[:, :], in_=x_in[r0:r0 + 128, :])], None, xt)
            rmsnorm_to_hT(cx, cm, xt, w_rep, hT, b * 128, D, eps, scratch)
        for jj in range(FC // 2):
            slab = next_slab()
            sv = slab[:, 0:2 * KC * 256].rearrange("p (a b) -> p a b", b=256)

            def ld(e, sv=sv, jj=jj):
                a = e.dma_start(out=sv[:, 0:KC, :],
                                in_=w_in[:, jj * 256:(jj + 1) * 256].rearrange("(kc p) n -> p kc n", p=128))
                b_ = e.dma_start(out=sv[:, KC:2 * KC, :],
                                 in_=w_in[:, F + jj * 256:F + (jj + 1) * 256].rearrange("(kc p) n -> p kc n", p=128))
                return [a, b_]

            cx.dma("pool", ld, None, slab, n=2)
            for j in range(2):
                c = jj * 2 + j
                for tt in range(T // 512):
                    k = cnt[0] % 2
                    cnt[0] += 1
                    for kc in range(KC):
                        cx.op("pe", lambda e, kc=kc, j=j, tt=tt, k=k, sv=sv: e.matmul(
                            pg[k][:, :], lhsT=sv[:, kc, j * 128:(j + 1) * 128], rhs=hT[:, kc, tt * 512:(tt + 1) * 512],
                            start=(kc == 0), stop=(kc == KC - 1)), [slab, hT], [pg[k]], inc=(kc == KC - 1))
                    for kc in range(KC):
                        cx.op("pe", lambda e, kc=kc, j=j, tt=tt, k=k, sv=sv: e.matmul(
                            pu[k][:, :], lhsT=sv[:, KC + kc, j * 128:(j + 1) * 128],
                            rhs=hT[:, kc, tt * 512:(tt + 1) * 512],
                            start=(kc == 0), stop=(kc == KC - 1)), [slab, hT], [pu[k]], inc=(kc == KC - 1))
                    cx.op("act", lambda e, k=k: e.activation(out=sg[k][:, :], in_=pg[k][:, :], func=AF.Silu),
                          [pg[k]], [sg[k]])
                    cx.op("dve", lambda e, k=k, c=c, tt=tt: e.tensor_tensor(
                        out=hid[:, c, tt * 512:(tt + 1) * 512], in0=sg[k][:, :], in1=pu[k][:, :], op=ALU.mult),
                        [sg[k], pu[k]], [hid])
        for nt in range(D // 256):
            slab = next_slab(big=True)
            sv = slab[:, 0:FC * 256].rearrange("p (a b) -> p a b", b=256)
            cx.dma("pool", lambda e, sv=sv, nt=nt: [e.dma_start(
                out=sv[:, :, :], in_=w_out[:, nt * 256:(nt + 1) * 256].rearrange("(c p) n -> p c n", p=128))],
                None, slab)
            for b in range(NB):
                k = cnt[1] % 2
                k4 = cnt[1] % 3
                cnt[1] += 1
                r0 = t0 + b * 128
                xsb, ysb = xs[k4], ys[k4]
                cx.dma("sp", lambda e, xsb=xsb, r0=r0, nt=nt: [e.dma_start(
                    out=xsb[:, :], in_=x_in[r0:r0 + 128, nt * 256:(nt + 1) * 256])], None, xsb)
                for c in range(FC):
                    cx.op("pe", lambda e, c=c, b=b, k=k, sv=sv: e.matmul(
                        py[k][:, 0:256], lhsT=hid[:, c, b * 128:(b + 1) * 128], rhs=sv[:, c, :],
                        start=(c == 0), stop=(c == FC - 1)), [slab, hid], [py[k]], inc=(c == FC - 1))
                cx.op("dve", lambda e, k=k, xsb=xsb, ysb=ysb: e.scalar_tensor_tensor(
                    out=ysb[:, :], in0=py[k][:, 0:256], scalar=0.5, in1=xsb[:, :], op0=ALU.mult, op1=ALU.add),
                    [py[k], xsb], [ysb])
                cx.dma("sp", lambda e, ysb=ysb, r0=r0, nt=nt: [e.dma_start(
                    out=x_out[r0:r0 + 128, nt * 256:(nt + 1) * 256], in_=ysb[:, :])], ysb, None)
    return ys


def finish(cx, out_bufs):
    cx.wait_all("sp", out_bufs)


def consts_array():
    c = np.zeros((128, 128), np.float32)
    c[:, 0:128] = np.eye(128, dtype=np.float32)
    return c


def outproj_phase(cx, cm, oT, w, x_in, x_out, NT, K, D, coef=1.0, T=1024):
    KC = K // 128
    NB = T // 128
    o_sb = cx.sbuf("o_sb", [128, KC * T], BF16, dma=True)
    slabs = [cx.sbuf("oslab", [128, KC * 256], BF16, dma=True) for _ in range(2)]
    py = [cx.psum("py", [128, 512], F32) for _ in range(2)]
    xs = [cx.sbuf("xs", [128, 256], F32, dma=True) for _ in range(3)]
    ys = [cx.sbuf("ys", [128, 256], F32, dma=True) for _ in range(3)]
    cnt = 0
    si = 0
    for st in range(NT // T):
        t0 = st * T
        cx.dma("sp", lambda e, t0=t0: [e.dma_start(out=o_sb[:, c * T:(c + 1) * T], in_=oT(c, t0, T)) for c in range(KC)], None, o_sb, n=KC)
        for nt in range(D // 256):
            slab = slabs[si % 2]
            si += 1
            sv = slab[:, 0:KC * 256].rearrange("p (a b) -> p a b", b=256)
            cx.dma("pool", lambda e, sv=sv, nt=nt: [e.dma_start(
                out=sv[:, :, :], in_=w[:, nt * 256:(nt + 1) * 256].rearrange("(c p) n -> p c n", p=128))],
                None, slab)
            for b in range(NB):
                k = cnt % 2
                k3 = cnt % 3
                cnt += 1
                r0 = t0 + b * 128
                xsb, ysb = xs[k3], ys[k3]
                cx.dma("sp", lambda e, xsb=xsb, r0=r0, nt=nt: [e.dma_start(
                    out=xsb[:, :], in_=x_in[r0:r0 + 128, nt * 256:(nt + 1) * 256])], None, xsb)
                for c in range(KC):
                    cx.op("pe", lambda e, c=c, b=b, k=k, sv=sv: e.matmul(
                        py[k][:, 0:256], lhsT=o_sb[:, c * T + b * 128:c * T + (b + 1) * 128], rhs=sv[:, c, :],
                        start=(c == 0), stop=(c == KC - 1)), [slab, o_sb], [py[k]], inc=(c == KC - 1))
                cx.op("dve", lambda e, k=k, xsb=xsb, ysb=ysb: e.scalar_tensor_tensor(
                    out=ysb[:, :], in0=py[k][:, 0:256], scalar=float(coef), in1=xsb[:, :], op0=ALU.mult, op1=ALU.add),
                    [py[k], xsb], [ysb])
                cx.dma("sp", lambda e, ysb=ysb, r0=r0, nt=nt: [e.dma_start(
                    out=x_out[r0:r0 + 128, nt * 256:(nt + 1) * 256], in_=ysb[:, :])], ysb, None)


def final_norm_phase(cx, cm, x_in, x_out, w_rep_ap, NT, D):
    w_rep = cx.sbuf("fw_rep", [128, D], F32, dma=True)
    cx.dma("sp", lambda e: [e.dma_start(out=w_rep[:, :], in_=w_rep_ap[:, :])], None, w_rep)
    xts = [cx.sbuf("fxt", [128, D], F32, dma=True) for _ in range(2)]
    yts = [cx.sbuf("fyt", [128, D], F32, dma=True) for _ in range(2)]
    junk = cx.sbuf("fjunk", [128, D], BF16)
    ss = cx.sbuf("fss", [128, 1], F32)
    rstd = cx.sbuf("frstd", [128, 1], F32)
    for b in range(NT // 128):
        xt, yt = xts[b % 2], yts[b % 2]
        cx.dma("sp", lambda e, xt=xt, b=b: [e.dma_start(out=xt[:, :], in_=x_in[b * 128:(b + 1) * 128, :])], None, xt)
        cx.op("act", lambda e, xt=xt: e.activation(out=junk[:, :], in_=xt[:, :], func=AF.Square, accum_out=ss[:, 0:1]),
              [xt], [junk, ss])
        cx.op("act", lambda e: e.activation(out=rstd[:, 0:1], in_=ss[:, 0:1], func=AF.Sqrt, scale=1.0 / D, bias=eps_ap(cx)),
              [ss], [rstd], scalars=[cx.eps_tile])
        cx.op("dve", lambda e: e.reciprocal(out=rstd[:, 0:1], in_=rstd[:, 0:1]), [rstd], [rstd])
        cx.op("dve", lambda e, xt=xt, yt=yt: e.scalar_tensor_tensor(
            out=yt[:, :], in0=xt[:, :], scalar=rstd[:, 0:1], in1=w_rep[:, :], op0=ALU.mult, op1=ALU.mult),
            [xt, w_rep], [yt], scalars=[rstd])
        cx.dma("sp", lambda e, yt=yt, b=b: [e.dma_start(out=x_out[b * 128:(b + 1) * 128, :], in_=yt[:, :])], yt, None)


def rope_tables(cx, pos_ap, invsign, P, N):
    C = cx.sbuf("rp_C", [P, N], F32)
    S = cx.sbuf("rp_S", [P, N], F32)
    with cx.scope():
        pi_ = cx.sbuf("rp_pi", [P, N], I32, dma=True)
        cx.dma("sp", lambda e: [e.dma_start(out=pi_[:, :], in_=pos_ap[0:P, 0:N])], None, pi_)
        ang = cx.sbuf("rp_ang", [P, N], F32)
        t = cx.sbuf("rp_t", [P, N], F32)
        cx.op("dve", lambda e: e.tensor_copy(out=ang[:, :], in_=pi_[:, :]), [pi_], [ang])
        cx.op("dve", lambda e: e.tensor_scalar(out=ang[:, :], in0=ang[:, :], scalar1=invsign[0:P, 0:1], scalar2=None,
                                                op0=ALU.mult), [ang], [ang], scalars=[invsign])
        cx.op("dve", lambda e: e.tensor_copy(out=t[:, :], in_=ang[:, :]), [ang], [t])
        sincos_reduce(cx, t, 0.0, [P, N])
        cx.op("act", lambda e: e.activation(out=S[:, :], in_=t[:, :], func=AF.Sin), [t], [S])
        cx.op("dve", lambda e: e.tensor_scalar(out=S[:, :], in0=S[:, :], scalar1=invsign[0:P, 1:2], scalar2=None,
                                                op0=ALU.mult), [S], [S], scalars=[invsign])
        sincos_reduce(cx, ang, float(np.pi / 2), [P, N])
        cx.op("act", lambda e: e.activation(out=C[:, :], in_=ang[:, :], func=AF.Sin), [ang], [C])
    return C, S


def load_invsign(cx, ap):
    b = cx.sbuf("invsign", [128, 2], F32, dma=True)
    cx.dma("sp", lambda e: [e.dma_start(out=b[:, :], in_=ap[:, :])], None, b)
    return b


def invsign_array(dim):
    half = dim // 2
    inv = (1.0 / (10000.0 ** (np.arange(0, dim, 2, dtype=np.float32) / np.float32(dim)))).astype(np.float32)
    a = np.zeros((128, 2), np.float32)
    for d in range(dim):
        a[d, 0] = inv[d % half]
        a[d, 1] = -1.0 if d < half else 1.0
    return a


class ProjWS:
    def __init__(self, cx, D, T, n_slab_elems):
        KC = D // 128
        self.KC, self.T, self.D = KC, T, D
        self.w_rep = cx.sbuf("pw_rep", [128, D], F32, dma=True)
        self.xt = cx.sbuf("pxt", [128, D], F32, dma=True)
        hbf = cx.sbuf("phbf", [128, D], BF16)
        self.scratch = dict(junk=hbf, ss=cx.sbuf("pss", [128, 1], F32), rstd=cx.sbuf("prstd", [128, 1], F32), hbf=hbf,
                            tp=[cx.psum("ptp", [128, 512], BF16) for _ in range(2)], tpi=[0])
        self.hT = cx.sbuf("phT", [128, KC, T], BF16)
        self.slabs = [cx.sbuf("pslab", [128, n_slab_elems], BF16, dma=True) for _ in range(2)]
        self.si = 0
        self.pa = [cx.psum("ppa", [128, 512], F32) for _ in range(2)]
        self.pb = [cx.psum("ppb", [128, 512], F32) for _ in range(2)]
        self.pi = 0
        self.t1 = [cx.sbuf("pt1", [128, 512], F32) for _ in range(2)]
        self.t2 = [cx.sbuf("pt2", [128, 512], F32) for _ in range(2)]
        self.ob = [cx.sbuf("pob", [128, 512], BF16, dma=True) for _ in range(3)]
        self.oi = 0

    def slab(self):
        s = self.slabs[self.si % 2]
        self.si += 1
        return s

    def out_buf(self):
        o = self.ob[self.oi % 3]
        self.oi += 1
        return o


def load_wcols(cx, slab, KC, W, w_ap, pieces):
    sv = slab[:, 0:KC * W].rearrange("p (a b) -> p a b", b=W)

    def ld(e):
        out = []
        for (dc, sc, wd) in pieces:
            out.append(e.dma_start(out=sv[:, :, dc:dc + wd],
                                   in_=w_ap[:, sc:sc + wd].rearrange("(kc p) n -> p kc n", p=128)))
        return out

    cx.dma("pool", ld, None, slab, n=len(pieces))
    return sv


def fm_mm(cx, ps, M, sv, c0, slab, xT, xbuf, cols, KC):
    n = cols[1] - cols[0]
    for kc in range(KC):
        cx.op("pe", lambda e, kc=kc: e.matmul(ps[0:M, 0:n], lhsT=sv[:, kc, c0:c0 + M], rhs=xT[:, kc, cols[0]:cols[1]],
                                              start=(kc == 0), stop=(kc == KC - 1)),
              [slab, xbuf], [ps], inc=(kc == KC - 1))


def tm_mm(cx, ps, N, sv, c0, slab, xT, xbuf, tok0, KC):
    for kc in range(KC):
        cx.op("pe", lambda e, kc=kc: e.matmul(ps[:, 0:N], lhsT=xT[:, kc, tok0:tok0 + 128], rhs=sv[:, kc, c0:c0 + N],
                                              start=(kc == 0), stop=(kc == KC - 1)),
              [slab, xbuf], [ps], inc=(kc == KC - 1))


def norm_block_to_hT(cx, cm, ws, x_in, r0, col0, eps=1e-6):
    cx.dma("sp", lambda e: [e.dma_start(out=ws.xt[:, :], in_=x_in[r0:r0 + 128, :])], None, ws.xt)
    rmsnorm_to_hT(cx, cm, ws.xt, ws.w_rep, ws.hT, col0, ws.D, eps, ws.scratch)


def rope_out(cx, ws, px, pp, M, C, S, tcol, dst_ap):
    i = ws.pi % 2
    t1, t2 = ws.t1[i], ws.t2[i]
    ob = ws.out_buf()
    cx.op("dve", lambda e: e.tensor_tensor(out=t1[0:M, :], in0=px[0:M, :], in1=C[0:M, tcol:tcol + 512], op=ALU.mult),
          [px, C], [t1])
    cx.op("dve", lambda e: e.tensor_tensor(out=t2[0:M, :], in0=pp[0:M, :], in1=S[0:M, tcol:tcol + 512], op=ALU.mult),
          [pp, S], [t2])
    cx.op("dve", lambda e: e.tensor_tensor(out=ob[0:M, :], in0=t1[0:M, :], in1=t2[0:M, :], op=ALU.add), [t1, t2], [ob])
    cx.dma("sp", lambda e: [e.dma_start(out=dst_ap, in_=ob[0:M, :])], ob, None)


def plain_out(cx, ws, px, M, N, dst_ap, func=None, src3=False):
    ob = ws.out_buf()
    if func is None:
        cx.op("act", lambda e: e.copy(out=ob[0:M, 0:N], in_=px[0:M, 0:N]), [px], [ob])
    else:
        cx.op("act", lambda e: e.activation(out=ob[0:M, 0:N], in_=px[0:M, 0:N], func=func), [px], [ob])
    if src3:
        cx.dma("sp", lambda e: [e.dma_start(out=dst_ap, in_=ob[0:M, 0:N].rearrange("p (j d) -> p j d", d=128))], ob, None)
    else:
        cx.dma("sp", lambda e: [e.dma_start(out=dst_ap, in_=ob[0:M, 0:N])], ob, None)


def nsa_proj_phase(cx, cm, x_in, w_rep_ap, w_in, pos_ap, invsign_ap, outs, NT, D, T=1024):
    KC = D // 128
    QC, KVC = 2048, 3072
    invsign = load_invsign(cx, invsign_ap)
    C, S = rope_tables(cx, pos_ap, invsign, 128, NT)
    ws = ProjWS(cx, D, T, KC * 512)
    cx.dma("sp", lambda e: [e.dma_start(out=ws.w_rep[:, :], in_=w_rep_ap[:, :])], None, ws.w_rep)
    for st in range(NT // T):
        t0 = st * T
        for b in range(T // 128):
            norm_block_to_hT(cx, cm, ws, x_in, t0 + b * 128, b * 128)
        roped = [("qT", h, h * 128) for h in range(16)] + \
                [("ksT", g, QC + 1 * 1024 + g * 128) for g in range(4)] + \
                [("kwT", g, QC + 2 * 1024 + g * 128) for g in range(4)]
        for (name, idx, c0) in roped:
            slab = ws.slab()
            sv = load_wcols(cx, slab, KC, 256, w_in, [(0, c0, 128), (128, c0 + 64, 64), (192, c0, 64)])
            for tt in range(T // 512):
                i = ws.pi % 2
                ws.pi += 1
                fm_mm(cx, ws.pa[i], 128, sv, 0, slab, ws.hT, ws.hT, (tt * 512, tt * 512 + 512), KC)
                fm_mm(cx, ws.pb[i], 128, sv, 128, slab, ws.hT, ws.hT, (tt * 512, tt * 512 + 512), KC)
                ws.pi -= 1
                rope_out(cx, ws, ws.pa[i], ws.pb[i], 128, C, S, t0 + tt * 512,
                         outs[name][idx * 128:(idx + 1) * 128, t0 + tt * 512:t0 + tt * 512 + 512])
                ws.pi += 1
        plain = [("kcT", g, QC + 0 * 1024 + g * 128) for g in range(4)] + \
                [("vcT", g, QC + 0 * 1024 + 512 + g * 128) for g in range(4)]
        for (name, idx, c0) in plain:
            slab = ws.slab()
            sv = load_wcols(cx, slab, KC, 128, w_in, [(0, c0, 128)])
            for tt in range(T // 512):
                i = ws.pi % 2
                ws.pi += 1
                fm_mm(cx, ws.pa[i], 128, sv, 0, slab, ws.hT, ws.hT, (tt * 512, tt * 512 + 512), KC)
                plain_out(cx, ws, ws.pa[i], 128, 512,
                          outs[name][idx * 128:(idx + 1) * 128, t0 + tt * 512:t0 + tt * 512 + 512])
        slab = ws.slab()
        sv = load_wcols(cx, slab, KC, 48, w_in, [(0, QC + KVC, 48)])
        for tt in range(T // 512):
            i = ws.pi % 2
            ws.pi += 1
            fm_mm(cx, ws.pa[i], 48, sv, 0, slab, ws.hT, ws.hT, (tt * 512, tt * 512 + 512), KC)
            plain_out(cx, ws, ws.pa[i], 48, 512, outs["gT"][0:48, t0 + tt * 512:t0 + tt * 512 + 512], func=AF.Sigmoid)
        for (name, c0) in (("vs", QC + 1 * 1024 + 512), ("vw", QC + 2 * 1024 + 512)):
            slab = ws.slab()
            sv = load_wcols(cx, slab, KC, 512, w_in, [(0, c0, 512)])
            for b in range(T // 128):
                i = ws.pi % 2
                ws.pi += 1
                tm_mm(cx, ws.pa[i], 512, sv, 0, slab, ws.hT, ws.hT, b * 128, KC)
                plain_out(cx, ws, ws.pa[i], 128, 512, outs[name](t0 + b * 128), src3=True)


class AttnWS:
    def __init__(self, cx, n_p=6):
        self.S = [cx.psum("aS", [128, 512], F32) for _ in range(3)]
        self.O = [cx.psum("aO", [128, 512], F32) for _ in range(2)]
        l_ = cx.psum("aL", [128, 512], F32)
        self.L = [l_, l_]
        self.P = [cx.sbuf("aP", [128, 512], BF16) for _ in range(n_p)]
        self.ones = cx.sbuf("aones", [128, 128], BF16)
        cx.op("dve", lambda e: e.memset(self.ones[:, :], 1.0), [], [self.ones])
        self.rl = [cx.sbuf("arl", [128, 512], F32) for _ in range(2)]
        self.Lacc = [cx.sbuf("aLacc", [128, 512], F32) for _ in range(2)]
        self.LaccB = [cx.sbuf("aLaccB", [128, 512], F32) for _ in range(2)]
        self.use_pool = False
        self.parity = 0
        self.ones_f = cx.sbuf("aones_f", [128, 128], F32)
        cx.op("dve", lambda e: e.memset(self.ones_f[:, :], 1.0), [], [self.ones_f])
        self.si = 0
        self.pi = 0
        self.oi = 0


def attn_score(cx, W, qk, clo, chi, scale, Pdst=None, blkmask=None, tilemask=None, addmask=None):
    Sp = W.S[W.si % len(W.S)]
    W.si += 1
    n_mm = len(qk) + (1 if addmask is not None else 0)
    j = 0
    for (qb, qa, kb, ka) in qk:
        j += 1
        cx.op("pe", lambda e, qa=qa, ka=ka, j=j: e.matmul(Sp[:, clo:chi], lhsT=ka, rhs=qa[:, clo:chi],
                                                          start=(j == 1), stop=(j == n_mm)),
              [qb, kb], [Sp], inc=(j == n_mm))
    if addmask is not None:
        lb, la, rb, ra = addmask
        cx.op("pe", lambda e: e.matmul(Sp[:, clo:chi], lhsT=la, rhs=ra[:, clo:chi], start=False, stop=True),
              [lb, rb], [Sp], inc=True)
    if Pdst is None:
        Pb = W.P[W.pi % len(W.P)]
        W.pi += 1
        Pa = Pb[:, :]
    else:
        Pb, Pa = Pdst
    cx.op("act", lambda e: e.activation(out=Pa[:, clo:chi], in_=Sp[:, clo:chi], func=AF.Exp, scale=float(scale)),
          [Sp], [Pb])
    if blkmask is not None:
        mb, ma, c0 = blkmask
        cx.op("dve", lambda e: e.tensor_tensor(out=Pa[:, c0:c0 + 128], in0=Pa[:, c0:c0 + 128], in1=ma, op=ALU.mult),
              [Pb, mb], [Pb])
    if tilemask is not None:
        mb, ma = tilemask
        cx.op("dve", lambda e: e.tensor_tensor(out=Pa[:, clo:chi], in0=Pa[:, clo:chi], in1=ma, op=ALU.mult),
              [Pb, mb], [Pb])
    return Pb, Pa


def attn_pv(cx, W, ol, P, v, clo, chi, first, last):
    Pb, Pa = P
    vb, va = v
    O, L = W.O[ol], W.L[ol]
    cx.op("pe", lambda e: e.matmul(O[:, clo:chi], lhsT=va, rhs=Pa[:, clo:chi], start=first, stop=last),
          [vb, Pb], [O], inc=last)
    La, Lb = W.Lacc[ol], W.LaccB[ol]
    if first:
        W.parity = 0
        cx.op("dve", lambda e: e.tensor_copy(out=La[:, clo:chi], in_=Pa[:, clo:chi]), [Pb], [La])
        if W.use_pool:
            cx.op("pool", lambda e: e.memset(Lb[:, :], 0.0), [], [Lb])
    else:
        W.parity += 1
        if W.use_pool and W.parity % 2 == 1:
            cx.op("pool", lambda e: e.tensor_tensor(out=Lb[:, clo:chi], in0=Lb[:, clo:chi], in1=Pa[:, clo:chi], op=ALU.add),
                  [Pb, Lb], [Lb])
        else:
            cx.op("dve", lambda e: e.tensor_tensor(out=La[:, clo:chi], in0=La[:, clo:chi], in1=Pa[:, clo:chi], op=ALU.add),
                  [Pb, La], [La])
    if last:
        if W.use_pool:
            cx.op("pe", lambda e: e.matmul(L[:, :], lhsT=W.ones_f[:, :], rhs=La[:, :], start=True, stop=False),
                  [W.ones_f, La], [L], inc=False)
            cx.op("pe", lambda e: e.matmul(L[:, :], lhsT=W.ones_f[:, :], rhs=Lb[:, :], start=False, stop=True),
                  [W.ones_f, Lb], [L], inc=True)
        else:
            cx.op("pe", lambda e: e.matmul(L[:, :], lhsT=W.ones_f[:, :], rhs=La[:, :], start=True, stop=True),
                  [W.ones_f, La], [L], inc=True)


def attn_chunk(cx, W, ol, qk, v, clo, chi, first, last, scale, Pdst=None, blkmask=None, tilemask=None, addmask=None):
    P = attn_score(cx, W, qk, clo, chi, scale, Pdst, blkmask, tilemask, addmask)
    attn_pv(cx, W, ol, P, v, clo, chi, first, last)


class AttnPipe:
    def __init__(self, cx, W, depth=2):
        self.cx, self.W = cx, W
        self.pending = []
        self.depth = depth

    def push(self, ol, qk, v, clo, chi, first, last, scale, after=None, **kw):
        P = attn_score(self.cx, self.W, qk, clo, chi, scale, **kw)
        self.pending.append((ol, P, v, clo, chi, first, last, after))
        while len(self.pending) > self.depth:
            self._drain()

    def _drain(self):
        ol, P, v, clo, chi, first, last, after = self.pending.pop(0)
        attn_pv(self.cx, self.W, ol, P, v, clo, chi, first, last)
        if after is not None:
            after()

    def flush(self):
        while self.pending:
            self._drain()


def attn_rl(cx, W, ol, gate=None):
    rl = W.rl[W.oi % 2]
    W.oi += 1
    L = W.L[ol]
    cx.op("dve", lambda e: e.tensor_scalar(out=rl[:, :], in0=L[:, :], scalar1=1e-30, scalar2=None, op0=ALU.max), [L], [rl])
    cx.op("dve", lambda e: e.reciprocal(out=rl[:, :], in_=rl[:, :]), [rl], [rl])
    if gate is not None:
        cx.op("dve", lambda e: e.tensor_tensor(out=rl[:, :], in0=rl[:, :], in1=gate[:, :], op=ALU.mult), [rl, gate], [rl])
    return rl


CB_TRI, CB_TRI2, CB_CMASK, CB_OV, CB_ENEG, CB_SELG = 0, 128, 256, 256 + 2560, 256 + 2560 + 2048, 256 + 2560 + 2048 + 8192
CB_W = CB_SELG + 12 * 128


def nsa_consts_bf16():
    c = np.zeros((128, CB_W), np.float32)
    k = np.arange(128)[:, None]
    q = np.arange(128)[None, :]
    c[:, CB_TRI:CB_TRI + 128] = (k <= q)
    c[:, CB_TRI2:CB_TRI2 + 128] = (q < k)
    nl = np.arange(128)[:, None]
    tl = np.arange(512)[None, :]
    for d in range(5):
        c[:, CB_CMASK + d * 512:CB_CMASK + (d + 1) * 512] = (16 * nl + 31 <= 512 * d + tl)
    for ch in range(8):
        n = 128 * ch + np.arange(128)[:, None]
        j = np.arange(256)[None, :]
        ov = (16 * n <= 64 * j + 63) & (16 * n + 31 >= 64 * j) & (n < 1023)
        c[:, CB_OV + ch * 256:CB_OV + (ch + 1) * 256] = ov
    jl = np.arange(128)[:, None]
    key = np.arange(128)[None, :]
    for kcl in range(64):
        c[:, CB_ENEG + kcl * 128:CB_ENEG + (kcl + 1) * 128] = np.where(jl == 2 * kcl + key // 64, -30000.0, 0.0)
    for i in range(12):
        c[i, CB_SELG + i * 128:CB_SELG + (i + 1) * 128] = 1.0
    return c.astype(ml_dtypes.bfloat16)


def nsa_ab_table(S):
    nb = S // 128
    t = (np.arange(nb)[:, None] * 128 + np.arange(128)[None, :])[:, :, None]
    cur = t // 64
    j = np.arange(256)[None, None, :]
    A = (j <= cur).astype(np.float32)
    B = np.where(j > cur, -1e30, 0.0).astype(np.float32)
    B = np.where((j == cur - 1) & (j > 0), 3e9, B)
    B = np.where((j == cur) & (j > 0), 2e9, B)
    B = np.where((j == 0), 1e9, B)
    B = np.where(j > cur, -1e30, B).astype(np.float32)
    return np.ascontiguousarray(np.concatenate([A, B], axis=-1))


def nsa_attn_phase(cx, cm, io, S, NTC=4096):
    NQT = S // 512
    NKC = S // 128
    NCMP = (S - 32) // 16 + 1
    NCC = (NCMP + 127) // 128
    scale = 128.0 ** -0.5
    cbf = cx.sbuf("cbf", [128, CB_W], BF16, dma=True)
    cx.dma("sp", lambda e: [e.dma_start(out=cbf[:, :], in_=io["cbf"][:, :])], None, cbf)
    kcmpT = cx.sbuf("kcmpT", [128, NCC * 128], BF16)
    vcmp = cx.sbuf("vcmp", [128, NCC, 128], BF16)
    cx.op("dve", lambda e: e.memset(kcmpT[:, :], 0.0), [], [kcmpT])
    cx.op("dve", lambda e: e.memset(vcmp[:, :, :], 0.0), [], [vcmp])
    with cx.scope():
        invsign = load_invsign(cx, io["invsign"])
        Cc, Sc = rope_tables(cx, io["pos_cmp"], invsign, 128, NCC * 128)
        tokT = cx.sbuf("tokT", [128, S], BF16, dma=True)
        w1s = cx.sbuf("w1s", [128, 32, 256], BF16, dma=True)
        w2s = cx.sbuf("w2s", [128, 2, 256], BF16, dma=True)
        pef = cx.sbuf("pef", [128, 32], F32, dma=True)
        peb = cx.sbuf("peb", [128, 32], BF16)
        bias = cx.sbuf("cbias", [128, 2], F32)
        gel = cx.sbuf("gel", [128, 2, NCC * 128], BF16)
        ph = [cx.psum("ph", [128, 512], F32) for _ in range(2)]
        pk = [cx.psum("pk", [128, 512], F32) for _ in range(2)]
        pb_ = cx.psum("pbias", [128, 512], F32)
        t1 = cx.sbuf("ct1", [128, 512], F32)
        t2 = cx.sbuf("ct2", [128, 512], F32)
        ntiles = [(n0, min(512, NCMP - n0)) for n0 in range(0, NCMP, 512)]
        hi = 0
        for kv in range(2):
            nm = "kcT" if kv == 0 else "vcT"
            cx.dma("sp", lambda e, nm=nm: [e.dma_start(out=tokT[:, rs * NTC:(rs + 1) * NTC], in_=io["feat"](nm, rs))
                                          for rs in range(S // NTC)], None, tokT, n=S // NTC)
            cx.dma("pool", lambda e, kv=kv: [e.dma_start(
                out=w1s[:, :, :], in_=io["w1"][kv].rearrange("(l d) h -> d l h", d=128))], None, w1s)

            def ldw2(e, kv=kv):
                a = e.dma_start(out=w2s[:, :, 0:128], in_=io["w2"][kv].rearrange("(c p) d -> p c d", p=128))
                b = e.dma_start(out=w2s[:, :, 128:192], in_=io["w2"][kv][:, 64:128].rearrange("(c p) d -> p c d", p=128))
                c = e.dma_start(out=w2s[:, :, 192:256], in_=io["w2"][kv][:, 0:64].rearrange("(c p) d -> p c d", p=128))
                return [a, b, c]

            cx.dma("pool", ldw2, None, w2s, n=3)
            cx.dma("sp", lambda e, kv=kv: [e.dma_start(out=pef[:, :], in_=io["peT"][kv])], None, pef)
            cx.op("dve", lambda e: e.tensor_copy(out=peb[:, :], in_=pef[:, :]), [pef], [peb])
            for hc in range(2):
                for l in range(32):
                    cx.op("pe", lambda e, hc=hc, l=l: e.matmul(pb_[:, hc:hc + 1], lhsT=w1s[:, l, hc * 128:(hc + 1) * 128],
                                                               rhs=peb[:, l:l + 1], start=(l == 0), stop=(l == 31)),
                          [w1s, peb], [pb_], inc=(l == 31))
            cx.op("dve", lambda e: e.tensor_copy(out=bias[:, 0:2], in_=pb_[:, 0:2]), [pb_], [bias])
            for (n0, cnt) in ntiles:
                for hc in range(2):
                    p = ph[hi % 2]
                    hi += 1
                    for l in range(32):
                        a0 = 16 * n0 + l
                        cx.op("pe", lambda e, hc=hc, l=l, p=p, a0=a0, cnt=cnt: e.matmul(
                            p[:, 0:cnt], lhsT=w1s[:, l, hc * 128:(hc + 1) * 128],
                            rhs=tokT[:, a0:a0 + 16 * (cnt - 1) + 1:16], start=(l == 0), stop=(l == 31)),
                            [w1s, tokT], [p], inc=(l == 31))
                    cx.op("act", lambda e, hc=hc, p=p, n0=n0, cnt=cnt: e.activation(
                        out=gel[:, hc, n0:n0 + cnt], in_=p[:, 0:cnt], func=AF.Gelu_apprx_tanh, bias=bias[:, hc:hc + 1]),
                        [p], [gel], scalars=[bias])
                if kv == 0:
                    for half in range(2):
                        p = pk[half]
                        for hc in range(2):
                            cx.op("pe", lambda e, hc=hc, p=p, half=half, n0=n0, cnt=cnt: e.matmul(
                                p[:, 0:cnt], lhsT=w2s[:, hc, half * 128:(half + 1) * 128], rhs=gel[:, hc, n0:n0 + cnt],
                                start=(hc == 0), stop=(hc == 1)), [w2s, gel], [p], inc=(hc == 1))
                    cx.op("dve", lambda e, n0=n0, cnt=cnt: e.tensor_tensor(
                        out=t1[:, 0:cnt], in0=pk[0][:, 0:cnt], in1=Cc[:, n0:n0 + cnt], op=ALU.mult), [pk[0], Cc], [t1])
                    cx.op("dve", lambda e, n0=n0, cnt=cnt: e.tensor_tensor(
                        out=t2[:, 0:cnt], in0=pk[1][:, 0:cnt], in1=Sc[:, n0:n0 + cnt], op=ALU.mult), [pk[1], Sc], [t2])
                    cx.op("dve", lambda e, n0=n0, cnt=cnt: e.tensor_tensor(
                        out=kcmpT[:, n0:n0 + cnt], in0=t1[:, 0:cnt], in1=t2[:, 0:cnt], op=ALU.add), [t1, t2], [kcmpT])
                else:
                    for c0 in range(n0, n0 + cnt, 128):
                        m = min(128, n0 + cnt - c0)
                        p = pk[(c0 // 128) % 2]
                        for hc in range(2):
                            cx.op("pe", lambda e, hc=hc, p=p, c0=c0, m=m: e.matmul(
                                p[0:m, 0:128], lhsT=gel[:, hc, c0:c0 + m], rhs=w2s[:, hc, 0:128],
                                start=(hc == 0), stop=(hc == 1)), [w2s, gel], [p], inc=(hc == 1))
                        cx.op("act", lambda e, p=p, c0=c0, m=m: e.copy(out=vcmp[0:m, c0 // 128, :], in_=p[0:m, 0:128]),
                              [p], [vcmp])
    W = AttnWS(cx)
    ksT = cx.sbuf("ksT", [128, S], BF16, dma=True)
    vs = cx.sbuf("vs", [128, NKC * 128], BF16, dma=True)
    cx.dma("sp", lambda e: [e.dma_start(out=ksT[:, rs * NTC:(rs + 1) * NTC], in_=io["feat"]("ksT", rs))
                            for rs in range(S // NTC)], None, ksT, n=S // NTC)
    CPR = NTC // 128
    cx.dma("sp", lambda e: [e.dma_start(out=vs[:, rs * NTC:(rs + 1) * NTC], in_=io["vrank"]("vs", rs))
                            for rs in range(S // NTC)], None, vs, n=S // NTC)
    qts = [cx.sbuf("qt", [128, 4 * 512], BF16, dma=True) for _ in range(2)]
    gts = [cx.sbuf("gt", [12, 512], BF16, dma=True) for _ in range(2)]
    kws = [cx.sbuf("kw", [128, 1024], BF16, dma=True) for _ in range(2)]
    vws = [cx.sbuf("vwb", [128, 8 * 128], BF16, dma=True) for _ in range(2)]
    abs_ = [cx.sbuf("ab", [128, 512], F32, dma=True) for _ in range(2)]
    pcn = cx.sbuf("pcn", [128, 4, NCC, 512], BF16)
    nselT = cx.sbuf("nselT", [128, 2, 512], BF16)
    acc = cx.sbuf("acc", [128, 4, 512], F32)
    accb = [cx.sbuf("accb", [128, 4 * 512], BF16, dma=True) for _ in range(2)]
    otmp = cx.sbuf("otmp", [128, 512], F32)
    gps = [cx.psum("gps", [128, 512], F32) for _ in range(1)]
    ips = cx.psum("ips", [128, 512], F32)
    seltp_ap = ips[:, 256:512].bitcast(BF16)
    impa = cx.sbuf("impa", [128, 256], F32)
    wk = cx.sbuf("wk", [128, 256], F32)
    mx = cx.sbuf("mx", [128, 16], F32)
    selb = cx.sbuf("selb", [128, 256], BF16)
    tps = cx.psum("ttp", [128, 256], BF16) if False else None
    abi = 0

    def gate_bcast(gt, i):
        g = gps[0]
        cx.op("pe", lambda e: e.matmul(g[:, :], lhsT=cbf[0:12, CB_SELG + i * 128:CB_SELG + (i + 1) * 128], rhs=gt[0:12, :],
                                       start=True, stop=True), [cbf, gt], [g])
        return g

    def finish_branch(ol, gt, r, br):
        g = gate_bcast(gt, r * 3 + br)
        rl = attn_rl(cx, W, ol, gate=g)
        O = W.O[ol]
        if br == 0:
            cx.op("dve", lambda e: e.tensor_tensor(out=acc[:, r, :], in0=O[:, :], in1=rl[:, :], op=ALU.mult), [O, rl], [acc])
        else:
            cx.op("dve", lambda e: e.tensor_tensor(out=otmp[:, :], in0=O[:, :], in1=rl[:, :], op=ALU.mult), [O, rl], [otmp])
            cx.op("dve", lambda e: e.tensor_tensor(out=acc[:, r, :], in0=acc[:, r, :], in1=otmp[:, :], op=ALU.add),
                  [otmp, acc], [acc])

    oli = 0
    for qt in range(NQT):
        q0 = qt * 512
        qb, gt, kw, vwb = qts[qt % 2], gts[qt % 2], kws[qt % 2], vws[qt % 2]
        cx.dma("sp", lambda e, qb=qb, q0=q0: [e.dma_start(out=qb[:, rh * 512:(rh + 1) * 512], in_=io["q"](rh, q0)) for rh in range(4)],
               None, qb, n=4)
        cx.dma("sp", lambda e, gt=gt, q0=q0: [e.dma_start(out=gt[:, :], in_=io["g"](q0))], None, gt)
        halves = [(q0 - 512, 0), (q0, 1)] if q0 > 0 else [(q0, 1)]
        cx.dma("sp", lambda e, kw=kw, halves=halves: [e.dma_start(
            out=kw[:, hh * 512:(hh + 1) * 512], in_=io["kwh"](qh)) for (qh, hh) in halves], None, kw, n=len(halves))
        cx.dma("sp", lambda e, vwb=vwb, halves=halves: [e.dma_start(
            out=vwb[:, hh * 512:(hh + 1) * 512], in_=io["vwh"](qh)) for (qh, hh) in halves], None, vwb, n=len(halves))
        vis = []
        for c in range(NCC):
            d = qt - 4 * c
            if d < 0:
                continue
            vis.append((c, d if d <= 4 else None))
        for r in range(4):
            ol = oli % 2
            oli += 1
            for ci, (c, d) in enumerate(vis):
                tm = None if d is None else (cbf, cbf[:, CB_CMASK + d * 512:CB_CMASK + (d + 1) * 512])
                attn_chunk(cx, W, ol, [(qb, qb[:, r * 512:(r + 1) * 512], kcmpT, kcmpT[:, c * 128:(c + 1) * 128])],
                           (vcmp, vcmp[:, c, :]), 0, 512, ci == 0, ci == len(vis) - 1, scale,
                           Pdst=(pcn, pcn[:, r, c, :]), tilemask=tm)
            rl0 = attn_rl(cx, W, ol)
            for (c, d) in vis:
                cx.op("dve", lambda e, r=r, c=c, rl0=rl0: e.tensor_tensor(
                    out=pcn[:, r, c, :], in0=pcn[:, r, c, :], in1=rl0[:, :], op=ALU.mult), [pcn, rl0], [pcn])
            g = gate_bcast(gt, r * 3 + 0)
            O = W.O[ol]
            cx.op("dve", lambda e, rl0=rl0, g=g: e.tensor_tensor(out=rl0[:, :], in0=rl0[:, :], in1=g[:, :], op=ALU.mult),
                  [rl0, g], [rl0])
            cx.op("dve", lambda e, r=r, O=O, rl0=rl0: e.tensor_tensor(out=acc[:, r, :], in0=O[:, :], in1=rl0[:, :], op=ALU.mult),
                  [O, rl0], [acc])
        for qbk in range(4):
            ab = abs_[abi % 2]
            abi += 1
            gb = qt * 4 + qbk
            cx.dma("sp", lambda e, ab=ab, gb=gb: [e.dma_start(out=ab[:, :], in_=io["ab"][gb])], None, ab)
            n = 4 * len(vis)
            i = 0
            for r in range(4):
                for (c, d) in vis:
                    i += 1
                    cx.op("pe", lambda e, r=r, c=c, i=i, qbk=qbk: e.matmul(
                        ips[:, 0:256], lhsT=pcn[:, r, c, qbk * 128:(qbk + 1) * 128],
                        rhs=cbf[:, CB_OV + c * 256:CB_OV + (c + 1) * 256], start=(i == 1), stop=(i == n)),
                        [pcn, cbf], [ips], inc=(i == n))
            cx.op("dve", lambda e, ab=ab: e.tensor_tensor(out=impa[:, :], in0=ips[:, 0:256], in1=ab[:, 0:256], op=ALU.mult),
                  [ips, ab], [impa])
            cx.op("dve", lambda e, ab=ab: e.tensor_tensor(out=impa[:, :], in0=impa[:, :], in1=ab[:, 256:512], op=ALU.add),
                  [impa, ab], [impa])
            cx.op("dve", lambda e: e.max(out=mx[:, 0:8], in_=impa[:, :]), [impa], [mx])
            cx.op("dve", lambda e: e.match_replace(out=wk[:, :], in_to_replace=mx[:, 0:8], in_values=impa[:, :],
                                                   imm_value=-3e38), [impa], [wk], scalars=[mx])
            cx.op("dve", lambda e: e.max(out=mx[:, 8:16], in_=wk[:, :]), [wk], [mx])
            cx.op("dve", lambda e: e.tensor_scalar(out=wk[:, :], in0=impa[:, :], scalar1=mx[:, 15:16], scalar2=None,
                                                   op0=ALU.is_ge), [impa], [wk], scalars=[mx])
            cx.op("dve", lambda e, ab=ab: e.tensor_tensor(out=wk[:, :], in0=wk[:, :], in1=ab[:, 0:256], op=ALU.mult),
                  [wk, ab], [wk])
            cx.op("dve", lambda e: e.tensor_scalar(out=selb[:, :], in0=wk[:, :], scalar1=-1.0, scalar2=1.0,
                                                   op0=ALU.mult, op1=ALU.add), [wk], [selb])
            for jc in range(2):
                cx.op("pe", lambda e, jc=jc: e.transpose(out=seltp_ap[:, jc * 128:(jc + 1) * 128],
                                                         in_=selb[:, jc * 128:(jc + 1) * 128],
                                                         identity=cm.ident[:, :]), [selb, cm.ident], [ips], inc=(jc == 1))
            cx.op("act", lambda e, qbk=qbk: e.copy(
                out=nselT[:, :, qbk * 128:(qbk + 1) * 128], in_=seltp_ap[:, 0:256].rearrange("p (a b) -> p a b", a=2)),
                [ips], [nselT])
        pipe = AttnPipe(cx, W)
        W.use_pool = True
        for r in range(4):
            ol = oli % 2
            oli += 1
            nk = 4 * (qt + 1)
            for kc in range(nk):
                m = kc - 4 * qt
                clo = 128 * max(m, 0)
                bm = None if m < 0 else (cbf, cbf[:, CB_TRI:CB_TRI + 128], clo)
                am = (cbf, cbf[:, CB_ENEG + (kc % 64) * 128:CB_ENEG + (kc % 64 + 1) * 128], nselT, nselT[:, kc // 64, :])
                aft = (lambda ol=ol, r=r: finish_branch(ol, gt, r, 1)) if kc == nk - 1 else None
                pipe.push(ol, [(qb, qb[:, r * 512:(r + 1) * 512], ksT, ksT[:, kc * 128:(kc + 1) * 128])],
                          (vs, vs[:, kc * 128:(kc + 1) * 128]), clo, 512, kc == 0, kc == nk - 1, scale, after=aft,
                          blkmask=bm, addmask=am)
        for r in range(4):
            ol = oli % 2
            oli += 1
            order = [-1, -4, -3, -2, 0, 1, 2, 3] if qt > 0 else [0, 1, 2, 3]
            for oi_, m in enumerate(order):
                if m < 0:
                    clo, chi = 0, 128 * (m + 5)
                    bm = (cbf, cbf[:, CB_TRI2:CB_TRI2 + 128], 128 * (m + 4))
                else:
                    clo, chi = 128 * m, 512
                    bm = (cbf, cbf[:, CB_TRI:CB_TRI + 128], clo)
                aft = (lambda ol=ol, r=r: finish_branch(ol, gt, r, 2)) if oi_ == len(order) - 1 else None
                pipe.push(ol, [(qb, qb[:, r * 512:(r + 1) * 512], kw, kw[:, (m + 4) * 128:(m + 5) * 128])],
                          (vwb, vwb[:, (m + 4) * 128:(m + 5) * 128]), clo, chi, oi_ == 0, oi_ == len(order) - 1, scale,
                          after=aft, blkmask=bm)
        pipe.flush()
        W.use_pool = False
        ab_ = accb[qt % 2]
        cx.op("act", lambda e, ab_=ab_: e.copy(out=ab_[:, :], in_=acc[:, :, :].rearrange("p a b -> p (a b)")), [acc], [ab_])
        cx.dma("sp", lambda e, ab_=ab_, q0=q0: [e.dma_start(out=io["og"](rh, q0), in_=ab_[:, rh * 512:(rh + 1) * 512]) for rh in range(4)],
               ab_, None, n=4)


def cx_tp(cx):
    if not hasattr(cx, "_tp") or cx._tp_scope is not cx.stack:
        cx._tp = cx.psum("seltp", [128, 512], BF16)
        cx._tp_scope = cx.stack
    return cx._tp


def mla_proj_phase(cx, cm, x_in, w_rep_ap, w_in, qn_rep_ap, kvn_rep_ap, w_uq, w_ukv, pos_ap, invsign_ap, outs, NT, D, T=1024):
    KC = D // 128
    invsign = load_invsign(cx, invsign_ap)
    C, S = rope_tables(cx, pos_ap, invsign, 64, NT)
    ws = ProjWS(cx, D, T, KC * 512)
    cx.dma("sp", lambda e: [e.dma_start(out=ws.w_rep[:, :], in_=w_rep_ap[:, :])], None, ws.w_rep)
    nrm = [cx.sbuf("mnrm", [128, 512], F32, dma=True) for _ in range(2)]
    cx.dma("sp", lambda e: [e.dma_start(out=nrm[0][:, :], in_=qn_rep_ap[:, :])], None, nrm[0])
    cx.dma("sp", lambda e: [e.dma_start(out=nrm[1][:, :], in_=kvn_rep_ap[:, :])], None, nrm[1])
    cT = [cx.sbuf("mcT", [128, 4, T], BF16) for _ in range(2)]
    cbf_ = cx.sbuf("mcbf", [128, 512], BF16)
    sc2 = dict(junk=cbf_, ss=cx.sbuf("mss", [128, 1], F32), rstd=cx.sbuf("mrstd", [128, 1], F32), hbf=cbf_,
               tp=ws.scratch["tp"], tpi=ws.scratch["tpi"])
    for st in range(NT // T):
        t0 = st * T
        for b in range(T // 128):
            norm_block_to_hT(cx, cm, ws, x_in, t0 + b * 128, b * 128)
        for which in range(2):
            slab = ws.slab()
            sv = load_wcols(cx, slab, KC, 512, w_in, [(0, which * 512, 512)])
            for b in range(T // 128):
                i = ws.pi % 2
                ws.pi += 1
                p = ws.pa[i]
                tm_mm(cx, p, 512, sv, 0, slab, ws.hT, ws.hT, b * 128, KC)
                junk, ss, rstd = sc2["junk"], sc2["ss"], sc2["rstd"]
                cx.op("act", lambda e, p=p: e.activation(out=junk[:, :], in_=p[:, :], func=AF.Square, accum_out=ss[:, 0:1]),
                      [p], [junk, ss])
                cx.op("act", lambda e: e.activation(out=rstd[:, 0:1], in_=ss[:, 0:1], func=AF.Sqrt, scale=1.0 / 512,
                                                    bias=eps_ap(cx)), [ss], [rstd], scalars=[cx.eps_tile])
                cx.op("dve", lambda e: e.reciprocal(out=rstd[:, 0:1], in_=rstd[:, 0:1]), [rstd], [rstd])
                cx.op("dve", lambda e, p=p, which=which: e.scalar_tensor_tensor(
                    out=cbf_[:, :], in0=p[:, :], scalar=rstd[:, 0:1], in1=nrm[which][:, :], op0=ALU.mult, op1=ALU.mult),
                    [p, nrm[which]], [cbf_], scalars=[rstd])
                transpose_to_fm(cx, cm, cbf_, cT[which], b * 128, 4, sc2)
        slab = ws.slab()
        sv = load_wcols(cx, slab, KC, 128, w_in, [(0, 1024, 64), (64, 1024 + 32, 32), (96, 1024, 32)])
        for tt in range(T // 512):
            i = ws.pi % 2
            fm_mm(cx, ws.pa[i], 64, sv, 0, slab, ws.hT, ws.hT, (tt * 512, tt * 512 + 512), KC)
            fm_mm(cx, ws.pb[i], 64, sv, 64, slab, ws.hT, ws.hT, (tt * 512, tt * 512 + 512), KC)
            rope_out(cx, ws, ws.pa[i], ws.pb[i], 64, C, S, t0 + tt * 512, outs["krT"][0:64, t0 + tt * 512:t0 + tt * 512 + 512])
            ws.pi += 1
        for h in range(16):
            c0 = h * 192
            slab = ws.slab()
            sv = load_wcols(cx, slab, 4, 256, w_uq, [(0, c0, 128), (128, c0 + 128, 64), (192, c0 + 160, 32), (224, c0 + 128, 32)])
            for tt in range(T // 512):
                cols = (tt * 512, tt * 512 + 512)
                i = ws.pi % 2
                ws.pi += 1
                fm_mm(cx, ws.pa[i], 128, sv, 0, slab, cT[0], cT[0], cols, 4)
                plain_out(cx, ws, ws.pa[i], 128, 512, outs["qnT"][h * 128:(h + 1) * 128, t0 + cols[0]:t0 + cols[1]])
                i = ws.pi % 2
                fm_mm(cx, ws.pa[i], 64, sv, 128, slab, cT[0], cT[0], cols, 4)
                fm_mm(cx, ws.pb[i], 64, sv, 192, slab, cT[0], cT[0], cols, 4)
                rope_out(cx, ws, ws.pa[i], ws.pb[i], 64, C, S, t0 + cols[0], outs["qrT"][h * 64:(h + 1) * 64, t0 + cols[0]:t0 + cols[1]])
                ws.pi += 1
        for h in range(16):
            slab = ws.slab()
            sv = load_wcols(cx, slab, 4, 128, w_ukv, [(0, h * 256, 128)])
            for tt in range(T // 512):
                cols = (tt * 512, tt * 512 + 512)
                i = ws.pi % 2
                ws.pi += 1
                fm_mm(cx, ws.pa[i], 128, sv, 0, slab, cT[1], cT[1], cols, 4)
                plain_out(cx, ws, ws.pa[i], 128, 512, outs["knT"][h * 128:(h + 1) * 128, t0 + cols[0]:t0 + cols[1]])
        for hq in range(4):
            slab = ws.slab()
            sv = load_wcols(cx, slab, 4, 512, w_ukv, [(j * 128, (hq * 4 + j) * 256 + 128, 128) for j in range(4)])
            for b in range(T // 128):
                i = ws.pi % 2
                ws.pi += 1
                tm_mm(cx, ws.pa[i], 512, sv, 0, slab, cT[1], cT[1], b * 128, 4)
                plain_out(cx, ws, ws.pa[i], 128, 512, outs["v"](t0 + b * 128, hq), src3=True)


def mla_attn_phase(cx, cm, io, S, NH=4, NTC=4096):
    NQT = S // 512
    NKC = S // 128
    scale = 192.0 ** -0.5
    W = AttnWS(cx)
    tri = cx.sbuf("mtri", [128, 128], BF16, dma=True)
    cx.dma("sp", lambda e: [e.dma_start(out=tri[:, :], in_=io["tri"][:, :])], None, tri)
    krT = cx.sbuf("krT", [64, S], BF16, dma=True)
    NR = S // NTC
    CPR = NTC // 128
    cx.dma("sp", lambda e: [e.dma_start(out=krT[:, rs * NTC:(rs + 1) * NTC], in_=io["kr"](rs)) for rs in range(NR)],
           None, krT, n=NR)
    kns = [cx.sbuf("knT", [128, S], BF16, dma=True) for _ in range(2)]
    vss = [cx.sbuf("mv", [128, NKC * 128], BF16, dma=True) for _ in range(2)]
    qns = [cx.sbuf("mqn", [128, 512], BF16, dma=True) for _ in range(2)]
    qrs = [cx.sbuf("mqr", [64, 512], BF16, dma=True) for _ in range(2)]
    obs = [cx.sbuf("mob", [128, 512], BF16, dma=True) for _ in range(2)]
    it = 0
    pipe = AttnPipe(cx, W)
    for h in range(NH):
        kn, vv = kns[h % 2], vss[h % 2]
        cx.dma("sp", lambda e, kn=kn, h=h: [e.dma_start(out=kn[:, rs * NTC:(rs + 1) * NTC], in_=io["kn"](h, rs))
                                            for rs in range(NR)], None, kn, n=NR)
        cx.dma("sp", lambda e, vv=vv, h=h: [e.dma_start(out=vv[:, rs * NTC:(rs + 1) * NTC], in_=io["v"](h, rs))
                                            for rs in range(NR)], None, vv, n=NR)
        for qt in range(NQT):
            q0 = qt * 512
            qn, qr, ob = qns[it % 2], qrs[it % 2], obs[it % 2]
            ol = it % 2
            it += 1
            cx.dma("sp", lambda e, qn=qn, h=h, q0=q0: [e.dma_start(out=qn[:, :], in_=io["qn"](h, q0))], None, qn)
            cx.dma("sp", lambda e, qr=qr, h=h, q0=q0: [e.dma_start(out=qr[:, :], in_=io["qr"](h, q0))], None, qr)
            nk = 4 * (qt + 1)
            for kc in range(nk):
                m = kc - 4 * qt
                clo = 128 * max(m, 0)
                bm = None if m < 0 else (tri, tri[:, :], clo)
                def fin(ol=ol, ob=ob, h=h, q0=q0):
                    rl = attn_rl(cx, W, ol)
                    O = W.O[ol]
                    cx.op("dve", lambda e: e.tensor_tensor(out=ob[:, :], in0=O[:, :], in1=rl[:, :], op=ALU.mult), [O, rl], [ob])
                    cx.dma("sp", lambda e: [e.dma_start(out=io["o"](h, q0), in_=ob[:, :])], ob, None)

                pipe.push(ol, [(qn, qn[:, :], kn, kn[:, kc * 128:(kc + 1) * 128]),
                               (qr, qr[0:64, :], krT, krT[0:64, kc * 128:(kc + 1) * 128])],
                          (vv, vv[:, kc * 128:(kc + 1) * 128]), clo, 512, kc == 0, kc == nk - 1, scale,
                          after=(fin if kc == nk - 1 else None), blkmask=bm)
    pipe.flush()


NCORE = 8
DM, DFF, SEQ_, NTC = 2048, 5632, 16384, 4096
GROUPS = [[0, 1, 2, 3], [4, 5, 6, 7]]
UA, UB, UC = 41, 16, 57


def rep128(v):
    return np.ascontiguousarray(np.tile(np.asarray(v, np.float32)[None, :], (128, 1)))


def exchange(cx, src2d, parts, copies):
    with cx.scope():
        bds = []
        bs = cx.dram_buf(src2d, "xs")
        for (u0, n, dst2d) in parts:
            bd = cx.dram_buf(dst2d, "xd")
            for j in range(n):
                u = u0 + j
                cx.cc(lambda e, u=u, j=j, dst2d=dst2d: e.collective_compute(
                    "AllGather", ALU.bypass, replica_groups=GROUPS, ins=[src2d[u * 128:(u + 1) * 128, :]],
                    outs=[dst2d[j * 512:(j + 1) * 512, :]]), bs, bd)
            bds.append(bd)
        cx.wait_all("sp", bds)
        mine = cx.dram_buf(None, "mine")
        for (out_ap, in_fn) in copies:
            cx.dma("sp", lambda e, out_ap=out_ap, in_fn=in_fn: [e.dma_start(out=out_ap, in_=in_fn())], None, mine)


def build_program():
    nc = bass.Bass("TRN2", target_bir_lowering=False)

    def I(name, shape, dt=F32):
        return nc.dram_tensor(name, list(shape), dt, kind="ExternalInput").ap()

    def T(name, shape, dt=F32):
        return nc.dram_tensor(name, list(shape), dt).ap()

    a_x = I("x", [NTC, DM]); a_rank = I("rank", [1, 2], I32); a_pos = I("pos", [128, NTC], I32)
    a_posc = I("pos_cmp", [128, 1024], I32); a_inv128 = I("inv128", [128, 2]); a_inv64 = I("inv64", [128, 2])
    a_c = I("consts", [128, 128]); a_cbf = I("cbf", [128, CB_W], BF16); a_ab = I("ab", [SEQ_ // 128, 128, 512])
    fw = [(I("fw_rep%d" % k, [128, DM]), I("f_w_in%d" % k, [DM, 2 * DFF]), I("f_w_out%d" % k, [DFF, DM])) for k in range(4)]
    a_mw0 = I("mw_rep0", [128, DM]); a_mw1 = I("mw_rep1", [128, DM])
    a_nw = I("nsa_w_in", [DM, 5168]); a_w1 = I("w1", [2, 4096, 256]); a_w2 = I("w2", [2, 256, 128]); a_peT = I("peT", [2, 128, 32])
    a_now = I("nsa_w_out", [2048, DM])
    a_mwi = I("mla_w_in", [DM, 1088]); a_qn = I("qn_rep", [128, 512]); a_kvn = I("kvn_rep", [128, 512])
    a_uq = I("w_uq", [512, 3072]); a_ukv = I("w_ukv", [512, 4096]); a_mow = I("mla_w_out", [2048, DM])
    a_fn = I("fn_rep", [128, DM])
    a_out = nc.dram_tensor("out", [NTC, DM], F32, kind="ExternalOutput").ap()
    xs_ = [T("xr%d" % i, [NTC, DM]) for i in range(6)]
    A_src = T("A_src", [UA * 128, NTC], BF16)
    Aq = T("Aq", [16 * 512, NTC], BF16); Af = T("Af", [16 * 512, NTC], BF16); Av = T("Av", [8 * 512, NTC], BF16); Ag = T("Ag", [512, NTC], BF16)
    B_src = T("B_src", [UB * 128, NTC], BF16); B_dst = T("B_dst", [UB * 512, NTC], BF16)
    myQ = T("myQ", [2048, NTC], BF16); myF = T("myF", [2048, NTC], BF16); myV = T("myV", [1024, NTC], BF16); myG = T("myG", [48, NTC], BF16)
    myO = T("myO", [2048, NTC], BF16); myO2 = T("myO2", [2048, NTC], BF16)
    myQn = T("myQn", [2048, NTC], BF16); myQr = T("myQr", [1024, NTC], BF16); myKn = T("myKn", [2048, NTC], BF16); myV2 = T("myV2", [2048, NTC], BF16)
    C_src = T("C_src", [UC * 128, NTC], BF16)
    Cqn = T("Cqn", [16 * 512, NTC], BF16); Cqr = T("Cqr", [8 * 512, NTC], BF16); Ckn = T("Ckn", [16 * 512, NTC], BF16)
    Ckr = T("Ckr", [512, NTC], BF16); Cv = T("Cv", [16 * 512, NTC], BF16)
    D_src = T("D_src", [UB * 128, NTC], BF16); D_dst = T("D_dst", [UB * 512, NTC], BF16)

    with ExitStack() as stack:
        cx = Ctx(nc, stack)
        cm = Common(cx, a_c)
        rv = cx.load_rank("sp", a_rank)

        def split(q0):
            return q0 // NTC, q0 % NTC

        with cx.scope():
            ffn_phase(cx, cm, a_x, xs_[0], fw[0][0], fw[0][1], fw[0][2], NTC, DM, DFF, 1e-6)
        VS = A_src[32 * 128:36 * 128, :].rearrange("(g p) c -> g p c", g=4)
        VW = A_src[36 * 128:40 * 128, :].rearrange("(g p) c -> g p c", g=4)
        outs = dict(qT=A_src[0:2048, :], kcT=A_src[2048:2560, :], vcT=A_src[2560:3072, :], ksT=A_src[3072:3584, :],
                    kwT=A_src[3584:4096, :], gT=A_src[40 * 128:40 * 128 + 48, :],
                    vs=lambda t0: VS[:, :, t0:t0 + 128].rearrange("g p d -> p g d"),
                    vw=lambda t0: VW[:, :, t0:t0 + 128].rearrange("g p d -> p g d"))
        with cx.scope():
            nsa_proj_phase(cx, cm, xs_[0], a_mw0, a_nw, a_pos, a_inv128, outs, NTC, DM)
        cpA = [(myQ[:, :], lambda: Aq[bass.ds(rv() * 2048, 2048), :])]
        for i_ in range(4):
            cpA.append((myF[i_ * 512:(i_ + 1) * 512, :], lambda i_=i_: Af[bass.ds(rv() * 512 + i_ * 2048, 512), :]))
        for i_ in range(2):
            cpA.append((myV[i_ * 512:(i_ + 1) * 512, :], lambda i_=i_: Av[bass.ds(rv() * 512 + i_ * 2048, 512), :]))
        for rs_ in range(4):
            cpA.append((myG[rs_ * 12:(rs_ + 1) * 12, :], lambda rs_=rs_: Ag[bass.ds(rv() * 12 + rs_ * 128, 12), :]))
        exchange(cx, A_src, [(0, 16, Aq), (16, 16, Af), (32, 8, Av), (40, 1, Ag)], cpA)
        FU = {"kcT": 0, "vcT": 1, "ksT": 2, "kwT": 3}
        VU = {"vs": 0, "vw": 1}

        def a_q(rh, q0):
            rs, c0 = split(q0)
            return myQ[rh * 512 + rs * 128:rh * 512 + rs * 128 + 128, c0:c0 + 512]

        def a_g(q0):
            rs, c0 = split(q0)
            return myG[rs * 12:(rs + 1) * 12, c0:c0 + 512]

        def a_feat(nm, rs):
            return myF[FU[nm] * 512 + rs * 128:FU[nm] * 512 + rs * 128 + 128, :]

        def a_kwh(qh):
            rs, c0 = split(qh)
            return myF[3 * 512 + rs * 128:3 * 512 + rs * 128 + 128, c0:c0 + 512]

        def a_vrank(nm, rs):
            return myV[VU[nm] * 512 + rs * 128:VU[nm] * 512 + rs * 128 + 128, :]

        def a_vwh(qh):
            rs, tl0 = split(qh)
            return myV[512 + rs * 128:512 + rs * 128 + 128, tl0:tl0 + 512]

        def b_og(src):
            def f(rh, q0):
                tb, c0 = split(q0)
                return src[(tb * 4 + rh) * 128:(tb * 4 + rh + 1) * 128, c0:c0 + 512]
            return f

        def b_oT(dst):
            def f(c, t0, Tn):
                g, rh = c // 4, c % 4
                return dst[rh * 512 + g * 128:rh * 512 + g * 128 + 128, t0:t0 + Tn]
            return f

        io = dict(q=a_q, g=a_g, feat=a_feat, kwh=a_kwh, vrank=a_vrank, vwh=a_vwh, og=b_og(B_src), cbf=a_cbf, ab=a_ab,
                  w1=a_w1, w2=a_w2, peT=a_peT, pos_cmp=a_posc, invsign=a_inv128)
        with cx.scope():
            nsa_attn_phase(cx, cm, io, SEQ_, NTC)
        exchange(cx, B_src, [(0, UB, B_dst)], [(myO[:, :], lambda: B_dst[bass.ds(rv() * 2048, 2048), :])])
        with cx.scope():
            outproj_phase(cx, cm, b_oT(myO), a_now, xs_[0], xs_[1], NTC, 2048, DM, 1.0)
        with cx.scope():
            ffn_phase(cx, cm, xs_[1], xs_[2], fw[1][0], fw[1][1], fw[1][2], NTC, DM, DFF, 1e-6)
        with cx.scope():
            ffn_phase(cx, cm, xs_[2], xs_[3], fw[2][0], fw[2][1], fw[2][2], NTC, DM, DFF, 1e-6)
        VV = C_src[41 * 128:57 * 128, :].rearrange("(h p) c -> h p c", h=16)
        mo = dict(qnT=C_src[0:2048, :], qrT=C_src[2048:3072, :], knT=C_src[3072:5120, :], krT=C_src[5120:5184, :],
                  v=lambda t0, hq: VV[hq * 4:(hq + 1) * 4, :, t0:t0 + 128].rearrange("h p d -> p h d"))
        with cx.scope():
            mla_proj_phase(cx, cm, xs_[3], a_mw1, a_mwi, a_qn, a_kvn, a_uq, a_ukv, a_pos, a_inv64, mo, NTC, DM)
        exchange(cx, C_src, [(0, 16, Cqn), (16, 8, Cqr), (24, 16, Ckn), (40, 1, Ckr), (41, 16, Cv)],
                 [(myQn[:, :], lambda: Cqn[bass.ds(rv() * 2048, 2048), :]), (myQr[:, :], lambda: Cqr[bass.ds(rv() * 1024, 1024), :]),
                  (myKn[:, :], lambda: Ckn[bass.ds(rv() * 2048, 2048), :]), (myV2[:, :], lambda: Cv[bass.ds(rv() * 2048, 2048), :])])

        def c_qn(hl, q0):
            rs, c0 = split(q0)
            return myQn[hl * 512 + rs * 128:hl * 512 + rs * 128 + 128, c0:c0 + 512]

        def c_qr(hl, q0):
            rs, c0 = split(q0)
            r0_ = (hl // 2) * 512 + rs * 128 + (hl % 2) * 64
            return myQr[r0_:r0_ + 64, c0:c0 + 512]

        def c_kn(hl, rs):
            return myKn[hl * 512 + rs * 128:hl * 512 + rs * 128 + 128, :]

        def c_kr(rs):
            return Ckr[rs * 128:rs * 128 + 64, :]

        def c_v(hl, rs):
            return myV2[hl * 512 + rs * 128:hl * 512 + rs * 128 + 128, :]

        io2 = dict(qn=c_qn, qr=c_qr, kn=c_kn, kr=c_kr, v=c_v, o=b_og(D_src), tri=a_cbf[:, CB_TRI:CB_TRI + 128])
        with cx.scope():
            mla_attn_phase(cx, cm, io2, SEQ_, 4, NTC)
        exchange(cx, D_src, [(0, UB, D_dst)], [(myO2[:, :], lambda: D_dst[bass.ds(rv() * 2048, 2048), :])])
        with cx.scope():
            outproj_phase(cx, cm, b_oT(myO2), a_mow, xs_[3], xs_[4], NTC, 2048, DM, 1.0)
        with cx.scope():
            ffn_phase(cx, cm, xs_[4], xs_[5], fw[3][0], fw[3][1], fw[3][2], NTC, DM, DFF, 1e-6)
        with cx.scope():
            final_norm_phase(cx, cm, xs_[5], a_out, a_fn, NTC, DM)
        cx.emit()
    return nc


def kernel(x, positions, ffn_norm_w, ffn_w_in, ffn_w_out, mix_norm_w, nsa_w_in, nsa_cmp_pe, nsa_cmp_w1, nsa_cmp_w2,
           nsa_w_out, mla_w_in, mla_q_norm_w, mla_kv_norm_w, mla_w_uq, mla_w_ukv, mla_w_out, final_norm_w):
    f = lambda a: np.ascontiguousarray(np.asarray(a))
    x = f(x); positions = f(positions).astype(np.int32)
    ffn_norm_w, ffn_w_in, ffn_w_out, mix_norm_w = f(ffn_norm_w), f(ffn_w_in), f(ffn_w_out), f(mix_norm_w)
    nc = build_program()
    shared = {"inv128": invsign_array(128), "inv64": invsign_array(64), "consts": consts_array(), "cbf": nsa_consts_bf16(),
              "ab": nsa_ab_table(SEQ_), "mw_rep0": rep128(mix_norm_w[0]), "mw_rep1": rep128(mix_norm_w[1]),
              "nsa_w_in": f(nsa_w_in)[0], "w1": f(nsa_cmp_w1)[0], "w2": f(nsa_cmp_w2)[0],
              "peT": np.ascontiguousarray(f(nsa_cmp_pe)[0].transpose(0, 2, 1)), "nsa_w_out": f(nsa_w_out)[0],
              "mla_w_in": f(mla_w_in)[0], "qn_rep": rep128(f(mla_q_norm_w)[0]), "kvn_rep": rep128(f(mla_kv_norm_w)[0]),
              "w_uq": f(mla_w_uq)[0], "w_ukv": f(mla_w_ukv)[0], "mla_w_out": f(mla_w_out)[0], "fn_rep": rep128(f(final_norm_w))}
    for k, (i, j) in enumerate([(0, 0), (0, 1), (1, 0), (1, 1)]):
        shared["fw_rep%d" % k] = rep128(ffn_norm_w[i, j])
        shared["f_w_in%d" % k] = ffn_w_in[i, j]
        shared["f_w_out%d" % k] = ffn_w_out[i, j]
    ncmp = (SEQ_ - 32) // 16 + 1
    maps = []
    for c in range(NCORE):
        b, r = c // 4, c % 4
        posc = np.zeros(1024, np.int32)
        posc[:ncmp] = positions[b, 16 * np.arange(ncmp) + 31]
        m = dict(shared)
        m["x"] = np.ascontiguousarray(x[b, r * NTC:(r + 1) * NTC])
        m["rank"] = np.array([[r, 0]], np.int32)
        m["pos"] = np.ascontiguousarray(np.tile(positions[b, r * NTC:(r + 1) * NTC][None, :], (128, 1)))
        m["pos_cmp"] = np.ascontiguousarray(np.tile(posc[None], (128, 1)))
        maps.append(m)
    res = run_bass_kernel_spmd(nc, maps, core_ids=list(range(NCORE)))
    out = np.empty((2, SEQ_, DM), np.float32)
    for c in range(NCORE):
        b, r = c // 4, c % 4
        out[b, r * NTC:(r + 1) * NTC] = np.asarray(res.results[c]["out"])
    return out
```

```python
import numpy as np
import ml_dtypes
from contextlib import ExitStack
import concourse.bass as bass
import concourse.mybir as mybir
from concourse.bass_utils import run_bass_kernel_spmd

F32 = mybir.dt.float32
BF16 = mybir.dt.bfloat16
I32 = mybir.dt.int32
AF = mybir.ActivationFunctionType
ALU = mybir.AluOpType
AX = mybir.AxisListType


class Buf:
    def __init__(self, t, name, dma_sem=None):
        self.t = t
        self.name = name
        self.w = None
        self.r = []
        self.dma_ent = dma_sem

    def __getitem__(self, idx):
        return self.t[idx]


CC_INC = 1


class Ctx:
    ENG = ("pe", "dve", "act", "pool", "sp")

    def __init__(self, nc, stack):
        self.nc = nc
        self.stack = stack
        self.root_stack = stack
        self.streams = {e: [] for e in self.ENG}
        self.sem = {}
        self.count = {e: 0 for e in self.ENG}
        self.seen = {e: {} for e in self.ENG}
        self.pending = {e: False for e in self.ENG}
        for e in self.ENG:
            self.sem[e] = stack.enter_context(nc.semaphore("sem_" + e))
        self.n_dma_sems = 0
        self.uid = 0
        self.free_dsems = []
        self.scope_dsems = []

    def _name(self, name):
        self.uid += 1
        return "%s_%d" % (name, self.uid)

    def new_dma_sem(self):
        if self.free_dsems:
            ent = self.free_dsems.pop()
        else:
            self.n_dma_sems += 1
            sem = self.root_stack.enter_context(self.nc.semaphore("dsem_%d" % self.n_dma_sems))
            ent = [sem, 0, "d%d" % self.n_dma_sems]
        self.scope_dsems.append(ent)
        return ent

    def barrier(self):
        for e in self.ENG:
            assert not self.pending[e], "engine %s has un-signalled trailing ops" % e
        for e in self.ENG:
            for o in self.ENG:
                if o != e and self.count[o] > 0:
                    self._wait(e, (o, self.sem[o], self.count[o]))
            for ent in self.scope_dsems:
                if ent[1] > 0:
                    self._wait(e, (ent[2], ent[0], ent[1]))

    def scope(self):
        cx = self

        class _Scope:
            def __enter__(s):
                s.saved_stack = cx.stack
                s.saved_dsems = cx.scope_dsems
                s.es = ExitStack()
                s.es.__enter__()
                cx.stack = s.es
                cx.scope_dsems = []
                return s

            def __exit__(s, *a):
                cx.barrier()
                cx.free_dsems.extend(cx.scope_dsems)
                cx.scope_dsems = s.saved_dsems
                cx.stack = s.saved_stack
                s.es.__exit__(None, None, None)
                return False

        return _Scope()

    def sbuf(self, name, shape, dt, dma=False):
        t = self.stack.enter_context(self.nc.sbuf_tensor(self._name(name), list(shape), dt))
        return Buf(t, name, self.new_dma_sem() if dma else None)

    def psum(self, name, shape, dt):
        t = self.stack.enter_context(self.nc.psum_tensor(self._name(name), list(shape), dt))
        return Buf(t, name)

    def dram_buf(self, ap, name):
        return Buf(ap, name, self.new_dma_sem())

    def _wait(self, eng, tok, force=False):
        if tok is None:
            return
        sem_key, sem, val = tok
        if sem_key == eng and not force:
            return
        if sem_key == eng and val > self.count[eng]:
            return
        if self.seen[eng].get(sem_key, 0) >= val:
            return
        self.seen[eng][sem_key] = val
        self.streams[eng].append(("wait", sem, val))

    def op(self, eng, fn, reads=(), writes=(), inc=True, scalars=()):
        for b in scalars:
            self._wait(eng, b.w, force=True)
        reads = list(reads) + list(scalars)
        for b in reads:
            self._wait(eng, b.w)
        for b in writes:
            self._wait(eng, b.w)
            for t in b.r:
                self._wait(eng, t)
        if inc:
            self.count[eng] += 1
            tok = (eng, self.sem[eng], self.count[eng])
            self.pending[eng] = False
        else:
            tok = (eng, self.sem[eng], self.count[eng] + 1)
            self.pending[eng] = True
        self.streams[eng].append(("op", fn, self.sem[eng] if inc else None))
        for b in writes:
            b.w = tok
            b.r = []
        for b in reads:
            if b in writes:
                continue
            b.r = [t for t in b.r if t[0] != eng] + [tok]

    def dma(self, q, fn, src, dst, n=1):
        if src is not None:
            self._wait(q, src.w)
        if dst is not None:
            self._wait(q, dst.w)
            for t in dst.r:
                self._wait(q, t)
        holder = dst if (dst is not None and dst.dma_ent is not None) else src
        assert holder is not None and holder.dma_ent is not None, "dma needs a Buf with dma_sem"
        ent = holder.dma_ent
        ent[1] += 16 * n
        tok = (ent[2], ent[0], ent[1])
        self.streams[q].append(("dma", fn, ent[0]))
        if dst is not None:
            dst.w = tok
            dst.r = []
        if src is not None:
            src.r = src.r + [tok]

    def cc(self, fn, src, dst):
        q = "pool"
        self._wait(q, src.w)
        self._wait(q, dst.w)
        for t in dst.r:
            self._wait(q, t)
        ent = dst.dma_ent
        ent[1] += CC_INC
        tok = (ent[2], ent[0], ent[1])
        self.streams[q].append(("cc", fn, ent[0]))
        dst.w = tok
        dst.r = []
        src.r = src.r + [tok]

    def load_rank(self, eng, rank_ap):
        holder = {}

        def fn(e):
            reg = e.alloc_register("rank_reg_%s" % eng)
            e.reg_load(reg, rank_ap[0:1, 0:1])
            holder["v"] = e.snap(reg, min_val=0, max_val=3)
            return None

        self.streams[eng].append(("raw", fn))
        return lambda: holder["v"]

    def wait_all(self, eng, bufs):
        for b in bufs:
            self._wait(eng, b.w)
            for t in b.r:
                self._wait(eng, t)

    def emit(self):
        nc = self.nc
        hmap = {"pe": "tensor", "dve": "vector", "act": "scalar", "pool": "gpsimd", "sp": "sync"}
        with nc.Block() as block:
            for ename in self.ENG:
                stream = self.streams[ename]

                def body(e, stream=stream):
                    for item in stream:
                        if item[0] == "wait":
                            e.wait_ge(item[1], item[2])
                        elif item[0] == "op":
                            ins = item[1](e)
                            if item[2] is not None:
                                ins.then_inc(item[2], 1)
                        elif item[0] == "raw":
                            item[1](e)
                        elif item[0] == "cc":
                            item[1](e).then_inc(item[2], CC_INC)
                        else:
                            lst = item[1](e)
                            for ins in lst:
                                ins.then_inc(item[2], 16)

                getattr(block, hmap[ename])(body)


class Common:
    def __init__(self, cx, consts_ap):
        self.cx = cx
        nc = cx.nc
        self.ident_f = cx.sbuf("ident_f", [128, 128], F32, dma=True)
        self.ident = cx.sbuf("ident", [128, 128], BF16)
        cx.dma("sp", lambda e: [e.dma_start(out=self.ident_f[:, :], in_=consts_ap[:, 0:128])], None, self.ident_f)
        cx.op("dve", lambda e: e.tensor_copy(out=self.ident[:, :], in_=self.ident_f[:, :]), [self.ident_f], [self.ident])
        self.eps = cx.sbuf("eps", [128, 1], F32)
        cx.op("dve", lambda e: e.memset(self.eps[:, :], 1e-6), [], [self.eps])
        cx.eps_tile = self.eps


def eps_ap(cx):
    return cx.eps_tile[:, 0:1]


def sincos_reduce(cx, t, shift, shape):
    P, N = shape
    ki = cx.sbuf("sc_ki", [P, N], I32)
    kf = cx.sbuf("sc_kf", [P, N], F32)
    TWO_PI = 2.0 * np.pi
    C1 = 6.28125
    C2 = TWO_PI - C1
    if shift != 0.0:
        cx.op("dve", lambda e: e.tensor_scalar(out=t[:, :], in0=t[:, :], scalar1=float(shift), scalar2=None, op0=ALU.add), [t], [t])
    cx.op("dve", lambda e: e.tensor_scalar(out=ki[:, :], in0=t[:, :], scalar1=float(1.0 / TWO_PI), scalar2=None, op0=ALU.mult), [t], [ki])
    cx.op("dve", lambda e: e.tensor_copy(out=kf[:, :], in_=ki[:, :]), [ki], [kf])
    cx.op("dve", lambda e: e.scalar_tensor_tensor(out=t[:, :], in0=kf[:, :], scalar=-C1, in1=t[:, :], op0=ALU.mult, op1=ALU.add), [kf, t], [t])
    cx.op("dve", lambda e: e.scalar_tensor_tensor(out=t[:, :], in0=kf[:, :], scalar=-C2, in1=t[:, :], op0=ALU.mult, op1=ALU.add), [kf, t], [t])
    cx.op("dve", lambda e: e.tensor_scalar(out=kf[:, :], in0=t[:, :], scalar1=float(-np.pi), scalar2=None, op0=ALU.is_lt), [t], [kf])
    cx.op("dve", lambda e: e.scalar_tensor_tensor(out=t[:, :], in0=kf[:, :], scalar=TWO_PI, in1=t[:, :], op0=ALU.mult, op1=ALU.add), [kf, t], [t])
    cx.op("dve", lambda e: e.tensor_scalar(out=kf[:, :], in0=t[:, :], scalar1=float(np.pi), scalar2=None, op0=ALU.is_gt), [t], [kf])
    cx.op("dve", lambda e: e.scalar_tensor_tensor(out=t[:, :], in0=kf[:, :], scalar=-TWO_PI, in1=t[:, :], op0=ALU.mult, op1=ALU.add), [kf, t], [t])
    cx.op("dve", lambda e: e.tensor_scalar(out=t[:, :], in0=t[:, :], scalar1=float(np.pi), scalar2=float(-np.pi), op0=ALU.min, op1=ALU.max), [t], [t])


def rmsnorm_to_hT(cx, cm, x_tile, w_rep, hT, col0, D, eps, scratch):
    junk, ss, rstd, hbf = scratch["junk"], scratch["ss"], scratch["rstd"], scratch["hbf"]
    KC = D // 128
    cx.op("act", lambda e: e.activation(out=junk[:, :], in_=x_tile[:, 0:D], func=AF.Square, accum_out=ss[:, 0:1]),
          [x_tile], [junk, ss])
    cx.op("act", lambda e: e.activation(out=rstd[:, 0:1], in_=ss[:, 0:1], func=AF.Sqrt, scale=1.0 / D, bias=eps_ap(cx)),
          [ss], [rstd], scalars=[cx.eps_tile])
    cx.op("dve", lambda e: e.reciprocal(out=rstd[:, 0:1], in_=rstd[:, 0:1]), [rstd], [rstd])
    cx.op("dve", lambda e: e.scalar_tensor_tensor(out=hbf[:, 0:D], in0=x_tile[:, 0:D], scalar=rstd[:, 0:1],
                                                   in1=w_rep[:, 0:D], op0=ALU.mult, op1=ALU.mult),
          [x_tile, w_rep], [hbf], scalars=[rstd])
    transpose_to_fm(cx, cm, hbf, hT, col0, KC, scratch)


def transpose_to_fm(cx, cm, src, dstT, col0, KC, scratch):
    tps = scratch["tp"]
    for g0 in range(0, KC, 4):
        g = min(4, KC - g0)
        tp = tps[scratch["tpi"][0] % len(tps)]
        scratch["tpi"][0] += 1
        for i in range(g):
            kc = g0 + i
            cx.op("pe", lambda e, kc=kc, i=i, tp=tp: e.transpose(out=tp[:, i * 128:(i + 1) * 128],
                                                                  in_=src[:, kc * 128:(kc + 1) * 128],
                                                                  identity=cm.ident[:, :]),
                  [src, cm.ident], [tp], inc=(i == g - 1))
        cx.op("act", lambda e, g0=g0, g=g, tp=tp: e.copy(
            out=dstT[:, g0:g0 + g, col0:col0 + 128],
            in_=tp[:, 0:g * 128].rearrange("p (a b) -> p a b", a=g)), [tp], [dstT])


def ffn_phase(cx, cm, x_in, x_out, w_rep_ap, w_in, w_out, NT, D, F, eps, T=1024):
    nc = cx.nc
    KC = D // 128
    FC = F // 128
    NB = T // 128
    assert NT % T == 0 and FC % 2 == 0 and D % 256 == 0
    SLABW = max(32 * 256, FC * 256)
    w_rep = cx.sbuf("w_rep", [128, D], F32, dma=True)
    cx.dma("sp", lambda e: [e.dma_start(out=w_rep[:, :], in_=w_rep_ap[:, :])], None, w_rep)
    xts = [cx.sbuf("xt", [128, D], F32, dma=True) for _ in range(1)]
    hbf_ = cx.sbuf("hbf", [128, D], BF16)
    scratch = dict(junk=hbf_, ss=cx.sbuf("ss", [128, 1], F32),
                   rstd=cx.sbuf("rstd", [128, 1], F32), hbf=hbf_,
                   tp=[cx.psum("tp", [128, 512], BF16) for _ in range(2)], tpi=[0])
    hT = cx.sbuf("hT", [128, KC, T], BF16)
    hid = cx.sbuf("hid", [128, FC, T], BF16)
    slabs = [cx.sbuf("slab", [128, FC * 256], BF16, dma=True) for _ in range(2)]
    assert FC * 256 >= 2 * KC * 256
    slab_i = [0]
    pg = [cx.psum("pg", [128, 512], F32) for _ in range(2)]
    pu = [cx.psum("pu", [128, 512], F32) for _ in range(2)]
    py = [cx.psum("py", [128, 512], F32) for _ in range(2)]
    sg = [cx.sbuf("sg", [128, 512], BF16) for _ in range(2)]
    xs = [cx.sbuf("xs", [128, 256], F32, dma=True) for _ in range(3)]
    ys = [cx.sbuf("ys", [128, 256], F32, dma=True) for _ in range(3)]
    cnt = [0, 0]

    def next_slab(big=False):
        n = 2
        s = slabs[slab_i[0] % n]
        slab_i[0] += 1
        return s

    for st in range(NT // T):
        t0 = st * T
        for b in range(NB):
            xt = xts[0]
            r0 = t0 + b * 128
            cx.dma("sp", lambda e, xt=xt, r0=r0: [e.dma_start(out=xt[:, :], in_=x_in[r0:r0 + 128, :])], None, xt)
            rmsnorm_to_hT(cx, cm, xt, w_rep, hT, b * 128, D, eps, scratch)
        for jj in range(FC // 2):
            slab = next_slab()
            sv = slab[:, 0:2 * KC * 256].rearrange("p (a b) -> p a b", b=256)

            def ld(e, sv=sv, jj=jj):
                a = e.dma_start(out=sv[:, 0:KC, :],
                                in_=w_in[:, jj * 256:(jj + 1) * 256].rearrange("(kc p) n -> p kc n", p=128))
                b_ = e.dma_start(out=sv[:, KC:2 * KC, :],
                                 in_=w_in[:, F + jj * 256:F + (jj + 1) * 256].rearrange("(kc p) n -> p kc n", p=128))
                return [a, b_]

            cx.dma("pool", ld, None, slab, n=2)
            for j in range(2):
                c = jj * 2 + j
                for tt in range(T // 512):
                    k = cnt[0] % 2
                    cnt[0] += 1
                    for kc in range(KC):
                        cx.op("pe", lambda e, kc=kc, j=j, tt=tt, k=k, sv=sv: e.matmul(
                            pg[k][:, :], lhsT=sv[:, kc, j * 128:(j + 1) * 128], rhs=hT[:, kc, tt * 512:(tt + 1) * 512],
                            start=(kc == 0), stop=(kc == KC - 1)), [slab, hT], [pg[k]], inc=(kc == KC - 1))
                    for kc in range(KC):
                        cx.op("pe", lambda e, kc=kc, j=j, tt=tt, k=k, sv=sv: e.matmul(
                            pu[k][:, :], lhsT=sv[:, KC + kc, j * 128:(j + 1) * 128],
                            rhs=hT[:, kc, tt * 512:(tt + 1) * 512],
                            start=(kc == 0), stop=(kc == KC - 1)), [slab, hT], [pu[k]], inc=(kc == KC - 1))
                    cx.op("act", lambda e, k=k: e.activation(out=sg[k][:, :], in_=pg[k][:, :], func=AF.Silu),
                          [pg[k]], [sg[k]])
                    cx.op("dve", lambda e, k=k, c=c, tt=tt: e.tensor_tensor(
                        out=hid[:, c, tt * 512:(tt + 1) * 512], in0=sg[k][:, :], in1=pu[k][:, :], op=ALU.mult),
                        [sg[k], pu[k]], [hid])
        for nt in range(D // 256):
            slab = next_slab(big=True)
            sv = slab[:, 0:FC * 256].rearrange("p (a b) -> p a b", b=256)
            cx.dma("pool", lambda e, sv=sv, nt=nt: [e.dma_start(
                out=sv[:, :, :], in_=w_out[:, nt * 256:(nt + 1) * 256].rearrange("(c p) n -> p c n", p=128))],
                None, slab)
            for b in range(NB):
                k = cnt[1] % 2
                k4 = cnt[1] % 3
                cnt[1] += 1
                r0 = t0 + b * 128
                xsb, ysb = xs[k4], ys[k4]
                cx.dma("sp", lambda e, xsb=xsb, r0=r0, nt=nt: [e.dma_start(
                    out=xsb[:, :], in_=x_in[r0:r0 + 128, nt * 256:(nt + 1) * 256])], None, xsb)
                for c in range(FC):
                    cx.op("pe", lambda e, c=c, b=b, k=k, sv=sv: e.matmul(
                        py[k][:, 0:256], lhsT=hid[:, c, b * 128:(b + 1) * 128], rhs=sv[:, c, :],
                        start=(c == 0), stop=(c == FC - 1)), [slab, hid], [py[k]], inc=(c == FC - 1))
                cx.op("dve", lambda e, k=k, xsb=xsb, ysb=ysb: e.scalar_tensor_tensor(
                    out=ysb[:, :], in0=py[k][:, 0:256], scalar=0.5, in1=xsb[:, :], op0=ALU.mult, op1=ALU.add),
                    [py[k], xsb], [ysb])
                cx.dma("sp", lambda e, ysb=ysb, r0=r0, nt=nt: [e.dma_start(
                    out=x_out[r0:r0 + 128, nt * 256:(nt + 1) * 256], in_=ysb[:, :])], ysb, None)
    return ys


def finish(cx, out_bufs):
    cx.wait_all("sp", out_bufs)


def consts_array():
    c = np.zeros((128, 128), np.float32)
    c[:, 0:128] = np.eye(128, dtype=np.float32)
    return c


def outproj_phase(cx, cm, oT, w, x_in, x_out, NT, K, D, coef=1.0, T=1024):
    KC = K // 128
    NB = T // 128
    o_sb = cx.sbuf("o_sb", [128, KC * T], BF16, dma=True)
    slabs = [cx.sbuf("oslab", [128, KC * 256], BF16, dma=True) for _ in range(2)]
    py = [cx.psum("py", [128, 512], F32) for _ in range(2)]
    xs = [cx.sbuf("xs", [128, 256], F32, dma=True) for _ in range(3)]
    ys = [cx.sbuf("ys", [128, 256], F32, dma=True) for _ in range(3)]
    cnt = 0
    si = 0
    for st in range(NT // T):
        t0 = st * T
        cx.dma("sp", lambda e, t0=t0: [e.dma_start(out=o_sb[:, c * T:(c + 1) * T], in_=oT(c, t0, T)) for c in range(KC)], None, o_sb, n=KC)
        for nt in range(D // 256):
            slab = slabs[si % 2]
            si += 1
            sv = slab[:, 0:KC * 256].rearrange("p (a b) -> p a b", b=256)
            cx.dma("pool", lambda e, sv=sv, nt=nt: [e.dma_start(
                out=sv[:, :, :], in_=w[:, nt * 256:(nt + 1) * 256].rearrange("(c p) n -> p c n", p=128))],
                None, slab)
            for b in range(NB):
                k = cnt % 2
                k3 = cnt % 3
                cnt += 1
                r0 = t0 + b * 128
                xsb, ysb = xs[k3], ys[k3]
                cx.dma("sp", lambda e, xsb=xsb, r0=r0, nt=nt: [e.dma_start(
                    out=xsb[:, :], in_=x_in[r0:r0 + 128, nt * 256:(nt + 1) * 256])], None, xsb)
                for c in range(KC):
                    cx.op("pe", lambda e, c=c, b=b, k=k, sv=sv: e.matmul(
                        py[k][:, 0:256], lhsT=o_sb[:, c * T + b * 128:c * T + (b + 1) * 128], rhs=sv[:, c, :],
                        start=(c == 0), stop=(c == KC - 1)), [slab, o_sb], [py[k]], inc=(c == KC - 1))
                cx.op("dve", lambda e, k=k, xsb=xsb, ysb=ysb: e.scalar_tensor_tensor(
                    out=ysb[:, :], in0=py[k][:, 0:256], scalar=float(coef), in1=xsb[:, :], op0=ALU.mult, op1=ALU.add),
                    [py[k], xsb], [ysb])
                cx.dma("sp", lambda e, ysb=ysb, r0=r0, nt=nt: [e.dma_start(
                    out=x_out[r0:r0 + 128, nt * 256:(nt + 1) * 256], in_=ysb[:, :])], ysb, None)


def final_norm_phase(cx, cm, x_in, x_out, w_rep_ap, NT, D):
    w_rep = cx.sbuf("fw_rep", [128, D], F32, dma=True)
    cx.dma("sp", lambda e: [e.dma_start(out=w_rep[:, :], in_=w_rep_ap[:, :])], None, w_rep)
    xts = [cx.sbuf("fxt", [128, D], F32, dma=True) for _ in range(2)]
    yts = [cx.sbuf("fyt", [128, D], F32, dma=True) for _ in range(2)]
    junk = cx.sbuf("fjunk", [128, D], BF16)
    ss = cx.sbuf("fss", [128, 1], F32)
    rstd = cx.sbuf("frstd", [128, 1], F32)
    for b in range(NT // 128):
        xt, yt = xts[b % 2], yts[b % 2]
        cx.dma("sp", lambda e, xt=xt, b=b: [e.dma_start(out=xt[:, :], in_=x_in[b * 128:(b + 1) * 128, :])], None, xt)
        cx.op("act", lambda e, xt=xt: e.activation(out=junk[:, :], in_=xt[:, :], func=AF.Square, accum_out=ss[:, 0:1]),
              [xt], [junk, ss])
        cx.op("act", lambda e: e.activation(out=rstd[:, 0:1], in_=ss[:, 0:1], func=AF.Sqrt, scale=1.0 / D, bias=eps_ap(cx)),
              [ss], [rstd], scalars=[cx.eps_tile])
        cx.op("dve", lambda e: e.reciprocal(out=rstd[:, 0:1], in_=rstd[:, 0:1]), [rstd], [rstd])
        cx.op("dve", lambda e, xt=xt, yt=yt: e.scalar_tensor_tensor(
            out=yt[:, :], in0=xt[:, :], scalar=rstd[:, 0:1], in1=w_rep[:, :], op0=ALU.mult, op1=ALU.mult),
            [xt, w_rep], [yt], scalars=[rstd])
        cx.dma("sp", lambda e, yt=yt, b=b: [e.dma_start(out=x_out[b * 128:(b + 1) * 128, :], in_=yt[:, :])], yt, None)


def rope_tables(cx, pos_ap, invsign, P, N):
    C = cx.sbuf("rp_C", [P, N], F32)
    S = cx.sbuf("rp_S", [P, N], F32)
    with cx.scope():
        pi_ = cx.sbuf("rp_pi", [P, N], I32, dma=True)
        cx.dma("sp", lambda e: [e.dma_start(out=pi_[:, :], in_=pos_ap[0:P, 0:N])], None, pi_)
        ang = cx.sbuf("rp_ang", [P, N], F32)
        t = cx.sbuf("rp_t", [P, N], F32)
        cx.op("dve", lambda e: e.tensor_copy(out=ang[:, :], in_=pi_[:, :]), [pi_], [ang])
        cx.op("dve", lambda e: e.tensor_scalar(out=ang[:, :], in0=ang[:, :], scalar1=invsign[0:P, 0:1], scalar2=None,
                                                op0=ALU.mult), [ang], [ang], scalars=[invsign])
        cx.op("dve", lambda e: e.tensor_copy(out=t[:, :], in_=ang[:, :]), [ang], [t])
        sincos_reduce(cx, t, 0.0, [P, N])
        cx.op("act", lambda e: e.activation(out=S[:, :], in_=t[:, :], func=AF.Sin), [t], [S])
        cx.op("dve", lambda e: e.tensor_scalar(out=S[:, :], in0=S[:, :], scalar1=invsign[0:P, 1:2], scalar2=None,
                                                op0=ALU.mult), [S], [S], scalars=[invsign])
        sincos_reduce(cx, ang, float(np.pi / 2), [P, N])
        cx.op("act", lambda e: e.activation(out=C[:, :], in_=ang[:, :], func=AF.Sin), [ang], [C])
    return C, S


def load_invsign(cx, ap):
    b = cx.sbuf("invsign", [128, 2], F32, dma=True)
    cx.dma("sp", lambda e: [e.dma_start(out=b[:, :], in_=ap[:, :])], None, b)
    return b


def invsign_array(dim):
    half = dim // 2
    inv = (1.0 / (10000.0 ** (np.arange(0, dim, 2, dtype=np.float32) / np.float32(dim)))).astype(np.float32)
    a = np.zeros((128, 2), np.float32)
    for d in range(dim):
        a[d, 0] = inv[d % half]
        a[d, 1] = -1.0 if d < half else 1.0
    return a


class ProjWS:
    def __init__(self, cx, D, T, n_slab_elems):
        KC = D // 128
        self.KC, self.T, self.D = KC, T, D
        self.w_rep = cx.sbuf("pw_rep", [128, D], F32, dma=True)
        self.xt = cx.sbuf("pxt", [128, D], F32, dma=True)
        hbf = cx.sbuf("phbf", [128, D], BF16)
        self.scratch = dict(junk=hbf, ss=cx.sbuf("pss", [128, 1], F32), rstd=cx.sbuf("prstd", [128, 1], F32), hbf=hbf,
                            tp=[cx.psum("ptp", [128, 512], BF16) for _ in range(2)], tpi=[0])
        self.hT = cx.sbuf("phT", [128, KC, T], BF16)
        self.slabs = [cx.sbuf("pslab", [128, n_slab_elems], BF16, dma=True) for _ in range(2)]
        self.si = 0
        self.pa = [cx.psum("ppa", [128, 512], F32) for _ in range(2)]
        self.pb = [cx.psum("ppb", [128, 512], F32) for _ in range(2)]
        self.pi = 0
        self.t1 = [cx.sbuf("pt1", [128, 512], F32) for _ in range(2)]
        self.t2 = [cx.sbuf("pt2", [128, 512], F32) for _ in range(2)]
        self.ob = [cx.sbuf("pob", [128, 512], BF16, dma=True) for _ in range(3)]
        self.oi = 0

    def slab(self):
        s = self.slabs[self.si % 2]
        self.si += 1
        return s

    def out_buf(self):
        o = self.ob[self.oi % 3]
        self.oi += 1
        return o


def load_wcols(cx, slab, KC, W, w_ap, pieces):
    sv = slab[:, 0:KC * W].rearrange("p (a b) -> p a b", b=W)

    def ld(e):
        out = []
        for (dc, sc, wd) in pieces:
            out.append(e.dma_start(out=sv[:, :, dc:dc + wd],
                                   in_=w_ap[:, sc:sc + wd].rearrange("(kc p) n -> p kc n", p=128)))
        return out

    cx.dma("pool", ld, None, slab, n=len(pieces))
    return sv


def fm_mm(cx, ps, M, sv, c0, slab, xT, xbuf, cols, KC):
    n = cols[1] - cols[0]
    for kc in range(KC):
        cx.op("pe", lambda e, kc=kc: e.matmul(ps[0:M, 0:n], lhsT=sv[:, kc, c0:c0 + M], rhs=xT[:, kc, cols[0]:cols[1]],
                                              start=(kc == 0), stop=(kc == KC - 1)),
              [slab, xbuf], [ps], inc=(kc == KC - 1))


def tm_mm(cx, ps, N, sv, c0, slab, xT, xbuf, tok0, KC):
    for kc in range(KC):
        cx.op("pe", lambda e, kc=kc: e.matmul(ps[:, 0:N], lhsT=xT[:, kc, tok0:tok0 + 128], rhs=sv[:, kc, c0:c0 + N],
                                              start=(kc == 0), stop=(kc == KC - 1)),
              [slab, xbuf], [ps], inc=(kc == KC - 1))


def norm_block_to_hT(cx, cm, ws, x_in, r0, col0, eps=1e-6):
    cx.dma("sp", lambda e: [e.dma_start(out=ws.xt[:, :], in_=x_in[r0:r0 + 128, :])], None, ws.xt)
    rmsnorm_to_hT(cx, cm, ws.xt, ws.w_rep, ws.hT, col0, ws.D, eps, ws.scratch)


def rope_out(cx, ws, px, pp, M, C, S, tcol, dst_ap):
    i = ws.pi % 2
    t1, t2 = ws.t1[i], ws.t2[i]
    ob = ws.out_buf()
    cx.op("dve", lambda e: e.tensor_tensor(out=t1[0:M, :], in0=px[0:M, :], in1=C[0:M, tcol:tcol + 512], op=ALU.mult),
          [px, C], [t1])
    cx.op("dve", lambda e: e.tensor_tensor(out=t2[0:M, :], in0=pp[0:M, :], in1=S[0:M, tcol:tcol + 512], op=ALU.mult),
          [pp, S], [t2])
    cx.op("dve", lambda e: e.tensor_tensor(out=ob[0:M, :], in0=t1[0:M, :], in1=t2[0:M, :], op=ALU.add), [t1, t2], [ob])
    cx.dma("sp", lambda e: [e.dma_start(out=dst_ap, in_=ob[0:M, :])], ob, None)


def plain_out(cx, ws, px, M, N, dst_ap, func=None, src3=False):
    ob = ws.out_buf()
    if func is None:
        cx.op("act", lambda e: e.copy(out=ob[0:M, 0:N], in_=px[0:M, 0:N]), [px], [ob])
    else:
        cx.op("act", lambda e: e.activation(out=ob[0:M, 0:N], in_=px[0:M, 0:N], func=func), [px], [ob])
    if src3:
        cx.dma("sp", lambda e: [e.dma_start(out=dst_ap, in_=ob[0:M, 0:N].rearrange("p (j d) -> p j d", d=128))], ob, None)
    else:
        cx.dma("sp", lambda e: [e.dma_start(out=dst_ap, in_=ob[0:M, 0:N])], ob, None)


def nsa_proj_phase(cx, cm, x_in, w_rep_ap, w_in, pos_ap, invsign_ap, outs, NT, D, T=1024):
    KC = D // 128
    QC, KVC = 2048, 3072
    invsign = load_invsign(cx, invsign_ap)
    C, S = rope_tables(cx, pos_ap, invsign, 128, NT)
    ws = ProjWS(cx, D, T, KC * 512)
    cx.dma("sp", lambda e: [e.dma_start(out=ws.w_rep[:, :], in_=w_rep_ap[:, :])], None, ws.w_rep)
    for st in range(NT // T):
        t0 = st * T
        for b in range(T // 128):
            norm_block_to_hT(cx, cm, ws, x_in, t0 + b * 128, b * 128)
        roped = [("qT", h, h * 128) for h in range(16)] + \
                [("ksT", g, QC + 1 * 1024 + g * 128) for g in range(4)] + \
                [("kwT", g, QC + 2 * 1024 + g * 128) for g in range(4)]
        for (name, idx, c0) in roped:
            slab = ws.slab()
            sv = load_wcols(cx, slab, KC, 256, w_in, [(0, c0, 128), (128, c0 + 64, 64), (192, c0, 64)])
            for tt in range(T // 512):
                i = ws.pi % 2
                ws.pi += 1
                fm_mm(cx, ws.pa[i], 128, sv, 0, slab, ws.hT, ws.hT, (tt * 512, tt * 512 + 512), KC)
                fm_mm(cx, ws.pb[i], 128, sv, 128, slab, ws.hT, ws.hT, (tt * 512, tt * 512 + 512), KC)
                ws.pi -= 1
                rope_out(cx, ws, ws.pa[i], ws.pb[i], 128, C, S, t0 + tt * 512,
                         outs[name][idx * 128:(idx + 1) * 128, t0 + tt * 512:t0 + tt * 512 + 512])
                ws.pi += 1
        plain = [("kcT", g, QC + 0 * 1024 + g * 128) for g in range(4)] + \
                [("vcT", g, QC + 0 * 1024 + 512 + g * 128) for g in range(4)]
        for (name, idx, c0) in plain:
            slab = ws.slab()
            sv = load_wcols(cx, slab, KC, 128, w_in, [(0, c0, 128)])
            for tt in range(T // 512):
                i = ws.pi % 2
                ws.pi += 1
                fm_mm(cx, ws.pa[i], 128, sv, 0, slab, ws.hT, ws.hT, (tt * 512, tt * 512 + 512), KC)
                plain_out(cx, ws, ws.pa[i], 128, 512,
                          outs[name][idx * 128:(idx + 1) * 128, t0 + tt * 512:t0 + tt * 512 + 512])
        slab = ws.slab()
        sv = load_wcols(cx, slab, KC, 48, w_in, [(0, QC + KVC, 48)])
        for tt in range(T // 512):
            i = ws.pi % 2
            ws.pi += 1
            fm_mm(cx, ws.pa[i], 48, sv, 0, slab, ws.hT, ws.hT, (tt * 512, tt * 512 + 512), KC)
            plain_out(cx, ws, ws.pa[i], 48, 512, outs["gT"][0:48, t0 + tt * 512:t0 + tt * 512 + 512], func=AF.Sigmoid)
        for (name, c0) in (("vs", QC + 1 * 1024 + 512), ("vw", QC + 2 * 1024 + 512)):
            slab = ws.slab()
            sv = load_wcols(cx, slab, KC, 512, w_in, [(0, c0, 512)])
            for b in range(T // 128):
                i = ws.pi % 2
                ws.pi += 1
                tm_mm(cx, ws.pa[i], 512, sv, 0, slab, ws.hT, ws.hT, b * 128, KC)
                plain_out(cx, ws, ws.pa[i], 128, 512, outs[name](t0 + b * 128), src3=True)


class AttnWS:
    def __init__(self, cx, n_p=6):
        self.S = [cx.psum("aS", [128, 512], F32) for _ in range(3)]
        self.O = [cx.psum("aO", [128, 512], F32) for _ in range(2)]
        l_ = cx.psum("aL", [128, 512], F32)
        self.L = [l_, l_]
        self.P = [cx.sbuf("aP", [128, 512], BF16) for _ in range(n_p)]
        self.ones = cx.sbuf("aones", [128, 128], BF16)
        cx.op("dve", lambda e: e.memset(self.ones[:, :], 1.0), [], [self.ones])
        self.rl = [cx.sbuf("arl", [128, 512], F32) for _ in range(2)]
        self.Lacc = [cx.sbuf("aLacc", [128, 512], F32) for _ in range(2)]
        self.LaccB = [cx.sbuf("aLaccB", [128, 512], F32) for _ in range(2)]
        self.use_pool = False
        self.parity = 0
        self.ones_f = cx.sbuf("aones_f", [128, 128], F32)
        cx.op("dve", lambda e: e.memset(self.ones_f[:, :], 1.0), [], [self.ones_f])
        self.si = 0
        self.pi = 0
        self.oi = 0


def attn_score(cx, W, qk, clo, chi, scale, Pdst=None, blkmask=None, tilemask=None, addmask=None):
    Sp = W.S[W.si % len(W.S)]
    W.si += 1
    n_mm = len(qk) + (1 if addmask is not None else 0)
    j = 0
    for (qb, qa, kb, ka) in qk:
        j += 1
        cx.op("pe", lambda e, qa=qa, ka=ka, j=j: e.matmul(Sp[:, clo:chi], lhsT=ka, rhs=qa[:, clo:chi],
                                                          start=(j == 1), stop=(j == n_mm)),
              [qb, kb], [Sp], inc=(j == n_mm))
    if addmask is not None:
        lb, la, rb, ra = addmask
        cx.op("pe", lambda e: e.matmul(Sp[:, clo:chi], lhsT=la, rhs=ra[:, clo:chi], start=False, stop=True),
              [lb, rb], [Sp], inc=True)
    if Pdst is None:
        Pb = W.P[W.pi % len(W.P)]
        W.pi += 1
        Pa = Pb[:, :]
    else:
        Pb, Pa = Pdst
    cx.op("act", lambda e: e.activation(out=Pa[:, clo:chi], in_=Sp[:, clo:chi], func=AF.Exp, scale=float(scale)),
          [Sp], [Pb])
    if blkmask is not None:
        mb, ma, c0 = blkmask
        cx.op("dve", lambda e: e.tensor_tensor(out=Pa[:, c0:c0 + 128], in0=Pa[:, c0:c0 + 128], in1=ma, op=ALU.mult),
              [Pb, mb], [Pb])
    if tilemask is not None:
        mb, ma = tilemask
        cx.op("dve", lambda e: e.tensor_tensor(out=Pa[:, clo:chi], in0=Pa[:, clo:chi], in1=ma, op=ALU.mult),
              [Pb, mb], [Pb])
    return Pb, Pa


def attn_pv(cx, W, ol, P, v, clo, chi, first, last):
    Pb, Pa = P
    vb, va = v
    O, L = W.O[ol], W.L[ol]
    cx.op("pe", lambda e: e.matmul(O[:, clo:chi], lhsT=va, rhs=Pa[:, clo:chi], start=first, stop=last),
          [vb, Pb], [O], inc=last)
    La, Lb = W.Lacc[ol], W.LaccB[ol]
    if first:
        W.parity = 0
        cx.op("dve", lambda e: e.tensor_copy(out=La[:, clo:chi], in_=Pa[:, clo:chi]), [Pb], [La])
        if W.use_pool:
            cx.op("pool", lambda e: e.memset(Lb[:, :], 0.0), [], [Lb])
    else:
        W.parity += 1
        if W.use_pool and W.parity % 2 == 1:
            cx.op("pool", lambda e: e.tensor_tensor(out=Lb[:, clo:chi], in0=Lb[:, clo:chi], in1=Pa[:, clo:chi], op=ALU.add),
                  [Pb, Lb], [Lb])
        else:
            cx.op("dve", lambda e: e.tensor_tensor(out=La[:, clo:chi], in0=La[:, clo:chi], in1=Pa[:, clo:chi], op=ALU.add),
                  [Pb, La], [La])
    if last:
        if W.use_pool:
            cx.op("pe", lambda e: e.matmul(L[:, :], lhsT=W.ones_f[:, :], rhs=La[:, :], start=True, stop=False),
                  [W.ones_f, La], [L], inc=False)
            cx.op("pe", lambda e: e.matmul(L[:, :], lhsT=W.ones_f[:, :], rhs=Lb[:, :], start=False, stop=True),
                  [W.ones_f, Lb], [L], inc=True)
        else:
            cx.op("pe", lambda e: e.matmul(L[:, :], lhsT=W.ones_f[:, :], rhs=La[:, :], start=True, stop=True),
                  [W.ones_f, La], [L], inc=True)


def attn_chunk(cx, W, ol, qk, v, clo, chi, first, last, scale, Pdst=None, blkmask=None, tilemask=None, addmask=None):
    P = attn_score(cx, W, qk, clo, chi, scale, Pdst, blkmask, tilemask, addmask)
    attn_pv(cx, W, ol, P, v, clo, chi, first, last)


class AttnPipe:
    def __init__(self, cx, W, depth=2):
        self.cx, self.W = cx, W
        self.pending = []
        self.depth = depth

    def push(self, ol, qk, v, clo, chi, first, last, scale, after=None, **kw):
        P = attn_score(self.cx, self.W, qk, clo, chi, scale, **kw)
        self.pending.append((ol, P, v, clo, chi, first, last, after))
        while len(self.pending) > self.depth:
            self._drain()

    def _drain(self):
        ol, P, v, clo, chi, first, last, after = self.pending.pop(0)
        attn_pv(self.cx, self.W, ol, P, v, clo, chi, first, last)
        if after is not None:
            after()

    def flush(self):
        while self.pending:
            self._drain()


def attn_rl(cx, W, ol, gate=None):
    rl = W.rl[W.oi % 2]
    W.oi += 1
    L = W.L[ol]
    cx.op("dve", lambda e: e.tensor_scalar(out=rl[:, :], in0=L[:, :], scalar1=1e-30, scalar2=None, op0=ALU.max), [L], [rl])
    cx.op("dve", lambda e: e.reciprocal(out=rl[:, :], in_=rl[:, :]), [rl], [rl])
    if gate is not None:
        cx.op("dve", lambda e: e.tensor_tensor(out=rl[:, :], in0=rl[:, :], in1=gate[:, :], op=ALU.mult), [rl, gate], [rl])
    return rl


CB_TRI, CB_TRI2, CB_CMASK, CB_OV, CB_ENEG, CB_SELG = 0, 128, 256, 256 + 2560, 256 + 2560 + 2048, 256 + 2560 + 2048 + 8192
CB_W = CB_SELG + 12 * 128


def nsa_consts_bf16():
    c = np.zeros((128, CB_W), np.float32)
    k = np.arange(128)[:, None]
    q = np.arange(128)[None, :]
    c[:, CB_TRI:CB_TRI + 128] = (k <= q)
    c[:, CB_TRI2:CB_TRI2 + 128] = (q < k)
    nl = np.arange(128)[:, None]
    tl = np.arange(512)[None, :]
    for d in range(5):
        c[:, CB_CMASK + d * 512:CB_CMASK + (d + 1) * 512] = (16 * nl + 31 <= 512 * d + tl)
    for ch in range(8):
        n = 128 * ch + np.arange(128)[:, None]
        j = np.arange(256)[None, :]
        ov = (16 * n <= 64 * j + 63) & (16 * n + 31 >= 64 * j) & (n < 1023)
        c[:, CB_OV + ch * 256:CB_OV + (ch + 1) * 256] = ov
    jl = np.arange(128)[:, None]
    key = np.arange(128)[None, :]
    for kcl in range(64):
        c[:, CB_ENEG + kcl * 128:CB_ENEG + (kcl + 1) * 128] = np.where(jl == 2 * kcl + key // 64, -30000.0, 0.0)
    for i in range(12):
        c[i, CB_SELG + i * 128:CB_SELG + (i + 1) * 128] = 1.0
    return c.astype(ml_dtypes.bfloat16)


def nsa_ab_table(S):
    nb = S // 128
    t = (np.arange(nb)[:, None] * 128 + np.arange(128)[None, :])[:, :, None]
    cur = t // 64
    j = np.arange(256)[None, None, :]
    A = (j <= cur).astype(np.float32)
    B = np.where(j > cur, -1e30, 0.0).astype(np.float32)
    B = np.where((j == cur - 1) & (j > 0), 3e9, B)
    B = np.where((j == cur) & (j > 0), 2e9, B)
    B = np.where((j == 0), 1e9, B)
    B = np.where(j > cur, -1e30, B).astype(np.float32)
    return np.ascontiguousarray(np.concatenate([A, B], axis=-1))


def nsa_attn_phase(cx, cm, io, S, NTC=4096):
    NQT = S // 512
    NKC = S // 128
    NCMP = (S - 32) // 16 + 1
    NCC = (NCMP + 127) // 128
    scale = 128.0 ** -0.5
    cbf = cx.sbuf("cbf", [128, CB_W], BF16, dma=True)
    cx.dma("sp", lambda e: [e.dma_start(out=cbf[:, :], in_=io["cbf"][:, :])], None, cbf)
    kcmpT = cx.sbuf("kcmpT", [128, NCC * 128], BF16)
    vcmp = cx.sbuf("vcmp", [128, NCC, 128], BF16)
    cx.op("dve", lambda e: e.memset(kcmpT[:, :], 0.0), [], [kcmpT])
    cx.op("dve", lambda e: e.memset(vcmp[:, :, :], 0.0), [], [vcmp])
    with cx.scope():
        invsign = load_invsign(cx, io["invsign"])
        Cc, Sc = rope_tables(cx, io["pos_cmp"], invsign, 128, NCC * 128)
        tokT = cx.sbuf("tokT", [128, S], BF16, dma=True)
        w1s = cx.sbuf("w1s", [128, 32, 256], BF16, dma=True)
        w2s = cx.sbuf("w2s", [128, 2, 256], BF16, dma=True)
        pef = cx.sbuf("pef", [128, 32], F32, dma=True)
        peb = cx.sbuf("peb", [128, 32], BF16)
        bias = cx.sbuf("cbias", [128, 2], F32)
        gel = cx.sbuf("gel", [128, 2, NCC * 128], BF16)
        ph = [cx.psum("ph", [128, 512], F32) for _ in range(2)]
        pk = [cx.psum("pk", [128, 512], F32) for _ in range(2)]
        pb_ = cx.psum("pbias", [128, 512], F32)
        t1 = cx.sbuf("ct1", [128, 512], F32)
        t2 = cx.sbuf("ct2", [128, 512], F32)
        ntiles = [(n0, min(512, NCMP - n0)) for n0 in range(0, NCMP, 512)]
        hi = 0
        for kv in range(2):
            nm = "kcT" if kv == 0 else "vcT"
            cx.dma("sp", lambda e, nm=nm: [e.dma_start(out=tokT[:, rs * NTC:(rs + 1) * NTC], in_=io["feat"](nm, rs))
                                          for rs in range(S // NTC)], None, tokT, n=S // NTC)
            cx.dma("pool", lambda e, kv=kv: [e.dma_start(
                out=w1s[:, :, :], in_=io["w1"][kv].rearrange("(l d) h -> d l h", d=128))], None, w1s)

            def ldw2(e, kv=kv):
                a = e.dma_start(out=w2s[:, :, 0:128], in_=io["w2"][kv].rearrange("(c p) d -> p c d", p=128))
                b = e.dma_start(out=w2s[:, :, 128:192], in_=io["w2"][kv][:, 64:128].rearrange("(c p) d -> p c d", p=128))
                c = e.dma_start(out=w2s[:, :, 192:256], in_=io["w2"][kv][:, 0:64].rearrange("(c p) d -> p c d", p=128))
                return [a, b, c]

            cx.dma("pool", ldw2, None, w2s, n=3)
            cx.dma("sp", lambda e, kv=kv: [e.dma_start(out=pef[:, :], in_=io["peT"][kv])], None, pef)
            cx.op("dve", lambda e: e.tensor_copy(out=peb[:, :], in_=pef[:, :]), [pef], [peb])
            for hc in range(2):
                for l in range(32):
                    cx.op("pe", lambda e, hc=hc, l=l: e.matmul(pb_[:, hc:hc + 1], lhsT=w1s[:, l, hc * 128:(hc + 1) * 128],
                                                               rhs=peb[:, l:l + 1], start=(l == 0), stop=(l == 31)),
                          [w1s, peb], [pb_], inc=(l == 31))
            cx.op("dve", lambda e: e.tensor_copy(out=bias[:, 0:2], in_=pb_[:, 0:2]), [pb_], [bias])
            for (n0, cnt) in ntiles:
                for hc in range(2):
                    p = ph[hi % 2]
                    hi += 1
                    for l in range(32):
                        a0 = 16 * n0 + l
                        cx.op("pe", lambda e, hc=hc, l=l, p=p, a0=a0, cnt=cnt: e.matmul(
                            p[:, 0:cnt], lhsT=w1s[:, l, hc * 128:(hc + 1) * 128],
                            rhs=tokT[:, a0:a0 + 16 * (cnt - 1) + 1:16], start=(l == 0), stop=(l == 31)),
                            [w1s, tokT], [p], inc=(l == 31))
                    cx.op("act", lambda e, hc=hc, p=p, n0=n0, cnt=cnt: e.activation(
                        out=gel[:, hc, n0:n0 + cnt], in_=p[:, 0:cnt], func=AF.Gelu_apprx_tanh, bias=bias[:, hc:hc + 1]),
                        [p], [gel], scalars=[bias])
                if kv == 0:
                    for half in range(2):
                        p = pk[half]
                        for hc in range(2):
                            cx.op("pe", lambda e, hc=hc, p=p, half=half, n0=n0, cnt=cnt: e.matmul(
                                p[:, 0:cnt], lhsT=w2s[:, hc, half * 128:(half + 1) * 128], rhs=gel[:, hc, n0:n0 + cnt],
                                start=(hc == 0), stop=(hc == 1)), [w2s, gel], [p], inc=(hc == 1))
                    cx.op("dve", lambda e, n0=n0, cnt=cnt: e.tensor_tensor(
                        out=t1[:, 0:cnt], in0=pk[0][:, 0:cnt], in1=Cc[:, n0:n0 + cnt], op=ALU.mult), [pk[0], Cc], [t1])
                    cx.op("dve", lambda e, n0=n0, cnt=cnt: e.tensor_tensor(
                        out=t2[:, 0:cnt], in0=pk[1][:, 0:cnt], in1=Sc[:, n0:n0 + cnt], op=ALU.mult), [pk[1], Sc], [t2])
                    cx.op("dve", lambda e, n0=n0, cnt=cnt: e.tensor_tensor(
                        out=kcmpT[:, n0:n0 + cnt], in0=t1[:, 0:cnt], in1=t2[:, 0:cnt], op=ALU.add), [t1, t2], [kcmpT])
                else:
                    for c0 in range(n0, n0 + cnt, 128):
                        m = min(128, n0 + cnt - c0)
                        p = pk[(c0 // 128) % 2]
                        for hc in range(2):
                            cx.op("pe", lambda e, hc=hc, p=p, c0=c0, m=m: e.matmul(
                                p[0:m, 0:128], lhsT=gel[:, hc, c0:c0 + m], rhs=w2s[:, hc, 0:128],
                                start=(hc == 0), stop=(hc == 1)), [w2s, gel], [p], inc=(hc == 1))
                        cx.op("act", lambda e, p=p, c0=c0, m=m: e.copy(out=vcmp[0:m, c0 // 128, :], in_=p[0:m, 0:128]),
                              [p], [vcmp])
    W = AttnWS(cx)
    ksT = cx.sbuf("ksT", [128, S], BF16, dma=True)
    vs = cx.sbuf("vs", [128, NKC * 128], BF16, dma=True)
    cx.dma("sp", lambda e: [e.dma_start(out=ksT[:, rs * NTC:(rs + 1) * NTC], in_=io["feat"]("ksT", rs))
                            for rs in range(S // NTC)], None, ksT, n=S // NTC)
    CPR = NTC // 128
    cx.dma("sp", lambda e: [e.dma_start(out=vs[:, rs * NTC:(rs + 1) * NTC], in_=io["vrank"]("vs", rs))
                            for rs in range(S // NTC)], None, vs, n=S // NTC)
    qts = [cx.sbuf("qt", [128, 4 * 512], BF16, dma=True) for _ in range(2)]
    gts = [cx.sbuf("gt", [12, 512], BF16, dma=True) for _ in range(2)]
    kws = [cx.sbuf("kw", [128, 1024], BF16, dma=True) for _ in range(2)]
    vws = [cx.sbuf("vwb", [128, 8 * 128], BF16, dma=True) for _ in range(2)]
    abs_ = [cx.sbuf("ab", [128, 512], F32, dma=True) for _ in range(2)]
    pcn = [[cx.sbuf("pcn", [128, 512], BF16) for _c in range(NCC)] for _r in range(4)]
    nselT = cx.sbuf("nselT", [128, 2, 512], BF16)
    acc = cx.sbuf("acc", [128, 4, 512], F32)
    accb = [cx.sbuf("accb", [128, 4 * 512], BF16, dma=True) for _ in range(2)]
    otmp = cx.sbuf("otmp", [128, 512], F32)
    gps = [cx.psum("gps", [128, 512], F32) for _ in range(1)]
    ips = cx.psum("ips", [128, 512], F32)
    seltp_ap = ips[:, 256:512].bitcast(BF16)
    impa = cx.sbuf("impa", [128, 256], F32)
    wk = cx.sbuf("wk", [128, 256], F32)
    mx = cx.sbuf("mx", [128, 16], F32)
    selb = cx.sbuf("selb", [128, 256], BF16)
    tps = cx.psum("ttp", [128, 256], BF16) if False else None
    abi = 0

    def gate_bcast(gt, i):
        g = gps[0]
        cx.op("pe", lambda e: e.matmul(g[:, :], lhsT=cbf[0:12, CB_SELG + i * 128:CB_SELG + (i + 1) * 128], rhs=gt[0:12, :],
                                       start=True, stop=True), [cbf, gt], [g])
        return g

    def finish_branch(ol, gt, r, br):
        g = gate_bcast(gt, r * 3 + br)
        rl = attn_rl(cx, W, ol, gate=g)
        O = W.O[ol]
        if br == 0:
            cx.op("dve", lambda e: e.tensor_tensor(out=acc[:, r, :], in0=O[:, :], in1=rl[:, :], op=ALU.mult), [O, rl], [acc])
        else:
            cx.op("dve", lambda e: e.tensor_tensor(out=otmp[:, :], in0=O[:, :], in1=rl[:, :], op=ALU.mult), [O, rl], [otmp])
            cx.op("dve", lambda e: e.tensor_tensor(out=acc[:, r, :], in0=acc[:, r, :], in1=otmp[:, :], op=ALU.add),
                  [otmp, acc], [acc])

    oli = 0
    for qt in range(NQT):
        q0 = qt * 512
        qb, gt, kw, vwb = qts[qt % 2], gts[qt % 2], kws[qt % 2], vws[qt % 2]
        cx.dma("sp", lambda e, qb=qb, q0=q0: [e.dma_start(out=qb[:, rh * 512:(rh + 1) * 512], in_=io["q"](rh, q0)) for rh in range(4)],
               None, qb, n=4)
        cx.dma("sp", lambda e, gt=gt, q0=q0: [e.dma_start(out=gt[:, :], in_=io["g"](q0))], None, gt)
        halves = [(q0 - 512, 0), (q0, 1)] if q0 > 0 else [(q0, 1)]
        cx.dma("sp", lambda e, kw=kw, halves=halves: [e.dma_start(
            out=kw[:, hh * 512:(hh + 1) * 512], in_=io["kwh"](qh)) for (qh, hh) in halves], None, kw, n=len(halves))
        cx.dma("sp", lambda e, vwb=vwb, halves=halves: [e.dma_start(
            out=vwb[:, hh * 512:(hh + 1) * 512], in_=io["vwh"](qh)) for (qh, hh) in halves], None, vwb, n=len(halves))
        vis = []
        for c in range(NCC):
            d = qt - 4 * c
            if d < 0:
                continue
            vis.append((c, d if d <= 4 else None))
        def cmp_fin(ol, r):
            rl0 = attn_rl(cx, W, ol)
            for (c, d) in vis:
                pb = pcn[r][c]
                cx.op("dve", lambda e, pb=pb, rl0=rl0: e.tensor_tensor(out=pb[:, :], in0=pb[:, :], in1=rl0[:, :], op=ALU.mult),
                      [pb, rl0], [pb])
            g = gate_bcast(gt, r * 3 + 0)
            O = W.O[ol]
            cx.op("dve", lambda e, rl0=rl0, g=g: e.tensor_tensor(out=rl0[:, :], in0=rl0[:, :], in1=g[:, :], op=ALU.mult),
                  [rl0, g], [rl0])
            cx.op("dve", lambda e, r=r, O=O, rl0=rl0: e.tensor_tensor(out=acc[:, r, :], in0=O[:, :], in1=rl0[:, :], op=ALU.mult),
                  [O, rl0], [acc])

        pipe_c = AttnPipe(cx, W)
        for r in range(4):
            ol = oli % 2
            oli += 1
            for ci, (c, d) in enumerate(vis):
                tm = None if d is None else (cbf, cbf[:, CB_CMASK + d * 512:CB_CMASK + (d + 1) * 512])
                aft = (lambda ol=ol, r=r: cmp_fin(ol, r)) if ci == len(vis) - 1 else None
                pipe_c.push(ol, [(qb, qb[:, r * 512:(r + 1) * 512], kcmpT, kcmpT[:, c * 128:(c + 1) * 128])],
                            (vcmp, vcmp[:, c, :]), 0, 512, ci == 0, ci == len(vis) - 1, scale, after=aft,
                            Pdst=(pcn[r][c], pcn[r][c][:, :]), tilemask=tm)
        pipe_c.flush()
        for qbk in range(4):
            ab = abs_[abi % 2]
            abi += 1
            gb = qt * 4 + qbk
            cx.dma("sp", lambda e, ab=ab, gb=gb: [e.dma_start(out=ab[:, :], in_=io["ab"][gb])], None, ab)
            n = 4 * len(vis)
            i = 0
            for r in range(4):
                for (c, d) in vis:
                    i += 1
                    pb = pcn[r][c]
                    cx.op("pe", lambda e, pb=pb, c=c, i=i, qbk=qbk: e.matmul(
                        ips[:, 0:256], lhsT=pb[:, qbk * 128:(qbk + 1) * 128],
                        rhs=cbf[:, CB_OV + c * 256:CB_OV + (c + 1) * 256], start=(i == 1), stop=(i == n)),
                        [pb, cbf], [ips], inc=(i == n))
            cx.op("dve", lambda e, ab=ab: e.tensor_tensor(out=impa[:, :], in0=ips[:, 0:256], in1=ab[:, 0:256], op=ALU.mult),
                  [ips, ab], [impa])
            cx.op("dve", lambda e, ab=ab: e.tensor_tensor(out=impa[:, :], in0=impa[:, :], in1=ab[:, 256:512], op=ALU.add),
                  [impa, ab], [impa])
            cx.op("dve", lambda e: e.max(out=mx[:, 0:8], in_=impa[:, :]), [impa], [mx])
            cx.op("dve", lambda e: e.match_replace(out=wk[:, :], in_to_replace=mx[:, 0:8], in_values=impa[:, :],
                                                   imm_value=-3e38), [impa], [wk], scalars=[mx])
            cx.op("dve", lambda e: e.max(out=mx[:, 8:16], in_=wk[:, :]), [wk], [mx])
            cx.op("dve", lambda e: e.tensor_scalar(out=wk[:, :], in0=impa[:, :], scalar1=mx[:, 15:16], scalar2=None,
                                                   op0=ALU.is_ge), [impa], [wk], scalars=[mx])
            cx.op("dve", lambda e, ab=ab: e.tensor_tensor(out=wk[:, :], in0=wk[:, :], in1=ab[:, 0:256], op=ALU.mult),
                  [wk, ab], [wk])
            cx.op("dve", lambda e: e.tensor_scalar(out=selb[:, :], in0=wk[:, :], scalar1=-1.0, scalar2=1.0,
                                                   op0=ALU.mult, op1=ALU.add), [wk], [selb])
            for jc in range(2):
                cx.op("pe", lambda e, jc=jc: e.transpose(out=seltp_ap[:, jc * 128:(jc + 1) * 128],
                                                         in_=selb[:, jc * 128:(jc + 1) * 128],
                                                         identity=cm.ident[:, :]), [selb, cm.ident], [ips], inc=(jc == 1))
            cx.op("act", lambda e, qbk=qbk: e.copy(
                out=nselT[:, :, qbk * 128:(qbk + 1) * 128], in_=seltp_ap[:, 0:256].rearrange("p (a b) -> p a b", a=2)),
                [ips], [nselT])
        pipe = AttnPipe(cx, W)
        W.use_pool = True
        for r in range(4):
            ol = oli % 2
            oli += 1
            nk = 4 * (qt + 1)
            for kc in range(nk):
                m = kc - 4 * qt
                clo = 128 * max(m, 0)
                bm = None if m < 0 else (cbf, cbf[:, CB_TRI:CB_TRI + 128], clo)
                am = (cbf, cbf[:, CB_ENEG + (kc % 64) * 128:CB_ENEG + (kc % 64 + 1) * 128], nselT, nselT[:, kc // 64, :])
                aft = (lambda ol=ol, r=r: finish_branch(ol, gt, r, 1)) if kc == nk - 1 else None
                pipe.push(ol, [(qb, qb[:, r * 512:(r + 1) * 512], ksT, ksT[:, kc * 128:(kc + 1) * 128])],
                          (vs, vs[:, kc * 128:(kc + 1) * 128]), clo, 512, kc == 0, kc == nk - 1, scale, after=aft,
                          blkmask=bm, addmask=am)
        for r in range(4):
            ol = oli % 2
            oli += 1
            order = [-1, -4, -3, -2, 0, 1, 2, 3] if qt > 0 else [0, 1, 2, 3]
            for oi_, m in enumerate(order):
                if m < 0:
                    clo, chi = 0, 128 * (m + 5)
                    bm = (cbf, cbf[:, CB_TRI2:CB_TRI2 + 128], 128 * (m + 4))
                else:
                    clo, chi = 128 * m, 512
                    bm = (cbf, cbf[:, CB_TRI:CB_TRI + 128], clo)
                aft = (lambda ol=ol, r=r: finish_branch(ol, gt, r, 2)) if oi_ == len(order) - 1 else None
                pipe.push(ol, [(qb, qb[:, r * 512:(r + 1) * 512], kw, kw[:, (m + 4) * 128:(m + 5) * 128])],
                          (vwb, vwb[:, (m + 4) * 128:(m + 5) * 128]), clo, chi, oi_ == 0, oi_ == len(order) - 1, scale,
                          after=aft, blkmask=bm)
        pipe.flush()
        W.use_pool = False
        ab_ = accb[qt % 2]
        cx.op("act", lambda e, ab_=ab_: e.copy(out=ab_[:, :], in_=acc[:, :, :].rearrange("p a b -> p (a b)")), [acc], [ab_])
        cx.dma("sp", lambda e, ab_=ab_, q0=q0: [e.dma_start(out=io["og"](rh, q0), in_=ab_[:, rh * 512:(rh + 1) * 512]) for rh in range(4)],
               ab_, None, n=4)
        if "after_store" in io and (qt + 1) % (NTC // 512) == 0:
            io["after_store"](qt // (NTC // 512), list(range(4)), accb)


def cx_tp(cx):
    if not hasattr(cx, "_tp") or cx._tp_scope is not cx.stack:
        cx._tp = cx.psum("seltp", [128, 512], BF16)
        cx._tp_scope = cx.stack
    return cx._tp


def mla_proj_phase(cx, cm, x_in, w_rep_ap, w_in, qn_rep_ap, kvn_rep_ap, w_uq, w_ukv, pos_ap, invsign_ap, outs, NT, D, T=1024):
    KC = D // 128
    invsign = load_invsign(cx, invsign_ap)
    C, S = rope_tables(cx, pos_ap, invsign, 64, NT)
    ws = ProjWS(cx, D, T, KC * 512)
    cx.dma("sp", lambda e: [e.dma_start(out=ws.w_rep[:, :], in_=w_rep_ap[:, :])], None, ws.w_rep)
    nrm = [cx.sbuf("mnrm", [128, 512], F32, dma=True) for _ in range(2)]
    cx.dma("sp", lambda e: [e.dma_start(out=nrm[0][:, :], in_=qn_rep_ap[:, :])], None, nrm[0])
    cx.dma("sp", lambda e: [e.dma_start(out=nrm[1][:, :], in_=kvn_rep_ap[:, :])], None, nrm[1])
    cT = [cx.sbuf("mcT", [128, 4, T], BF16) for _ in range(2)]
    cbf_ = cx.sbuf("mcbf", [128, 512], BF16)
    sc2 = dict(junk=cbf_, ss=cx.sbuf("mss", [128, 1], F32), rstd=cx.sbuf("mrstd", [128, 1], F32), hbf=cbf_,
               tp=ws.scratch["tp"], tpi=ws.scratch["tpi"])
    for st in range(NT // T):
        t0 = st * T
        for b in range(T // 128):
            norm_block_to_hT(cx, cm, ws, x_in, t0 + b * 128, b * 128)
        for which in range(2):
            slab = ws.slab()
            sv = load_wcols(cx, slab, KC, 512, w_in, [(0, which * 512, 512)])
            for b in range(T // 128):
                i = ws.pi % 2
                ws.pi += 1
                p = ws.pa[i]
                tm_mm(cx, p, 512, sv, 0, slab, ws.hT, ws.hT, b * 128, KC)
                junk, ss, rstd = sc2["junk"], sc2["ss"], sc2["rstd"]
                cx.op("act", lambda e, p=p: e.activation(out=junk[:, :], in_=p[:, :], func=AF.Square, accum_out=ss[:, 0:1]),
                      [p], [junk, ss])
                cx.op("act", lambda e: e.activation(out=rstd[:, 0:1], in_=ss[:, 0:1], func=AF.Sqrt, scale=1.0 / 512,
                                                    bias=eps_ap(cx)), [ss], [rstd], scalars=[cx.eps_tile])
                cx.op("dve", lambda e: e.reciprocal(out=rstd[:, 0:1], in_=rstd[:, 0:1]), [rstd], [rstd])
                cx.op("dve", lambda e, p=p, which=which: e.scalar_tensor_tensor(
                    out=cbf_[:, :], in0=p[:, :], scalar=rstd[:, 0:1], in1=nrm[which][:, :], op0=ALU.mult, op1=ALU.mult),
                    [p, nrm[which]], [cbf_], scalars=[rstd])
                transpose_to_fm(cx, cm, cbf_, cT[which], b * 128, 4, sc2)
        slab = ws.slab()
        sv = load_wcols(cx, slab, KC, 128, w_in, [(0, 1024, 64), (64, 1024 + 32, 32), (96, 1024, 32)])
        for tt in range(T // 512):
            i = ws.pi % 2
            fm_mm(cx, ws.pa[i], 64, sv, 0, slab, ws.hT, ws.hT, (tt * 512, tt * 512 + 512), KC)
            fm_mm(cx, ws.pb[i], 64, sv, 64, slab, ws.hT, ws.hT, (tt * 512, tt * 512 + 512), KC)
            rope_out(cx, ws, ws.pa[i], ws.pb[i], 64, C, S, t0 + tt * 512, outs["krT"][0:64, t0 + tt * 512:t0 + tt * 512 + 512])
            ws.pi += 1
        for h in range(16):
            c0 = h * 192
            slab = ws.slab()
            sv = load_wcols(cx, slab, 4, 256, w_uq, [(0, c0, 128), (128, c0 + 128, 64), (192, c0 + 160, 32), (224, c0 + 128, 32)])
            for tt in range(T // 512):
                cols = (tt * 512, tt * 512 + 512)
                i = ws.pi % 2
                ws.pi += 1
                fm_mm(cx, ws.pa[i], 128, sv, 0, slab, cT[0], cT[0], cols, 4)
                plain_out(cx, ws, ws.pa[i], 128, 512, outs["qnT"][h * 128:(h + 1) * 128, t0 + cols[0]:t0 + cols[1]])
                i = ws.pi % 2
                fm_mm(cx, ws.pa[i], 64, sv, 128, slab, cT[0], cT[0], cols, 4)
                fm_mm(cx, ws.pb[i], 64, sv, 192, slab, cT[0], cT[0], cols, 4)
                rope_out(cx, ws, ws.pa[i], ws.pb[i], 64, C, S, t0 + cols[0], outs["qrT"][h * 64:(h + 1) * 64, t0 + cols[0]:t0 + cols[1]])
                ws.pi += 1
        for h in range(16):
            slab = ws.slab()
            sv = load_wcols(cx, slab, 4, 128, w_ukv, [(0, h * 256, 128)])
            for tt in range(T // 512):
                cols = (tt * 512, tt * 512 + 512)
                i = ws.pi % 2
                ws.pi += 1
                fm_mm(cx, ws.pa[i], 128, sv, 0, slab, cT[1], cT[1], cols, 4)
                plain_out(cx, ws, ws.pa[i], 128, 512, outs["knT"][h * 128:(h + 1) * 128, t0 + cols[0]:t0 + cols[1]])
        for hq in range(4):
            slab = ws.slab()
            sv = load_wcols(cx, slab, 4, 512, w_ukv, [(j * 128, (hq * 4 + j) * 256 + 128, 128) for j in range(4)])
            for b in range(T // 128):
                i = ws.pi % 2
                ws.pi += 1
                tm_mm(cx, ws.pa[i], 512, sv, 0, slab, cT[1], cT[1], b * 128, 4)
                plain_out(cx, ws, ws.pa[i], 128, 512, outs["v"](t0 + b * 128, hq), src3=True)


def mla_attn_phase(cx, cm, io, S, NH=4, NTC=4096):
    NQT = S // 512
    NKC = S // 128
    scale = 192.0 ** -0.5
    W = AttnWS(cx)
    tri = cx.sbuf("mtri", [128, 128], BF16, dma=True)
    cx.dma("sp", lambda e: [e.dma_start(out=tri[:, :], in_=io["tri"][:, :])], None, tri)
    krT = cx.sbuf("krT", [64, S], BF16, dma=True)
    NR = S // NTC
    CPR = NTC // 128
    cx.dma("sp", lambda e: [e.dma_start(out=krT[:, rs * NTC:(rs + 1) * NTC], in_=io["kr"](rs)) for rs in range(NR)],
           None, krT, n=NR)
    kns = [cx.sbuf("knT", [128, S], BF16, dma=True) for _ in range(2)]
    vss = [cx.sbuf("mv", [128, NKC * 128], BF16, dma=True) for _ in range(2)]
    qns = [cx.sbuf("mqn", [128, 512], BF16, dma=True) for _ in range(2)]
    qrs = [cx.sbuf("mqr", [64, 512], BF16, dma=True) for _ in range(2)]
    obs = [cx.sbuf("mob", [128, 512], BF16, dma=True) for _ in range(2)]
    it = 0
    pipe = AttnPipe(cx, W)
    for h in range(NH):
        kn, vv = kns[h % 2], vss[h % 2]
        cx.dma("sp", lambda e, kn=kn, h=h: [e.dma_start(out=kn[:, rs * NTC:(rs + 1) * NTC], in_=io["kn"](h, rs))
                                            for rs in range(NR)], None, kn, n=NR)
        cx.dma("sp", lambda e, vv=vv, h=h: [e.dma_start(out=vv[:, rs * NTC:(rs + 1) * NTC], in_=io["v"](h, rs))
                                            for rs in range(NR)], None, vv, n=NR)
        for qt in range(NQT):
            q0 = qt * 512
            qn, qr, ob = qns[it % 2], qrs[it % 2], obs[it % 2]
            ol = it % 2
            it += 1
            cx.dma("sp", lambda e, qn=qn, h=h, q0=q0: [e.dma_start(out=qn[:, :], in_=io["qn"](h, q0))], None, qn)
            cx.dma("sp", lambda e, qr=qr, h=h, q0=q0: [e.dma_start(out=qr[:, :], in_=io["qr"](h, q0))], None, qr)
            nk = 4 * (qt + 1)
            for kc in range(nk):
                m = kc - 4 * qt
                clo = 128 * max(m, 0)
                bm = None if m < 0 else (tri, tri[:, :], clo)
                def fin(ol=ol, ob=ob, h=h, q0=q0):
                    rl = attn_rl(cx, W, ol)
                    O = W.O[ol]
                    cx.op("dve", lambda e: e.tensor_tensor(out=ob[:, :], in0=O[:, :], in1=rl[:, :], op=ALU.mult), [O, rl], [ob])
                    cx.dma("sp", lambda e: [e.dma_start(out=io["o"](h, q0), in_=ob[:, :])], ob, None)
                    if "after_store" in io and (q0 // 512 + 1) % (NTC // 512) == 0:
                        io["after_store"](q0 // NTC, [h], obs)

                pipe.push(ol, [(qn, qn[:, :], kn, kn[:, kc * 128:(kc + 1) * 128]),
                               (qr, qr[0:64, :], krT, krT[0:64, kc * 128:(kc + 1) * 128])],
                          (vv, vv[:, kc * 128:(kc + 1) * 128]), clo, 512, kc == 0, kc == nk - 1, scale,
                          after=(fin if kc == nk - 1 else None), blkmask=bm)
    pipe.flush()


NCORE = 8
DM, DFF, SEQ_, NTC = 2048, 5632, 16384, 4096
GROUPS = [[0, 1, 2, 3], [4, 5, 6, 7]]
UA, UB, UC = 41, 16, 57


def rep128(v):
    return np.ascontiguousarray(np.tile(np.asarray(v, np.float32)[None, :], (128, 1)))


def exchange(cx, src2d, parts, copies):
    with cx.scope():
        bds = []
        bs = cx.dram_buf(src2d, "xs")
        for (u0, n, dst2d) in parts:
            bd = cx.dram_buf(dst2d, "xd")
            for j in range(n):
                u = u0 + j
                cx.cc(lambda e, u=u, j=j, dst2d=dst2d: e.collective_compute(
                    "AllGather", ALU.bypass, replica_groups=GROUPS, ins=[src2d[u * 128:(u + 1) * 128, :]],
                    outs=[dst2d[j * 512:(j + 1) * 512, :]]), bs, bd)
            bds.append(bd)
        cx.wait_all("sp", bds)
        mine = cx.dram_buf(None, "mine")
        for (out_ap, in_fn) in copies:
            cx.dma("sp", lambda e, out_ap=out_ap, in_fn=in_fn: [e.dma_start(out=out_ap, in_=in_fn())], None, mine)


def out_exchange(cx, io, src2d, dst2d, mine_ap, rv, run_phase):
    with cx.scope():
        bs = cx.dram_buf(src2d, "xs")
        bd = cx.dram_buf(dst2d, "xd")

        def after_store(tb, hls, store_bufs):
            cx.wait_all("pool", store_bufs)
            for hl in hls:
                u = tb * 4 + hl
                cx.cc(lambda e, u=u: e.collective_compute(
                    "AllGather", ALU.bypass, replica_groups=GROUPS, ins=[src2d[u * 128:(u + 1) * 128, :]],
                    outs=[dst2d[u * 512:(u + 1) * 512, :]]), bs, bd)

        io["after_store"] = after_store
        with cx.scope():
            run_phase()
        cx.wait_all("sp", [bd])
        mine = cx.dram_buf(None, "mine")
        cx.dma("sp", lambda e: [e.dma_start(out=mine_ap[:, :], in_=dst2d[bass.ds(rv() * 2048, 2048), :])], None, mine)


def build_program():
    nc = bass.Bass("TRN2", target_bir_lowering=False)

    def I(name, shape, dt=F32):
        return nc.dram_tensor(name, list(shape), dt, kind="ExternalInput").ap()

    def T(name, shape, dt=F32):
        return nc.dram_tensor(name, list(shape), dt).ap()

    a_x = I("x", [NTC, DM]); a_rank = I("rank", [1, 2], I32); a_pos = I("pos", [128, NTC], I32)
    a_posc = I("pos_cmp", [128, 1024], I32); a_inv128 = I("inv128", [128, 2]); a_inv64 = I("inv64", [128, 2])
    a_c = I("consts", [128, 128]); a_cbf = I("cbf", [128, CB_W], BF16); a_ab = I("ab", [SEQ_ // 128, 128, 512])
    fw = [(I("fw_rep%d" % k, [128, DM]), I("f_w_in%d" % k, [DM, 2 * DFF]), I("f_w_out%d" % k, [DFF, DM])) for k in range(4)]
    a_mw0 = I("mw_rep0", [128, DM]); a_mw1 = I("mw_rep1", [128, DM])
    a_nw = I("nsa_w_in", [DM, 5168]); a_w1 = I("w1", [2, 4096, 256]); a_w2 = I("w2", [2, 256, 128]); a_peT = I("peT", [2, 128, 32])
    a_now = I("nsa_w_out", [2048, DM])
    a_mwi = I("mla_w_in", [DM, 1088]); a_qn = I("qn_rep", [128, 512]); a_kvn = I("kvn_rep", [128, 512])
    a_uq = I("w_uq", [512, 3072]); a_ukv = I("w_ukv", [512, 4096]); a_mow = I("mla_w_out", [2048, DM])
    a_fn = I("fn_rep", [128, DM])
    a_out = nc.dram_tensor("out", [NTC, DM], F32, kind="ExternalOutput").ap()
    xs_ = [T("xr%d" % i, [NTC, DM]) for i in range(6)]
    A_src = T("A_src", [UA * 128, NTC], BF16)
    Aq = T("Aq", [16 * 512, NTC], BF16); Af = T("Af", [16 * 512, NTC], BF16); Av = T("Av", [8 * 512, NTC], BF16); Ag = T("Ag", [512, NTC], BF16)
    B_src = T("B_src", [UB * 128, NTC], BF16); B_dst = T("B_dst", [UB * 512, NTC], BF16)
    myQ = T("myQ", [2048, NTC], BF16); myF = T("myF", [2048, NTC], BF16); myV = T("myV", [1024, NTC], BF16); myG = T("myG", [48, NTC], BF16)
    myO = T("myO", [2048, NTC], BF16); myO2 = T("myO2", [2048, NTC], BF16)
    myQn = T("myQn", [2048, NTC], BF16); myQr = T("myQr", [1024, NTC], BF16); myKn = T("myKn", [2048, NTC], BF16); myV2 = T("myV2", [2048, NTC], BF16)
    C_src = T("C_src", [UC * 128, NTC], BF16)
    Cqn = T("Cqn", [16 * 512, NTC], BF16); Cqr = T("Cqr", [8 * 512, NTC], BF16); Ckn = T("Ckn", [16 * 512, NTC], BF16)
    Ckr = T("Ckr", [512, NTC], BF16); Cv = T("Cv", [16 * 512, NTC], BF16)
    D_src = T("D_src", [UB * 128, NTC], BF16); D_dst = T("D_dst", [UB * 512, NTC], BF16)

    with ExitStack() as stack:
        cx = Ctx(nc, stack)
        cm = Common(cx, a_c)
        rv = cx.load_rank("sp", a_rank)

        def split(q0):
            return q0 // NTC, q0 % NTC

        with cx.scope():
            ffn_phase(cx, cm, a_x, xs_[0], fw[0][0], fw[0][1], fw[0][2], NTC, DM, DFF, 1e-6)
        VS = A_src[32 * 128:36 * 128, :].rearrange("(g p) c -> g p c", g=4)
        VW = A_src[36 * 128:40 * 128, :].rearrange("(g p) c -> g p c", g=4)
        outs = dict(qT=A_src[0:2048, :], kcT=A_src[2048:2560, :], vcT=A_src[2560:3072, :], ksT=A_src[3072:3584, :],
                    kwT=A_src[3584:4096, :], gT=A_src[40 * 128:40 * 128 + 48, :],
                    vs=lambda t0: VS[:, :, t0:t0 + 128].rearrange("g p d -> p g d"),
                    vw=lambda t0: VW[:, :, t0:t0 + 128].rearrange("g p d -> p g d"))
        with cx.scope():
            nsa_proj_phase(cx, cm, xs_[0], a_mw0, a_nw, a_pos, a_inv128, outs, NTC, DM)
        cpA = [(myQ[:, :], lambda: Aq[bass.ds(rv() * 2048, 2048), :])]
        for i_ in range(4):
            cpA.append((myF[i_ * 512:(i_ + 1) * 512, :], lambda i_=i_: Af[bass.ds(rv() * 512 + i_ * 2048, 512), :]))
        for i_ in range(2):
            cpA.append((myV[i_ * 512:(i_ + 1) * 512, :], lambda i_=i_: Av[bass.ds(rv() * 512 + i_ * 2048, 512), :]))
        for rs_ in range(4):
            cpA.append((myG[rs_ * 12:(rs_ + 1) * 12, :], lambda rs_=rs_: Ag[bass.ds(rv() * 12 + rs_ * 128, 12), :]))
        exchange(cx, A_src, [(0, 16, Aq), (16, 16, Af), (32, 8, Av), (40, 1, Ag)], cpA)
        FU = {"kcT": 0, "vcT": 1, "ksT": 2, "kwT": 3}
        VU = {"vs": 0, "vw": 1}

        def a_q(rh, q0):
            rs, c0 = split(q0)
            return myQ[rh * 512 + rs * 128:rh * 512 + rs * 128 + 128, c0:c0 + 512]

        def a_g(q0):
            rs, c0 = split(q0)
            return myG[rs * 12:(rs + 1) * 12, c0:c0 + 512]

        def a_feat(nm, rs):
            return myF[FU[nm] * 512 + rs * 128:FU[nm] * 512 + rs * 128 + 128, :]

        def a_kwh(qh):
            rs, c0 = split(qh)
            return myF[3 * 512 + rs * 128:3 * 512 + rs * 128 + 128, c0:c0 + 512]

        def a_vrank(nm, rs):
            return myV[VU[nm] * 512 + rs * 128:VU[nm] * 512 + rs * 128 + 128, :]

        def a_vwh(qh):
            rs, tl0 = split(qh)
            return myV[512 + rs * 128:512 + rs * 128 + 128, tl0:tl0 + 512]

        def b_og(src):
            def f(rh, q0):
                tb, c0 = split(q0)
                return src[(tb * 4 + rh) * 128:(tb * 4 + rh + 1) * 128, c0:c0 + 512]
            return f

        def b_oT(dst):
            def f(c, t0, Tn):
                g, rh = c // 4, c % 4
                return dst[rh * 512 + g * 128:rh * 512 + g * 128 + 128, t0:t0 + Tn]
            return f

        io = dict(q=a_q, g=a_g, feat=a_feat, kwh=a_kwh, vrank=a_vrank, vwh=a_vwh, og=b_og(B_src), cbf=a_cbf, ab=a_ab,
                  w1=a_w1, w2=a_w2, peT=a_peT, pos_cmp=a_posc, invsign=a_inv128)
        out_exchange(cx, io, B_src, B_dst, myO, rv, lambda: nsa_attn_phase(cx, cm, io, SEQ_, NTC))
        with cx.scope():
            outproj_phase(cx, cm, b_oT(myO), a_now, xs_[0], xs_[1], NTC, 2048, DM, 1.0)
        with cx.scope():
            ffn_phase(cx, cm, xs_[1], xs_[2], fw[1][0], fw[1][1], fw[1][2], NTC, DM, DFF, 1e-6)
        with cx.scope():
            ffn_phase(cx, cm, xs_[2], xs_[3], fw[2][0], fw[2][1], fw[2][2], NTC, DM, DFF, 1e-6)
        VV = C_src[41 * 128:57 * 128, :].rearrange("(h p) c -> h p c", h=16)
        mo = dict(qnT=C_src[0:2048, :], qrT=C_src[2048:3072, :], knT=C_src[3072:5120, :], krT=C_src[5120:5184, :],
                  v=lambda t0, hq: VV[hq * 4:(hq + 1) * 4, :, t0:t0 + 128].rearrange("h p d -> p h d"))
        with cx.scope():
            mla_proj_phase(cx, cm, xs_[3], a_mw1, a_mwi, a_qn, a_kvn, a_uq, a_ukv, a_pos, a_inv64, mo, NTC, DM)
        exchange(cx, C_src, [(0, 16, Cqn), (16, 8, Cqr), (24, 16, Ckn), (40, 1, Ckr), (41, 16, Cv)],
                 [(myQn[:, :], lambda: Cqn[bass.ds(rv() * 2048, 2048), :]), (myQr[:, :], lambda: Cqr[bass.ds(rv() * 1024, 1024), :]),
                  (myKn[:, :], lambda: Ckn[bass.ds(rv() * 2048, 2048), :]), (myV2[:, :], lambda: Cv[bass.ds(rv() * 2048, 2048), :])])

        def c_qn(hl, q0):
            rs, c0 = split(q0)
            return myQn[hl * 512 + rs * 128:hl * 512 + rs * 128 + 128, c0:c0 + 512]

        def c_qr(hl, q0):
            rs, c0 = split(q0)
            r0_ = (hl // 2) * 512 + rs * 128 + (hl % 2) * 64
            return myQr[r0_:r0_ + 64, c0:c0 + 512]

        def c_kn(hl, rs):
            return myKn[hl * 512 + rs * 128:hl * 512 + rs * 128 + 128, :]

        def c_kr(rs):
            return Ckr[rs * 128:rs * 128 + 64, :]

        def c_v(hl, rs):
            return myV2[hl * 512 + rs * 128:hl * 512 + rs * 128 + 128, :]

        io2 = dict(qn=c_qn, qr=c_qr, kn=c_kn, kr=c_kr, v=c_v, o=b_og(D_src), tri=a_cbf[:, CB_TRI:CB_TRI + 128])
        out_exchange(cx, io2, D_src, D_dst, myO2, rv, lambda: mla_attn_phase(cx, cm, io2, SEQ_, 4, NTC))
        with cx.scope():
            outproj_phase(cx, cm, b_oT(myO2), a_mow, xs_[3], xs_[4], NTC, 2048, DM, 1.0)
        with cx.scope():
            ffn_phase(cx, cm, xs_[4], xs_[5], fw[3][0], fw[3][1], fw[3][2], NTC, DM, DFF, 1e-6)
        with cx.scope():
            final_norm_phase(cx, cm, xs_[5], a_out, a_fn, NTC, DM)
        cx.emit()
    return nc


def kernel(x, positions, ffn_norm_w, ffn_w_in, ffn_w_out, mix_norm_w, nsa_w_in, nsa_cmp_pe, nsa_cmp_w1, nsa_cmp_w2,
           nsa_w_out, mla_w_in, mla_q_norm_w, mla_kv_norm_w, mla_w_uq, mla_w_ukv, mla_w_out, final_norm_w):
    f = lambda a: np.ascontiguousarray(np.asarray(a))
    x = f(x); positions = f(positions).astype(np.int32)
    ffn_norm_w, ffn_w_in, ffn_w_out, mix_norm_w = f(ffn_norm_w), f(ffn_w_in), f(ffn_w_out), f(mix_norm_w)
    nc = build_program()
    shared = {"inv128": invsign_array(128), "inv64": invsign_array(64), "consts": consts_array(), "cbf": nsa_consts_bf16(),
              "ab": nsa_ab_table(SEQ_), "mw_rep0": rep128(mix_norm_w[0]), "mw_rep1": rep128(mix_norm_w[1]),
              "nsa_w_in": f(nsa_w_in)[0], "w1": f(nsa_cmp_w1)[0], "w2": f(nsa_cmp_w2)[0],
              "peT": np.ascontiguousarray(f(nsa_cmp_pe)[0].transpose(0, 2, 1)), "nsa_w_out": f(nsa_w_out)[0],
              "mla_w_in": f(mla_w_in)[0], "qn_rep": rep128(f(mla_q_norm_w)[0]), "kvn_rep": rep128(f(mla_kv_norm_w)[0]),
              "w_uq": f(mla_w_uq)[0], "w_ukv": f(mla_w_ukv)[0], "mla_w_out": f(mla_w_out)[0], "fn_rep": rep128(f(final_norm_w))}
    for k, (i, j) in enumerate([(0, 0), (0, 1), (1, 0), (1, 1)]):
        shared["fw_rep%d" % k] = rep128(ffn_norm_w[i, j])
        shared["f_w_in%d" % k] = ffn_w_in[i, j]
        shared["f_w_out%d" % k] = ffn_w_out[i, j]
    ncmp = (SEQ_ - 32) // 16 + 1
    maps = []
    for c in range(NCORE):
        b, r = c // 4, c % 4
        posc = np.zeros(1024, np.int32)
        posc[:ncmp] = positions[b, 16 * np.arange(ncmp) + 31]
        m = dict(shared)
        m["x"] = np.ascontiguousarray(x[b, r * NTC:(r + 1) * NTC])
        m["rank"] = np.array([[r, 0]], np.int32)
        m["pos"] = np.ascontiguousarray(np.tile(positions[b, r * NTC:(r + 1) * NTC][None, :], (128, 1)))
        m["pos_cmp"] = np.ascontiguousarray(np.tile(posc[None], (128, 1)))
        maps.append(m)
    res = run_bass_kernel_spmd(nc, maps, core_ids=list(range(NCORE)))
    out = np.empty((2, SEQ_, DM), np.float32)
    for c in range(NCORE):
        b, r = c // 4, c % 4
        out[b, r * NTC:(r + 1) * NTC] = np.asarray(res.results[c]["out"])
    return out
```

```python
import numpy as np
import ml_dtypes
from contextlib import ExitStack
import concourse.bass as bass
import concourse.mybir as mybir
from concourse.bass_utils import run_bass_kernel_spmd

F32 = mybir.dt.float32
BF16 = mybir.dt.bfloat16
I32 = mybir.dt.int32
AF = mybir.ActivationFunctionType
ALU = mybir.AluOpType
AX = mybir.AxisListType


class Buf:
    def __init__(self, t, name, dma_sem=None):
        self.t = t
        self.name = name
        self.w = None
        self.r = []
        self.dma_ent = dma_sem

    def __getitem__(self, idx):
        return self.t[idx]


CC_INC = 1


class Ctx:
    ENG = ("pe", "dve", "act", "pool", "sp")

    def __init__(self, nc, stack):
        self.nc = nc
        self.stack = stack
        self.root_stack = stack
        self.streams = {e: [] for e in self.ENG}
        self.sem = {}
        self.count = {e: 0 for e in self.ENG}
        self.seen = {e: {} for e in self.ENG}
        self.pending = {e: False for e in self.ENG}
        for e in self.ENG:
            self.sem[e] = stack.enter_context(nc.semaphore("sem_" + e))
        self.n_dma_sems = 0
        self.uid = 0
        self.free_dsems = []
        self.scope_dsems = []

    def _name(self, name):
        self.uid += 1
        return "%s_%d" % (name, self.uid)

    def new_dma_sem(self):
        if self.free_dsems:
            ent = self.free_dsems.pop()
        else:
            self.n_dma_sems += 1
            sem = self.root_stack.enter_context(self.nc.semaphore("dsem_%d" % self.n_dma_sems))
            ent = [sem, 0, "d%d" % self.n_dma_sems]
        self.scope_dsems.append(ent)
        return ent

    def barrier(self):
        for e in self.ENG:
            assert not self.pending[e], "engine %s has un-signalled trailing ops" % e
        for e in self.ENG:
            for o in self.ENG:
                if o != e and self.count[o] > 0:
                    self._wait(e, (o, self.sem[o], self.count[o]))
            for ent in self.scope_dsems:
                if ent[1] > 0:
                    self._wait(e, (ent[2], ent[0], ent[1]))

    def scope(self):
        cx = self

        class _Scope:
            def __enter__(s):
                s.saved_stack = cx.stack
                s.saved_dsems = cx.scope_dsems
                s.es = ExitStack()
                s.es.__enter__()
                cx.stack = s.es
                cx.scope_dsems = []
                return s

            def __exit__(s, *a):
                cx.barrier()
                cx.free_dsems.extend(cx.scope_dsems)
                cx.scope_dsems = s.saved_dsems
                cx.stack = s.saved_stack
                s.es.__exit__(None, None, None)
                return False

        return _Scope()

    def sbuf(self, name, shape, dt, dma=False):
        t = self.stack.enter_context(self.nc.sbuf_tensor(self._name(name), list(shape), dt))
        return Buf(t, name, self.new_dma_sem() if dma else None)

    def psum(self, name, shape, dt):
        t = self.stack.enter_context(self.nc.psum_tensor(self._name(name), list(shape), dt))
        return Buf(t, name)

    def dram_buf(self, ap, name):
        return Buf(ap, name, self.new_dma_sem())

    def _wait(self, eng, tok, force=False):
        if tok is None:
            return
        sem_key, sem, val = tok
        if sem_key == eng and not force:
            return
        if sem_key == eng and val > self.count[eng]:
            return
        if self.seen[eng].get(sem_key, 0) >= val:
            return
        self.seen[eng][sem_key] = val
        self.streams[eng].append(("wait", sem, val))

    def op(self, eng, fn, reads=(), writes=(), inc=True, scalars=()):
        for b in scalars:
            self._wait(eng, b.w, force=True)
        reads = list(reads) + list(scalars)
        for b in reads:
            self._wait(eng, b.w)
        for b in writes:
            self._wait(eng, b.w)
            for t in b.r:
                self._wait(eng, t)
        if inc:
            self.count[eng] += 1
            tok = (eng, self.sem[eng], self.count[eng])
            self.pending[eng] = False
        else:
            tok = (eng, self.sem[eng], self.count[eng] + 1)
            self.pending[eng] = True
        self.streams[eng].append(("op", fn, self.sem[eng] if inc else None))
        for b in writes:
            b.w = tok
            b.r = []
        for b in reads:
            if b in writes:
                continue
            b.r = [t for t in b.r if t[0] != eng] + [tok]

    def dma(self, q, fn, src, dst, n=1):
        if src is not None:
            self._wait(q, src.w)
        if dst is not None:
            self._wait(q, dst.w)
            for t in dst.r:
                self._wait(q, t)
        holder = dst if (dst is not None and dst.dma_ent is not None) else src
        assert holder is not None and holder.dma_ent is not None, "dma needs a Buf with dma_sem"
        ent = holder.dma_ent
        ent[1] += 16 * n
        tok = (ent[2], ent[0], ent[1])
        self.streams[q].append(("dma", fn, ent[0]))
        if dst is not None:
            dst.w = tok
            dst.r = []
        if src is not None:
            src.r = src.r + [tok]

    def cc(self, fn, src, dst):
        q = "pool"
        self._wait(q, src.w)
        self._wait(q, dst.w)
        for t in dst.r:
            self._wait(q, t)
        ent = dst.dma_ent
        ent[1] += CC_INC
        tok = (ent[2], ent[0], ent[1])
        self.streams[q].append(("cc", fn, ent[0]))
        dst.w = tok
        dst.r = []
        src.r = src.r + [tok]

    def load_rank(self, eng, rank_ap):
        holder = {}

        def fn(e):
            reg = e.alloc_register("rank_reg_%s" % eng)
            e.reg_load(reg, rank_ap[0:1, 0:1])
            holder["v"] = e.snap(reg, min_val=0, max_val=3)
            return None

        self.streams[eng].append(("raw", fn))
        return lambda: holder["v"]

    def wait_all(self, eng, bufs):
        for b in bufs:
            self._wait(eng, b.w)
            for t in b.r:
                self._wait(eng, t)

    def emit(self):
        nc = self.nc
        hmap = {"pe": "tensor", "dve": "vector", "act": "scalar", "pool": "gpsimd", "sp": "sync"}
        with nc.Block() as block:
            for ename in self.ENG:
                stream = self.streams[ename]

                def body(e, stream=stream):
                    for item in stream:
                        if item[0] == "wait":
                            e.wait_ge(item[1], item[2])
                        elif item[0] == "op":
                            ins = item[1](e)
                            if item[2] is not None:
                                ins.then_inc(item[2], 1)
                        elif item[0] == "raw":
                            item[1](e)
                        elif item[0] == "cc":
                            item[1](e).then_inc(item[2], CC_INC)
                        else:
                            lst = item[1](e)
                            for ins in lst:
                                ins.then_inc(item[2], 16)

                getattr(block, hmap[ename])(body)


class Common:
    def __init__(self, cx, consts_ap):
        self.cx = cx
        nc = cx.nc
        self.ident_f = cx.sbuf("ident_f", [128, 128], F32, dma=True)
        self.ident = cx.sbuf("ident", [128, 128], BF16)
        cx.dma("sp", lambda e: [e.dma_start(out=self.ident_f[:, :], in_=consts_ap[:, 0:128])], None, self.ident_f)
        cx.op("dve", lambda e: e.tensor_copy(out=self.ident[:, :], in_=self.ident_f[:, :]), [self.ident_f], [self.ident])
        self.eps = cx.sbuf("eps", [128, 1], F32)
        cx.op("dve", lambda e: e.memset(self.eps[:, :], 1e-6), [], [self.eps])
        cx.eps_tile = self.eps


def eps_ap(cx):
    return cx.eps_tile[:, 0:1]


def sincos_reduce(cx, t, shift, shape):
    P, N = shape
    ki = cx.sbuf("sc_ki", [P, N], I32)
    kf = cx.sbuf("sc_kf", [P, N], F32)
    TWO_PI = 2.0 * np.pi
    C1 = 6.28125
    C2 = TWO_PI - C1
    if shift != 0.0:
        cx.op("dve", lambda e: e.tensor_scalar(out=t[:, :], in0=t[:, :], scalar1=float(shift), scalar2=None, op0=ALU.add), [t], [t])
    cx.op("dve", lambda e: e.tensor_scalar(out=ki[:, :], in0=t[:, :], scalar1=float(1.0 / TWO_PI), scalar2=None, op0=ALU.mult), [t], [ki])
    cx.op("dve", lambda e: e.tensor_copy(out=kf[:, :], in_=ki[:, :]), [ki], [kf])
    cx.op("dve", lambda e: e.scalar_tensor_tensor(out=t[:, :], in0=kf[:, :], scalar=-C1, in1=t[:, :], op0=ALU.mult, op1=ALU.add), [kf, t], [t])
    cx.op("dve", lambda e: e.scalar_tensor_tensor(out=t[:, :], in0=kf[:, :], scalar=-C2, in1=t[:, :], op0=ALU.mult, op1=ALU.add), [kf, t], [t])
    cx.op("dve", lambda e: e.tensor_scalar(out=kf[:, :], in0=t[:, :], scalar1=float(-np.pi), scalar2=None, op0=ALU.is_lt), [t], [kf])
    cx.op("dve", lambda e: e.scalar_tensor_tensor(out=t[:, :], in0=kf[:, :], scalar=TWO_PI, in1=t[:, :], op0=ALU.mult, op1=ALU.add), [kf, t], [t])
    cx.op("dve", lambda e: e.tensor_scalar(out=kf[:, :], in0=t[:, :], scalar1=float(np.pi), scalar2=None, op0=ALU.is_gt), [t], [kf])
    cx.op("dve", lambda e: e.scalar_tensor_tensor(out=t[:, :], in0=kf[:, :], scalar=-TWO_PI, in1=t[:, :], op0=ALU.mult, op1=ALU.add), [kf, t], [t])
    cx.op("dve", lambda e: e.tensor_scalar(out=t[:, :], in0=t[:, :], scalar1=float(np.pi), scalar2=float(-np.pi), op0=ALU.min, op1=ALU.max), [t], [t])


def rmsnorm_to_hT(cx, cm, x_tile, w_rep, hT, col0, D, eps, scratch):
    junk, ss, rstd, hbf = scratch["junk"], scratch["ss"], scratch["rstd"], scratch["hbf"]
    KC = D // 128
    cx.op("act", lambda e: e.activation(out=junk[:, :], in_=x_tile[:, 0:D], func=AF.Square, accum_out=ss[:, 0:1]),
          [x_tile], [junk, ss])
    cx.op("act", lambda e: e.activation(out=rstd[:, 0:1], in_=ss[:, 0:1], func=AF.Sqrt, scale=1.0 / D, bias=eps_ap(cx)),
          [ss], [rstd], scalars=[cx.eps_tile])
    cx.op("dve", lambda e: e.reciprocal(out=rstd[:, 0:1], in_=rstd[:, 0:1]), [rstd], [rstd])
    cx.op("dve", lambda e: e.scalar_tensor_tensor(out=hbf[:, 0:D], in0=x_tile[:, 0:D], scalar=rstd[:, 0:1],
                                                   in1=w_rep[:, 0:D], op0=ALU.mult, op1=ALU.mult),
          [x_tile, w_rep], [hbf], scalars=[rstd])
    transpose_to_fm(cx, cm, hbf, hT, col0, KC, scratch)


def transpose_to_fm(cx, cm, src, dstT, col0, KC, scratch):
    tps = scratch["tp"]
    for g0 in range(0, KC, 4):
        g = min(4, KC - g0)
        tp = tps[scratch["tpi"][0] % len(tps)]
        scratch["tpi"][0] += 1
        for i in range(g):
            kc = g0 + i
            cx.op("pe", lambda e, kc=kc, i=i, tp=tp: e.transpose(out=tp[:, i * 128:(i + 1) * 128],
                                                                  in_=src[:, kc * 128:(kc + 1) * 128],
                                                                  identity=cm.ident[:, :]),
                  [src, cm.ident], [tp], inc=(i == g - 1))
        cx.op("act", lambda e, g0=g0, g=g, tp=tp: e.copy(
            out=dstT[:, g0:g0 + g, col0:col0 + 128],
            in_=tp[:, 0:g * 128].rearrange("p (a b) -> p a b", a=g)), [tp], [dstT])


def ffn_phase(cx, cm, x_in, x_out, w_rep_ap, w_in, w_out, NT, D, F, eps, T=1024):
    nc = cx.nc
    KC = D // 128
    FC = F // 128
    NB = T // 128
    assert NT % T == 0 and FC % 2 == 0 and D % 256 == 0
    SLABW = max(32 * 256, FC * 256)
    w_rep = cx.sbuf("w_rep", [128, D], F32, dma=True)
    cx.dma("sp", lambda e: [e.dma_start(out=w_rep[:, :], in_=w_rep_ap[:, :])], None, w_rep)
    xts = [cx.sbuf("xt", [128, D], F32, dma=True) for _ in range(1)]
    hbf_ = cx.sbuf("hbf", [128, D], BF16)
    scratch = dict(junk=hbf_, ss=cx.sbuf("ss", [128, 1], F32),
                   rstd=cx.sbuf("rstd", [128, 1], F32), hbf=hbf_,
                   tp=[cx.psum("tp", [128, 512], BF16) for _ in range(2)], tpi=[0])
    hT = cx.sbuf("hT", [128, KC, T], BF16)
    hid = cx.sbuf("hid", [128, FC, T], BF16)
    slabs = [cx.sbuf("slab", [128, FC * 256], BF16, dma=True) for _ in range(2)]
    assert FC * 256 >= 2 * KC * 256
    slab_i = [0]
    pg = [cx.psum("pg", [128, 512], F32) for _ in range(2)]
    pu = [cx.psum("pu", [128, 512], F32) for _ in range(2)]
    py = [cx.psum("py", [128, 512], F32) for _ in range(2)]
    sg = [cx.sbuf("sg", [128, 512], BF16) for _ in range(2)]
    xs = [cx.sbuf("xs", [128, 256], F32, dma=True) for _ in range(3)]
    ys = [cx.sbuf("ys", [128, 256], F32, dma=True) for _ in range(3)]
    cnt = [0, 0]

    def next_slab(big=False):
        n = 2
        s = slabs[slab_i[0] % n]
        slab_i[0] += 1
        return s

    for st in range(NT // T):
        t0 = st * T
        for b in range(NB):
            xt = xts[0]
            r0 = t0 + b * 128
            cx.dma("sp", lambda e, xt=xt, r0=r0: [e.dma_start(out=xt[:, :], in_=x_in[r0:r0 + 128, :])], None, xt)
            rmsnorm_to_hT(cx, cm, xt, w_rep, hT, b * 128, D, eps, scratch)
        for jj in range(FC // 2):
            slab = next_slab()
            sv = slab[:, 0:2 * KC * 256].rearrange("p (a b) -> p a b", b=256)

            def ld(e, sv=sv, jj=jj):
                a = e.dma_start(out=sv[:, 0:KC, :],
                                in_=w_in[:, jj * 256:(jj + 1) * 256].rearrange("(kc p) n -> p kc n", p=128))
                b_ = e.dma_start(out=sv[:, KC:2 * KC, :],
                                 in_=w_in[:, F + jj * 256:F + (jj + 1) * 256].rearrange("(kc p) n -> p kc n", p=128))
                return [a, b_]

            cx.dma("pool", ld, None, slab, n=2)
            for j in range(2):
                c = jj * 2 + j
                for tt in range(T // 512):
                    k = cnt[0] % 2
                    cnt[0] += 1
                    for kc in range(KC):
                        cx.op("pe", lambda e, kc=kc, j=j, tt=tt, k=k, sv=sv: e.matmul(
                            pg[k][:, :], lhsT=sv[:, kc, j * 128:(j + 1) * 128], rhs=hT[:, kc, tt * 512:(tt + 1) * 512],
                            start=(kc == 0), stop=(kc == KC - 1)), [slab, hT], [pg[k]], inc=(kc == KC - 1))
                    for kc in range(KC):
                        cx.op("pe", lambda e, kc=kc, j=j, tt=tt, k=k, sv=sv: e.matmul(
                            pu[k][:, :], lhsT=sv[:, KC + kc, j * 128:(j + 1) * 128],
                            rhs=hT[:, kc, tt * 512:(tt + 1) * 512],
                            start=(kc == 0), stop=(kc == KC - 1)), [slab, hT], [pu[k]], inc=(kc == KC - 1))
                    cx.op("act", lambda e, k=k: e.activation(out=sg[k][:, :], in_=pg[k][:, :], func=AF.Silu),
                          [pg[k]], [sg[k]])
                    cx.op("dve", lambda e, k=k, c=c, tt=tt: e.tensor_tensor(
                        out=hid[:, c, tt * 512:(tt + 1) * 512], in0=sg[k][:, :], in1=pu[k][:, :], op=ALU.mult),
                        [sg[k], pu[k]], [hid])
        for nt in range(D // 256):
            slab = next_slab(big=True)
            sv = slab[:, 0:FC * 256].rearrange("p (a b) -> p a b", b=256)
            cx.dma("pool", lambda e, sv=sv, nt=nt: [e.dma_start(
                out=sv[:, :, :], in_=w_out[:, nt * 256:(nt + 1) * 256].rearrange("(c p) n -> p c n", p=128))],
                None, slab)
            for b in range(NB):
                k = cnt[1] % 2
                k4 = cnt[1] % 3
                cnt[1] += 1
                r0 = t0 + b * 128
                xsb, ysb = xs[k4], ys[k4]
                cx.dma("sp", lambda e, xsb=xsb, r0=r0, nt=nt: [e.dma_start(
                    out=xsb[:, :], in_=x_in[r0:r0 + 128, nt * 256:(nt + 1) * 256])], None, xsb)
                for c in range(FC):
                    cx.op("pe", lambda e, c=c, b=b, k=k, sv=sv: e.matmul(
                        py[k][:, 0:256], lhsT=hid[:, c, b * 128:(b + 1) * 128], rhs=sv[:, c, :],
                        start=(c == 0), stop=(c == FC - 1)), [slab, hid], [py[k]], inc=(c == FC - 1))
                cx.op("dve", lambda e, k=k, xsb=xsb, ysb=ysb: e.scalar_tensor_tensor(
                    out=ysb[:, :], in0=py[k][:, 0:256], scalar=0.5, in1=xsb[:, :], op0=ALU.mult, op1=ALU.add),
                    [py[k], xsb], [ysb])
                cx.dma("sp", lambda e, ysb=ysb, r0=r0, nt=nt: [e.dma_start(
                    out=x_out[r0:r0 + 128, nt * 256:(nt + 1) * 256], in_=ysb[:, :])], ysb, None)
    return ys


def finish(cx, out_bufs):
    cx.wait_all("sp", out_bufs)


def consts_array():
    c = np.zeros((128, 128), np.float32)
    c[:, 0:128] = np.eye(128, dtype=np.float32)
    return c


def outproj_phase(cx, cm, oT, w, x_in, x_out, NT, K, D, coef=1.0, T=1024):
    KC = K // 128
    NB = T // 128
    o_sb = cx.sbuf("o_sb", [128, KC * T], BF16, dma=True)
    slabs = [cx.sbuf("oslab", [128, KC * 256], BF16, dma=True) for _ in range(2)]
    py = [cx.psum("py", [128, 512], F32) for _ in range(2)]
    xs = [cx.sbuf("xs", [128, 256], F32, dma=True) for _ in range(3)]
    ys = [cx.sbuf("ys", [128, 256], F32, dma=True) for _ in range(3)]
    cnt = 0
    si = 0
    for st in range(NT // T):
        t0 = st * T
        cx.dma("sp", lambda e, t0=t0: [e.dma_start(out=o_sb[:, c * T:(c + 1) * T], in_=oT(c, t0, T)) for c in range(KC)], None, o_sb, n=KC)
        for nt in range(D // 256):
            slab = slabs[si % 2]
            si += 1
            sv = slab[:, 0:KC * 256].rearrange("p (a b) -> p a b", b=256)
            cx.dma("pool", lambda e, sv=sv, nt=nt: [e.dma_start(
                out=sv[:, :, :], in_=w[:, nt * 256:(nt + 1) * 256].rearrange("(c p) n -> p c n", p=128))],
                None, slab)
            for b in range(NB):
                k = cnt % 2
                k3 = cnt % 3
                cnt += 1
                r0 = t0 + b * 128
                xsb, ysb = xs[k3], ys[k3]
                cx.dma("sp", lambda e, xsb=xsb, r0=r0, nt=nt: [e.dma_start(
                    out=xsb[:, :], in_=x_in[r0:r0 + 128, nt * 256:(nt + 1) * 256])], None, xsb)
                for c in range(KC):
                    cx.op("pe", lambda e, c=c, b=b, k=k, sv=sv: e.matmul(
                        py[k][:, 0:256], lhsT=o_sb[:, c * T + b * 128:c * T + (b + 1) * 128], rhs=sv[:, c, :],
                        start=(c == 0), stop=(c == KC - 1)), [slab, o_sb], [py[k]], inc=(c == KC - 1))
                cx.op("dve", lambda e, k=k, xsb=xsb, ysb=ysb: e.scalar_tensor_tensor(
                    out=ysb[:, :], in0=py[k][:, 0:256], scalar=float(coef), in1=xsb[:, :], op0=ALU.mult, op1=ALU.add),
                    [py[k], xsb], [ysb])
                cx.dma("sp", lambda e, ysb=ysb, r0=r0, nt=nt: [e.dma_start(
                    out=x_out[r0:r0 + 128, nt * 256:(nt + 1) * 256], in_=ysb[:, :])], ysb, None)


def final_norm_phase(cx, cm, x_in, x_out, w_rep_ap, NT, D):
    w_rep = cx.sbuf("fw_rep", [128, D], F32, dma=True)
    cx.dma("sp", lambda e: [e.dma_start(out=w_rep[:, :], in_=w_rep_ap[:, :])], None, w_rep)
    xts = [cx.sbuf("fxt", [128, D], F32, dma=True) for _ in range(2)]
    yts = [cx.sbuf("fyt", [128, D], F32, dma=True) for _ in range(2)]
    junk = cx.sbuf("fjunk", [128, D], BF16)
    ss = cx.sbuf("fss", [128, 1], F32)
    rstd = cx.sbuf("frstd", [128, 1], F32)
    for b in range(NT // 128):
        xt, yt = xts[b % 2], yts[b % 2]
        cx.dma("sp", lambda e, xt=xt, b=b: [e.dma_start(out=xt[:, :], in_=x_in[b * 128:(b + 1) * 128, :])], None, xt)
        cx.op("act", lambda e, xt=xt: e.activation(out=junk[:, :], in_=xt[:, :], func=AF.Square, accum_out=ss[:, 0:1]),
              [xt], [junk, ss])
        cx.op("act", lambda e: e.activation(out=rstd[:, 0:1], in_=ss[:, 0:1], func=AF.Sqrt, scale=1.0 / D, bias=eps_ap(cx)),
              [ss], [rstd], scalars=[cx.eps_tile])
        cx.op("dve", lambda e: e.reciprocal(out=rstd[:, 0:1], in_=rstd[:, 0:1]), [rstd], [rstd])
        cx.op("dve", lambda e, xt=xt, yt=yt: e.scalar_tensor_tensor(
            out=yt[:, :], in0=xt[:, :], scalar=rstd[:, 0:1], in1=w_rep[:, :], op0=ALU.mult, op1=ALU.mult),
            [xt, w_rep], [yt], scalars=[rstd])
        cx.dma("sp", lambda e, yt=yt, b=b: [e.dma_start(out=x_out[b * 128:(b + 1) * 128, :], in_=yt[:, :])], yt, None)


def rope_tables(cx, pos_ap, invsign, P, N):
    C = cx.sbuf("rp_C", [P, N], F32)
    S = cx.sbuf("rp_S", [P, N], F32)
    with cx.scope():
        pi_ = cx.sbuf("rp_pi", [P, N], I32, dma=True)
        cx.dma("sp", lambda e: [e.dma_start(out=pi_[:, :], in_=pos_ap[0:P, 0:N])], None, pi_)
        ang = cx.sbuf("rp_ang", [P, N], F32)
        t = cx.sbuf("rp_t", [P, N], F32)
        cx.op("dve", lambda e: e.tensor_copy(out=ang[:, :], in_=pi_[:, :]), [pi_], [ang])
        cx.op("dve", lambda e: e.tensor_scalar(out=ang[:, :], in0=ang[:, :], scalar1=invsign[0:P, 0:1], scalar2=None,
                                                op0=ALU.mult), [ang], [ang], scalars=[invsign])
        cx.op("dve", lambda e: e.tensor_copy(out=t[:, :], in_=ang[:, :]), [ang], [t])
        sincos_reduce(cx, t, 0.0, [P, N])
        cx.op("act", lambda e: e.activation(out=S[:, :], in_=t[:, :], func=AF.Sin), [t], [S])
        cx.op("dve", lambda e: e.tensor_scalar(out=S[:, :], in0=S[:, :], scalar1=invsign[0:P, 1:2], scalar2=None,
                                                op0=ALU.mult), [S], [S], scalars=[invsign])
        sincos_reduce(cx, ang, float(np.pi / 2), [P, N])
        cx.op("act", lambda e: e.activation(out=C[:, :], in_=ang[:, :], func=AF.Sin), [ang], [C])
    return C, S


def load_invsign(cx, ap):
    b = cx.sbuf("invsign", [128, 2], F32, dma=True)
    cx.dma("sp", lambda e: [e.dma_start(out=b[:, :], in_=ap[:, :])], None, b)
    return b


def invsign_array(dim):
    half = dim // 2
    inv = (1.0 / (10000.0 ** (np.arange(0, dim, 2, dtype=np.float32) / np.float32(dim)))).astype(np.float32)
    a = np.zeros((128, 2), np.float32)
    for d in range(dim):
        a[d, 0] = inv[d % half]
        a[d, 1] = -1.0 if d < half else 1.0
    return a


class ProjWS:
    def __init__(self, cx, D, T, n_slab_elems):
        KC = D // 128
        self.KC, self.T, self.D = KC, T, D
        self.w_rep = cx.sbuf("pw_rep", [128, D], F32, dma=True)
        self.xt = cx.sbuf("pxt", [128, D], F32, dma=True)
        hbf = cx.sbuf("phbf", [128, D], BF16)
        self.scratch = dict(junk=hbf, ss=cx.sbuf("pss", [128, 1], F32), rstd=cx.sbuf("prstd", [128, 1], F32), hbf=hbf,
                            tp=[cx.psum("ptp", [128, 512], BF16) for _ in range(2)], tpi=[0])
        self.hT = cx.sbuf("phT", [128, KC, T], BF16)
        self.slabs = [cx.sbuf("pslab", [128, n_slab_elems], BF16, dma=True) for _ in range(2)]
        self.si = 0
        self.pa = [cx.psum("ppa", [128, 512], F32) for _ in range(2)]
        self.pb = [cx.psum("ppb", [128, 512], F32) for _ in range(2)]
        self.pi = 0
        self.t1 = [cx.sbuf("pt1", [128, 512], F32) for _ in range(2)]
        self.t2 = [cx.sbuf("pt2", [128, 512], F32) for _ in range(2)]
        self.ob = [cx.sbuf("pob", [128, 512], BF16, dma=True) for _ in range(3)]
        self.oi = 0

    def slab(self):
        s = self.slabs[self.si % 2]
        self.si += 1
        return s

    def out_buf(self):
        o = self.ob[self.oi % 3]
        self.oi += 1
        return o


def load_wcols(cx, slab, KC, W, w_ap, pieces):
    sv = slab[:, 0:KC * W].rearrange("p (a b) -> p a b", b=W)

    def ld(e):
        out = []
        for (dc, sc, wd) in pieces:
            out.append(e.dma_start(out=sv[:, :, dc:dc + wd],
                                   in_=w_ap[:, sc:sc + wd].rearrange("(kc p) n -> p kc n", p=128)))
        return out

    cx.dma("pool", ld, None, slab, n=len(pieces))
    return sv


def fm_mm(cx, ps, M, sv, c0, slab, xT, xbuf, cols, KC):
    n = cols[1] - cols[0]
    for kc in range(KC):
        cx.op("pe", lambda e, kc=kc: e.matmul(ps[0:M, 0:n], lhsT=sv[:, kc, c0:c0 + M], rhs=xT[:, kc, cols[0]:cols[1]],
                                              start=(kc == 0), stop=(kc == KC - 1)),
              [slab, xbuf], [ps], inc=(kc == KC - 1))


def tm_mm(cx, ps, N, sv, c0, slab, xT, xbuf, tok0, KC):
    for kc in range(KC):
        cx.op("pe", lambda e, kc=kc: e.matmul(ps[:, 0:N], lhsT=xT[:, kc, tok0:tok0 + 128], rhs=sv[:, kc, c0:c0 + N],
                                              start=(kc == 0), stop=(kc == KC - 1)),
              [slab, xbuf], [ps], inc=(kc == KC - 1))


def norm_block_to_hT(cx, cm, ws, x_in, r0, col0, eps=1e-6):
    cx.dma("sp", lambda e: [e.dma_start(out=ws.xt[:, :], in_=x_in[r0:r0 + 128, :])], None, ws.xt)
    rmsnorm_to_hT(cx, cm, ws.xt, ws.w_rep, ws.hT, col0, ws.D, eps, ws.scratch)


def rope_out(cx, ws, px, pp, M, C, S, tcol, dst_ap):
    i = ws.pi % 2
    t1, t2 = ws.t1[i], ws.t2[i]
    ob = ws.out_buf()
    cx.op("dve", lambda e: e.tensor_tensor(out=t1[0:M, :], in0=px[0:M, :], in1=C[0:M, tcol:tcol + 512], op=ALU.mult),
          [px, C], [t1])
    cx.op("dve", lambda e: e.tensor_tensor(out=t2[0:M, :], in0=pp[0:M, :], in1=S[0:M, tcol:tcol + 512], op=ALU.mult),
          [pp, S], [t2])
    cx.op("dve", lambda e: e.tensor_tensor(out=ob[0:M, :], in0=t1[0:M, :], in1=t2[0:M, :], op=ALU.add), [t1, t2], [ob])
    cx.dma("sp", lambda e: [e.dma_start(out=dst_ap, in_=ob[0:M, :])], ob, None)


def plain_out(cx, ws, px, M, N, dst_ap, func=None, src3=False):
    ob = ws.out_buf()
    if func is None:
        cx.op("act", lambda e: e.copy(out=ob[0:M, 0:N], in_=px[0:M, 0:N]), [px], [ob])
    else:
        cx.op("act", lambda e: e.activation(out=ob[0:M, 0:N], in_=px[0:M, 0:N], func=func), [px], [ob])
    if src3:
        cx.dma("sp", lambda e: [e.dma_start(out=dst_ap, in_=ob[0:M, 0:N].rearrange("p (j d) -> p j d", d=128))], ob, None)
    else:
        cx.dma("sp", lambda e: [e.dma_start(out=dst_ap, in_=ob[0:M, 0:N])], ob, None)


def nsa_proj_phase(cx, cm, x_in, w_rep_ap, w_in, pos_ap, invsign_ap, outs, NT, D, T=1024):
    KC = D // 128
    QC, KVC = 2048, 3072
    invsign = load_invsign(cx, invsign_ap)
    C, S = rope_tables(cx, pos_ap, invsign, 128, NT)
    ws = ProjWS(cx, D, T, KC * 512)
    cx.dma("sp", lambda e: [e.dma_start(out=ws.w_rep[:, :], in_=w_rep_ap[:, :])], None, ws.w_rep)
    for st in range(NT // T):
        t0 = st * T
        for b in range(T // 128):
            norm_block_to_hT(cx, cm, ws, x_in, t0 + b * 128, b * 128)
        roped = [("qT", h, h * 128) for h in range(16)] + \
                [("ksT", g, QC + 1 * 1024 + g * 128) for g in range(4)] + \
                [("kwT", g, QC + 2 * 1024 + g * 128) for g in range(4)]
        for (name, idx, c0) in roped:
            slab = ws.slab()
            sv = load_wcols(cx, slab, KC, 256, w_in, [(0, c0, 128), (128, c0 + 64, 64), (192, c0, 64)])
            for tt in range(T // 512):
                i = ws.pi % 2
                ws.pi += 1
                fm_mm(cx, ws.pa[i], 128, sv, 0, slab, ws.hT, ws.hT, (tt * 512, tt * 512 + 512), KC)
                fm_mm(cx, ws.pb[i], 128, sv, 128, slab, ws.hT, ws.hT, (tt * 512, tt * 512 + 512), KC)
                ws.pi -= 1
                rope_out(cx, ws, ws.pa[i], ws.pb[i], 128, C, S, t0 + tt * 512,
                         outs[name][idx * 128:(idx + 1) * 128, t0 + tt * 512:t0 + tt * 512 + 512])
                ws.pi += 1
        plain = [("kcT", g, QC + 0 * 1024 + g * 128) for g in range(4)] + \
                [("vcT", g, QC + 0 * 1024 + 512 + g * 128) for g in range(4)]
        for (name, idx, c0) in plain:
            slab = ws.slab()
            sv = load_wcols(cx, slab, KC, 128, w_in, [(0, c0, 128)])
            for tt in range(T // 512):
                i = ws.pi % 2
                ws.pi += 1
                fm_mm(cx, ws.pa[i], 128, sv, 0, slab, ws.hT, ws.hT, (tt * 512, tt * 512 + 512), KC)
                plain_out(cx, ws, ws.pa[i], 128, 512,
                          outs[name][idx * 128:(idx + 1) * 128, t0 + tt * 512:t0 + tt * 512 + 512])
        slab = ws.slab()
        sv = load_wcols(cx, slab, KC, 48, w_in, [(0, QC + KVC, 48)])
        for tt in range(T // 512):
            i = ws.pi % 2
            ws.pi += 1
            fm_mm(cx, ws.pa[i], 48, sv, 0, slab, ws.hT, ws.hT, (tt * 512, tt * 512 + 512), KC)
            plain_out(cx, ws, ws.pa[i], 48, 512, outs["gT"][0:48, t0 + tt * 512:t0 + tt * 512 + 512], func=AF.Sigmoid)
        for (name, c0) in (("vs", QC + 1 * 1024 + 512), ("vw", QC + 2 * 1024 + 512)):
            slab = ws.slab()
            sv = load_wcols(cx, slab, KC, 512, w_in, [(0, c0, 512)])
            for b in range(T // 128):
                i = ws.pi % 2
                ws.pi += 1
                tm_mm(cx, ws.pa[i], 512, sv, 0, slab, ws.hT, ws.hT, b * 128, KC)
                plain_out(cx, ws, ws.pa[i], 128, 512, outs[name](t0 + b * 128), src3=True)


class AttnWS:
    def __init__(self, cx, n_p=6):
        self.S = [cx.psum("aS", [128, 512], F32) for _ in range(3)]
        self.O = [cx.psum("aO", [128, 512], F32) for _ in range(2)]
        l_ = cx.psum("aL", [128, 512], F32)
        self.L = [l_, l_]
        self.P = [cx.sbuf("aP", [128, 512], BF16) for _ in range(n_p)]
        self.ones = cx.sbuf("aones", [128, 128], BF16)
        cx.op("dve", lambda e: e.memset(self.ones[:, :], 1.0), [], [self.ones])
        self.rl = [cx.sbuf("arl", [128, 512], F32) for _ in range(2)]
        self.Lacc = [cx.sbuf("aLacc", [128, 512], F32) for _ in range(2)]
        self.LaccB = [cx.sbuf("aLaccB", [128, 512], F32) for _ in range(2)]
        self.use_pool = False
        self.parity = 0
        self.ones_f = cx.sbuf("aones_f", [128, 128], F32)
        cx.op("dve", lambda e: e.memset(self.ones_f[:, :], 1.0), [], [self.ones_f])
        self.si = 0
        self.pi = 0
        self.oi = 0


def attn_score(cx, W, qk, clo, chi, scale, Pdst=None, blkmask=None, tilemask=None, addmask=None):
    Sp = W.S[W.si % len(W.S)]
    W.si += 1
    n_mm = len(qk) + (1 if addmask is not None else 0)
    j = 0
    for (qb, qa, kb, ka) in qk:
        j += 1
        cx.op("pe", lambda e, qa=qa, ka=ka, j=j: e.matmul(Sp[:, clo:chi], lhsT=ka, rhs=qa[:, clo:chi],
                                                          start=(j == 1), stop=(j == n_mm)),
              [qb, kb], [Sp], inc=(j == n_mm))
    if addmask is not None:
        lb, la, rb, ra = addmask
        cx.op("pe", lambda e: e.matmul(Sp[:, clo:chi], lhsT=la, rhs=ra[:, clo:chi], start=False, stop=True),
              [lb, rb], [Sp], inc=True)
    if Pdst is None:
        Pb = W.P[W.pi % len(W.P)]
        W.pi += 1
        Pa = Pb[:, :]
    else:
        Pb, Pa = Pdst
    cx.op("act", lambda e: e.activation(out=Pa[:, clo:chi], in_=Sp[:, clo:chi], func=AF.Exp, scale=float(scale)),
          [Sp], [Pb])
    if blkmask is not None:
        mb, ma, c0 = blkmask
        cx.op("dve", lambda e: e.tensor_tensor(out=Pa[:, c0:c0 + 128], in0=Pa[:, c0:c0 + 128], in1=ma, op=ALU.mult),
              [Pb, mb], [Pb])
    if tilemask is not None:
        mb, ma = tilemask
        cx.op("dve", lambda e: e.tensor_tensor(out=Pa[:, clo:chi], in0=Pa[:, clo:chi], in1=ma, op=ALU.mult),
              [Pb, mb], [Pb])
    return Pb, Pa


def attn_pv(cx, W, ol, P, v, clo, chi, first, last):
    Pb, Pa = P
    vb, va = v
    O, L = W.O[ol], W.L[ol]
    cx.op("pe", lambda e: e.matmul(O[:, clo:chi], lhsT=va, rhs=Pa[:, clo:chi], start=first, stop=last),
          [vb, Pb], [O], inc=last)
    La, Lb = W.Lacc[ol], W.LaccB[ol]
    if first:
        W.parity = 0
        cx.op("dve", lambda e: e.tensor_copy(out=La[:, clo:chi], in_=Pa[:, clo:chi]), [Pb], [La])
        if W.use_pool:
            cx.op("pool", lambda e: e.memset(Lb[:, :], 0.0), [], [Lb])
    else:
        W.parity += 1
        if W.use_pool and W.parity % 2 == 1:
            cx.op("pool", lambda e: e.tensor_tensor(out=Lb[:, clo:chi], in0=Lb[:, clo:chi], in1=Pa[:, clo:chi], op=ALU.add),
                  [Pb, Lb], [Lb])
        else:
            cx.op("dve", lambda e: e.tensor_tensor(out=La[:, clo:chi], in0=La[:, clo:chi], in1=Pa[:, clo:chi], op=ALU.add),
                  [Pb, La], [La])
    if last:
        if W.use_pool:
            cx.op("pe", lambda e: e.matmul(L[:, :], lhsT=W.ones_f[:, :], rhs=La[:, :], start=True, stop=False),
                  [W.ones_f, La], [L], inc=False)
            cx.op("pe", lambda e: e.matmul(L[:, :], lhsT=W.ones_f[:, :], rhs=Lb[:, :], start=False, stop=True),
                  [W.ones_f, Lb], [L], inc=True)
        else:
            cx.op("pe", lambda e: e.matmul(L[:, :], lhsT=W.ones_f[:, :], rhs=La[:, :], start=True, stop=True),
                  [W.ones_f, La], [L], inc=True)


def attn_chunk(cx, W, ol, qk, v, clo, chi, first, last, scale, Pdst=None, blkmask=None, tilemask=None, addmask=None):
    P = attn_score(cx, W, qk, clo, chi, scale, Pdst, blkmask, tilemask, addmask)
    attn_pv(cx, W, ol, P, v, clo, chi, first, last)


class AttnPipe:
    def __init__(self, cx, W, depth=2):
        self.cx, self.W = cx, W
        self.pending = []
        self.depth = depth

    def push(self, ol, qk, v, clo, chi, first, last, scale, after=None, **kw):
        P = attn_score(self.cx, self.W, qk, clo, chi, scale, **kw)
        self.pending.append((ol, P, v, clo, chi, first, last, after))
        while len(self.pending) > self.depth:
            self._drain()

    def _drain(self):
        ol, P, v, clo, chi, first, last, after = self.pending.pop(0)
        attn_pv(self.cx, self.W, ol, P, v, clo, chi, first, last)
        if after is not None:
            after()

    def flush(self):
        while self.pending:
            self._drain()


def attn_rl(cx, W, ol, gate=None):
    rl = W.rl[W.oi % 2]
    W.oi += 1
    L = W.L[ol]
    cx.op("dve", lambda e: e.tensor_scalar(out=rl[:, :], in0=L[:, :], scalar1=1e-30, scalar2=None, op0=ALU.max), [L], [rl])
    cx.op("dve", lambda e: e.reciprocal(out=rl[:, :], in_=rl[:, :]), [rl], [rl])
    if gate is not None:
        cx.op("dve", lambda e: e.tensor_tensor(out=rl[:, :], in0=rl[:, :], in1=gate[:, :], op=ALU.mult), [rl, gate], [rl])
    return rl


CB_TRI, CB_TRI2, CB_CMASK, CB_OV, CB_ENEG, CB_SELG = 0, 128, 256, 256 + 2560, 256 + 2560 + 2048, 256 + 2560 + 2048 + 8192
CB_W = CB_SELG + 12 * 128


def nsa_consts_bf16():
    c = np.zeros((128, CB_W), np.float32)
    k = np.arange(128)[:, None]
    q = np.arange(128)[None, :]
    c[:, CB_TRI:CB_TRI + 128] = (k <= q)
    c[:, CB_TRI2:CB_TRI2 + 128] = (q < k)
    nl = np.arange(128)[:, None]
    tl = np.arange(512)[None, :]
    for d in range(5):
        c[:, CB_CMASK + d * 512:CB_CMASK + (d + 1) * 512] = (16 * nl + 31 <= 512 * d + tl)
    for ch in range(8):
        n = 128 * ch + np.arange(128)[:, None]
        j = np.arange(256)[None, :]
        ov = (16 * n <= 64 * j + 63) & (16 * n + 31 >= 64 * j) & (n < 1023)
        c[:, CB_OV + ch * 256:CB_OV + (ch + 1) * 256] = ov
    jl = np.arange(128)[:, None]
    key = np.arange(128)[None, :]
    for kcl in range(64):
        c[:, CB_ENEG + kcl * 128:CB_ENEG + (kcl + 1) * 128] = np.where(jl == 2 * kcl + key // 64, -30000.0, 0.0)
    for i in range(12):
        c[i, CB_SELG + i * 128:CB_SELG + (i + 1) * 128] = 1.0
    return c.astype(ml_dtypes.bfloat16)


def nsa_ab_table(S):
    nb = S // 128
    t = (np.arange(nb)[:, None] * 128 + np.arange(128)[None, :])[:, :, None]
    cur = t // 64
    j = np.arange(256)[None, None, :]
    A = (j <= cur).astype(np.float32)
    B = np.where(j > cur, -1e30, 0.0).astype(np.float32)
    B = np.where((j == cur - 1) & (j > 0), 3e9, B)
    B = np.where((j == cur) & (j > 0), 2e9, B)
    B = np.where((j == 0), 1e9, B)
    B = np.where(j > cur, -1e30, B).astype(np.float32)
    return np.ascontiguousarray(np.concatenate([A, B], axis=-1))


def nsa_attn_phase(cx, cm, io, S, NTC=4096):
    NQT = S // 512
    NKC = S // 128
    NCMP = (S - 32) // 16 + 1
    NCC = (NCMP + 127) // 128
    scale = 128.0 ** -0.5
    cbf = cx.sbuf("cbf", [128, CB_W], BF16, dma=True)
    cx.dma("sp", lambda e: [e.dma_start(out=cbf[:, :], in_=io["cbf"][:, :])], None, cbf)
    kcmpT = cx.sbuf("kcmpT", [128, NCC * 128], BF16)
    vcmp = cx.sbuf("vcmp", [128, NCC, 128], BF16)
    cx.op("dve", lambda e: e.memset(kcmpT[:, :], 0.0), [], [kcmpT])
    cx.op("dve", lambda e: e.memset(vcmp[:, :, :], 0.0), [], [vcmp])
    with cx.scope():
        invsign = load_invsign(cx, io["invsign"])
        Cc, Sc = rope_tables(cx, io["pos_cmp"], invsign, 128, NCC * 128)
        tokT = cx.sbuf("tokT", [128, S], BF16, dma=True)
        w1s = cx.sbuf("w1s", [128, 32, 256], BF16, dma=True)
        w2s = cx.sbuf("w2s", [128, 2, 256], BF16, dma=True)
        pef = cx.sbuf("pef", [128, 32], F32, dma=True)
        peb = cx.sbuf("peb", [128, 32], BF16)
        bias = cx.sbuf("cbias", [128, 2], F32)
        gel = cx.sbuf("gel", [128, 2, NCC * 128], BF16)
        ph = [cx.psum("ph", [128, 512], F32) for _ in range(2)]
        pk = [cx.psum("pk", [128, 512], F32) for _ in range(2)]
        pb_ = cx.psum("pbias", [128, 512], F32)
        t1 = cx.sbuf("ct1", [128, 512], F32)
        t2 = cx.sbuf("ct2", [128, 512], F32)
        ntiles = [(n0, min(512, NCMP - n0)) for n0 in range(0, NCMP, 512)]
        hi = 0
        for kv in range(2):
            nm = "kcT" if kv == 0 else "vcT"
            cx.dma("sp", lambda e, nm=nm: [e.dma_start(out=tokT[:, rs * NTC:(rs + 1) * NTC], in_=io["feat"](nm, rs))
                                          for rs in range(S // NTC)], None, tokT, n=S // NTC)
            cx.dma("pool", lambda e, kv=kv: [e.dma_start(
                out=w1s[:, :, :], in_=io["w1"][kv].rearrange("(l d) h -> d l h", d=128))], None, w1s)

            def ldw2(e, kv=kv):
                a = e.dma_start(out=w2s[:, :, 0:128], in_=io["w2"][kv].rearrange("(c p) d -> p c d", p=128))
                b = e.dma_start(out=w2s[:, :, 128:192], in_=io["w2"][kv][:, 64:128].rearrange("(c p) d -> p c d", p=128))
                c = e.dma_start(out=w2s[:, :, 192:256], in_=io["w2"][kv][:, 0:64].rearrange("(c p) d -> p c d", p=128))
                return [a, b, c]

            cx.dma("pool", ldw2, None, w2s, n=3)
            cx.dma("sp", lambda e, kv=kv: [e.dma_start(out=pef[:, :], in_=io["peT"][kv])], None, pef)
            cx.op("dve", lambda e: e.tensor_copy(out=peb[:, :], in_=pef[:, :]), [pef], [peb])
            for hc in range(2):
                for l in range(32):
                    cx.op("pe", lambda e, hc=hc, l=l: e.matmul(pb_[:, hc:hc + 1], lhsT=w1s[:, l, hc * 128:(hc + 1) * 128],
                                                               rhs=peb[:, l:l + 1], start=(l == 0), stop=(l == 31)),
                          [w1s, peb], [pb_], inc=(l == 31))
            cx.op("dve", lambda e: e.tensor_copy(out=bias[:, 0:2], in_=pb_[:, 0:2]), [pb_], [bias])
            for (n0, cnt) in ntiles:
                for hc in range(2):
                    p = ph[hi % 2]
                    hi += 1
                    for l in range(32):
                        a0 = 16 * n0 + l
                        cx.op("pe", lambda e, hc=hc, l=l, p=p, a0=a0, cnt=cnt: e.matmul(
                            p[:, 0:cnt], lhsT=w1s[:, l, hc * 128:(hc + 1) * 128],
                            rhs=tokT[:, a0:a0 + 16 * (cnt - 1) + 1:16], start=(l == 0), stop=(l == 31)),
                            [w1s, tokT], [p], inc=(l == 31))
                    cx.op("act", lambda e, hc=hc, p=p, n0=n0, cnt=cnt: e.activation(
                        out=gel[:, hc, n0:n0 + cnt], in_=p[:, 0:cnt], func=AF.Gelu_apprx_tanh, bias=bias[:, hc:hc + 1]),
                        [p], [gel], scalars=[bias])
                if kv == 0:
                    for half in range(2):
                        p = pk[half]
                        for hc in range(2):
                            cx.op("pe", lambda e, hc=hc, p=p, half=half, n0=n0, cnt=cnt: e.matmul(
                                p[:, 0:cnt], lhsT=w2s[:, hc, half * 128:(half + 1) * 128], rhs=gel[:, hc, n0:n0 + cnt],
                                start=(hc == 0), stop=(hc == 1)), [w2s, gel], [p], inc=(hc == 1))
                    cx.op("dve", lambda e, n0=n0, cnt=cnt: e.tensor_tensor(
                        out=t1[:, 0:cnt], in0=pk[0][:, 0:cnt], in1=Cc[:, n0:n0 + cnt], op=ALU.mult), [pk[0], Cc], [t1])
                    cx.op("dve", lambda e, n0=n0, cnt=cnt: e.tensor_tensor(
                        out=t2[:, 0:cnt], in0=pk[1][:, 0:cnt], in1=Sc[:, n0:n0 + cnt], op=ALU.mult), [pk[1], Sc], [t2])
                    cx.op("dve", lambda e, n0=n0, cnt=cnt: e.tensor_tensor(
                        out=kcmpT[:, n0:n0 + cnt], in0=t1[:, 0:cnt], in1=t2[:, 0:cnt], op=ALU.add), [t1, t2], [kcmpT])
                else:
                    for c0 in range(n0, n0 + cnt, 128):
                        m = min(128, n0 + cnt - c0)
                        p = pk[(c0 // 128) % 2]
                        for hc in range(2):
                            cx.op("pe", lambda e, hc=hc, p=p, c0=c0, m=m: e.matmul(
                                p[0:m, 0:128], lhsT=gel[:, hc, c0:c0 + m], rhs=w2s[:, hc, 0:128],
                                start=(hc == 0), stop=(hc == 1)), [w2s, gel], [p], inc=(hc == 1))
                        cx.op("act", lambda e, p=p, c0=c0, m=m: e.copy(out=vcmp[0:m, c0 // 128, :], in_=p[0:m, 0:128]),
                              [p], [vcmp])
    W = AttnWS(cx)
    ksT = cx.sbuf("ksT", [128, S], BF16, dma=True)
    vs = cx.sbuf("vs", [128, NKC * 128], BF16, dma=True)
    cx.dma("sp", lambda e: [e.dma_start(out=ksT[:, rs * NTC:(rs + 1) * NTC], in_=io["feat"]("ksT", rs))
                            for rs in range(S // NTC)], None, ksT, n=S // NTC)
    CPR = NTC // 128
    cx.dma("sp", lambda e: [e.dma_start(out=vs[:, rs * NTC:(rs + 1) * NTC], in_=io["vrank"]("vs", rs))
                            for rs in range(S // NTC)], None, vs, n=S // NTC)
    qts = [cx.sbuf("qt", [128, 4 * 512], BF16, dma=True) for _ in range(2)]
    gts = [cx.sbuf("gt", [12, 512], BF16, dma=True) for _ in range(2)]
    kws = [cx.sbuf("kw", [128, 1024], BF16, dma=True) for _ in range(2)]
    vws = [cx.sbuf("vwb", [128, 8 * 128], BF16, dma=True) for _ in range(2)]
    abs_ = [cx.sbuf("ab", [128, 512], F32, dma=True) for _ in range(2)]
    pcn = [[cx.sbuf("pcn", [128, 512], BF16) for _c in range(NCC)] for _r in range(4)]
    nselT = cx.sbuf("nselT", [128, 2, 512], BF16)
    acc = cx.sbuf("acc", [128, 4, 512], F32)
    accb = [cx.sbuf("accb", [128, 4 * 512], BF16, dma=True) for _ in range(2)]
    otmp = cx.sbuf("otmp", [128, 512], F32)
    gps = [cx.psum("gps", [128, 512], F32) for _ in range(1)]
    ips = cx.psum("ips", [128, 512], F32)
    seltp_ap = ips[:, 256:512].bitcast(BF16)
    impa = cx.sbuf("impa", [128, 256], F32)
    wk = cx.sbuf("wk", [128, 256], F32)
    mx = cx.sbuf("mx", [128, 16], F32)
    selb = cx.sbuf("selb", [128, 256], BF16)
    tps = cx.psum("ttp", [128, 256], BF16) if False else None
    abi = 0

    def gate_bcast(gt, i):
        g = gps[0]
        cx.op("pe", lambda e: e.matmul(g[:, :], lhsT=cbf[0:12, CB_SELG + i * 128:CB_SELG + (i + 1) * 128], rhs=gt[0:12, :],
                                       start=True, stop=True), [cbf, gt], [g])
        return g

    def finish_branch(ol, gt, r, br):
        g = gate_bcast(gt, r * 3 + br)
        rl = attn_rl(cx, W, ol, gate=g)
        O = W.O[ol]
        if br == 0:
            cx.op("dve", lambda e: e.tensor_tensor(out=acc[:, r, :], in0=O[:, :], in1=rl[:, :], op=ALU.mult), [O, rl], [acc])
        else:
            cx.op("dve", lambda e: e.tensor_tensor(out=otmp[:, :], in0=O[:, :], in1=rl[:, :], op=ALU.mult), [O, rl], [otmp])
            cx.op("dve", lambda e: e.tensor_tensor(out=acc[:, r, :], in0=acc[:, r, :], in1=otmp[:, :], op=ALU.add),
                  [otmp, acc], [acc])

    oli = 0
    for qt in range(NQT):
        q0 = qt * 512
        qb, gt, kw, vwb = qts[qt % 2], gts[qt % 2], kws[qt % 2], vws[qt % 2]
        cx.dma("sp", lambda e, qb=qb, q0=q0: [e.dma_start(out=qb[:, rh * 512:(rh + 1) * 512], in_=io["q"](rh, q0)) for rh in range(4)],
               None, qb, n=4)
        cx.dma("sp", lambda e, gt=gt, q0=q0: [e.dma_start(out=gt[:, :], in_=io["g"](q0))], None, gt)
        halves = [(q0 - 512, 0), (q0, 1)] if q0 > 0 else [(q0, 1)]
        cx.dma("sp", lambda e, kw=kw, halves=halves: [e.dma_start(
            out=kw[:, hh * 512:(hh + 1) * 512], in_=io["kwh"](qh)) for (qh, hh) in halves], None, kw, n=len(halves))
        cx.dma("sp", lambda e, vwb=vwb, halves=halves: [e.dma_start(
            out=vwb[:, hh * 512:(hh + 1) * 512], in_=io["vwh"](qh)) for (qh, hh) in halves], None, vwb, n=len(halves))
        vis = []
        for c in range(NCC):
            d = qt - 4 * c
            if d < 0:
                continue
            vis.append((c, d if d <= 4 else None))
        def cmp_fin(ol, r):
            rl0 = attn_rl(cx, W, ol)
            for (c, d) in vis:
                pb = pcn[r][c]
                cx.op("dve", lambda e, pb=pb, rl0=rl0: e.tensor_tensor(out=pb[:, :], in0=pb[:, :], in1=rl0[:, :], op=ALU.mult),
                      [pb, rl0], [pb])
            g = gate_bcast(gt, r * 3 + 0)
            O = W.O[ol]
            cx.op("dve", lambda e, rl0=rl0, g=g: e.tensor_tensor(out=rl0[:, :], in0=rl0[:, :], in1=g[:, :], op=ALU.mult),
                  [rl0, g], [rl0])
            cx.op("dve", lambda e, r=r, O=O, rl0=rl0: e.tensor_tensor(out=acc[:, r, :], in0=O[:, :], in1=rl0[:, :], op=ALU.mult),
                  [O, rl0], [acc])

        pipe_c = AttnPipe(cx, W)
        for r in range(4):
            ol = oli % 2
            oli += 1
            for ci, (c, d) in enumerate(vis):
                tm = None if d is None else (cbf, cbf[:, CB_CMASK + d * 512:CB_CMASK + (d + 1) * 512])
                aft = (lambda ol=ol, r=r: cmp_fin(ol, r)) if ci == len(vis) - 1 else None
                pipe_c.push(ol, [(qb, qb[:, r * 512:(r + 1) * 512], kcmpT, kcmpT[:, c * 128:(c + 1) * 128])],
                            (vcmp, vcmp[:, c, :]), 0, 512, ci == 0, ci == len(vis) - 1, scale, after=aft,
                            Pdst=(pcn[r][c], pcn[r][c][:, :]), tilemask=tm)
        pipe_c.flush()
        for qbk in range(4):
            ab = abs_[abi % 2]
            abi += 1
            gb = qt * 4 + qbk
            cx.dma("sp", lambda e, ab=ab, gb=gb: [e.dma_start(out=ab[:, :], in_=io["ab"][gb])], None, ab)
            n = 4 * len(vis)
            i = 0
            for r in range(4):
                for (c, d) in vis:
                    i += 1
                    pb = pcn[r][c]
                    cx.op("pe", lambda e, pb=pb, c=c, i=i, qbk=qbk: e.matmul(
                        ips[:, 0:256], lhsT=pb[:, qbk * 128:(qbk + 1) * 128],
                        rhs=cbf[:, CB_OV + c * 256:CB_OV + (c + 1) * 256], start=(i == 1), stop=(i == n)),
                        [pb, cbf], [ips], inc=(i == n))
            cx.op("dve", lambda e, ab=ab: e.tensor_tensor(out=impa[:, :], in0=ips[:, 0:256], in1=ab[:, 0:256], op=ALU.mult),
                  [ips, ab], [impa])
            cx.op("dve", lambda e, ab=ab: e.tensor_tensor(out=impa[:, :], in0=impa[:, :], in1=ab[:, 256:512], op=ALU.add),
                  [impa, ab], [impa])
            cx.op("dve", lambda e: e.max(out=mx[:, 0:8], in_=impa[:, :]), [impa], [mx])
            cx.op("dve", lambda e: e.match_replace(out=wk[:, :], in_to_replace=mx[:, 0:8], in_values=impa[:, :],
                                                   imm_value=-3e38), [impa], [wk], scalars=[mx])
            cx.op("dve", lambda e: e.max(out=mx[:, 8:16], in_=wk[:, :]), [wk], [mx])
            cx.op("dve", lambda e: e.tensor_scalar(out=wk[:, :], in0=impa[:, :], scalar1=mx[:, 15:16], scalar2=None,
                                                   op0=ALU.is_ge), [impa], [wk], scalars=[mx])
            cx.op("dve", lambda e, ab=ab: e.tensor_tensor(out=wk[:, :], in0=wk[:, :], in1=ab[:, 0:256], op=ALU.mult),
                  [wk, ab], [wk])
            cx.op("dve", lambda e: e.tensor_scalar(out=selb[:, :], in0=wk[:, :], scalar1=-1.0, scalar2=1.0,
                                                   op0=ALU.mult, op1=ALU.add), [wk], [selb])
            for jc in range(2):
                cx.op("pe", lambda e, jc=jc: e.transpose(out=seltp_ap[:, jc * 128:(jc + 1) * 128],
                                                         in_=selb[:, jc * 128:(jc + 1) * 128],
                                                         identity=cm.ident[:, :]), [selb, cm.ident], [ips], inc=(jc == 1))
            cx.op("act", lambda e, qbk=qbk: e.copy(
                out=nselT[:, :, qbk * 128:(qbk + 1) * 128], in_=seltp_ap[:, 0:256].rearrange("p (a b) -> p a b", a=2)),
                [ips], [nselT])
        pipe = AttnPipe(cx, W)
        W.use_pool = True
        for r in range(4):
            ol = oli % 2
            oli += 1
            nk = 4 * (qt + 1)
            for kc in range(nk):
                m = kc - 4 * qt
                clo = 128 * max(m, 0)
                bm = None if m < 0 else (cbf, cbf[:, CB_TRI:CB_TRI + 128], clo)
                am = (cbf, cbf[:, CB_ENEG + (kc % 64) * 128:CB_ENEG + (kc % 64 + 1) * 128], nselT, nselT[:, kc // 64, :])
                aft = (lambda ol=ol, r=r: finish_branch(ol, gt, r, 1)) if kc == nk - 1 else None
                pipe.push(ol, [(qb, qb[:, r * 512:(r + 1) * 512], ksT, ksT[:, kc * 128:(kc + 1) * 128])],
                          (vs, vs[:, kc * 128:(kc + 1) * 128]), clo, 512, kc == 0, kc == nk - 1, scale, after=aft,
                          blkmask=bm, addmask=am)
        for r in range(4):
            ol = oli % 2
            oli += 1
            order = [-1, -4, -3, -2, 0, 1, 2, 3] if qt > 0 else [0, 1, 2, 3]
            for oi_, m in enumerate(order):
                if m < 0:
                    clo, chi = 0, 128 * (m + 5)
                    bm = (cbf, cbf[:, CB_TRI2:CB_TRI2 + 128], 128 * (m + 4))
                else:
                    clo, chi = 128 * m, 512
                    bm = (cbf, cbf[:, CB_TRI:CB_TRI + 128], clo)
                aft = (lambda ol=ol, r=r: finish_branch(ol, gt, r, 2)) if oi_ == len(order) - 1 else None
                pipe.push(ol, [(qb, qb[:, r * 512:(r + 1) * 512], kw, kw[:, (m + 4) * 128:(m + 5) * 128])],
                          (vwb, vwb[:, (m + 4) * 128:(m + 5) * 128]), clo, chi, oi_ == 0, oi_ == len(order) - 1, scale,
                          after=aft, blkmask=bm)
        pipe.flush()
        W.use_pool = False
        ab_ = accb[qt % 2]
        cx.op("act", lambda e, ab_=ab_: e.copy(out=ab_[:, :], in_=acc[:, :, :].rearrange("p a b -> p (a b)")), [acc], [ab_])
        cx.dma("sp", lambda e, ab_=ab_, q0=q0: [e.dma_start(out=io["og"](rh, q0), in_=ab_[:, rh * 512:(rh + 1) * 512]) for rh in range(4)],
               ab_, None, n=4)
        if "after_store" in io and (qt + 1) % (NTC // 512) == 0:
            io["after_store"](qt // (NTC // 512), list(range(4)), accb)


def cx_tp(cx):
    if not hasattr(cx, "_tp") or cx._tp_scope is not cx.stack:
        cx._tp = cx.psum("seltp", [128, 512], BF16)
        cx._tp_scope = cx.stack
    return cx._tp


def mla_proj_phase(cx, cm, x_in, w_rep_ap, w_in, qn_rep_ap, kvn_rep_ap, w_uq, w_ukv, pos_ap, invsign_ap, outs, NT, D, T=1024):
    KC = D // 128
    invsign = load_invsign(cx, invsign_ap)
    C, S = rope_tables(cx, pos_ap, invsign, 64, NT)
    ws = ProjWS(cx, D, T, KC * 512)
    cx.dma("sp", lambda e: [e.dma_start(out=ws.w_rep[:, :], in_=w_rep_ap[:, :])], None, ws.w_rep)
    nrm = [cx.sbuf("mnrm", [128, 512], F32, dma=True) for _ in range(2)]
    cx.dma("sp", lambda e: [e.dma_start(out=nrm[0][:, :], in_=qn_rep_ap[:, :])], None, nrm[0])
    cx.dma("sp", lambda e: [e.dma_start(out=nrm[1][:, :], in_=kvn_rep_ap[:, :])], None, nrm[1])
    cT = [cx.sbuf("mcT", [128, 4, T], BF16) for _ in range(2)]
    cbf_ = cx.sbuf("mcbf", [128, 512], BF16)
    sc2 = dict(junk=cbf_, ss=cx.sbuf("mss", [128, 1], F32), rstd=cx.sbuf("mrstd", [128, 1], F32), hbf=cbf_,
               tp=ws.scratch["tp"], tpi=ws.scratch["tpi"])
    for st in range(NT // T):
        t0 = st * T
        for b in range(T // 128):
            norm_block_to_hT(cx, cm, ws, x_in, t0 + b * 128, b * 128)
        for which in range(2):
            slab = ws.slab()
            sv = load_wcols(cx, slab, KC, 512, w_in, [(0, which * 512, 512)])
            for b in range(T // 128):
                i = ws.pi % 2
                ws.pi += 1
                p = ws.pa[i]
                tm_mm(cx, p, 512, sv, 0, slab, ws.hT, ws.hT, b * 128, KC)
                junk, ss, rstd = sc2["junk"], sc2["ss"], sc2["rstd"]
                cx.op("act", lambda e, p=p: e.activation(out=junk[:, :], in_=p[:, :], func=AF.Square, accum_out=ss[:, 0:1]),
                      [p], [junk, ss])
                cx.op("act", lambda e: e.activation(out=rstd[:, 0:1], in_=ss[:, 0:1], func=AF.Sqrt, scale=1.0 / 512,
                                                    bias=eps_ap(cx)), [ss], [rstd], scalars=[cx.eps_tile])
                cx.op("dve", lambda e: e.reciprocal(out=rstd[:, 0:1], in_=rstd[:, 0:1]), [rstd], [rstd])
                cx.op("dve", lambda e, p=p, which=which: e.scalar_tensor_tensor(
                    out=cbf_[:, :], in0=p[:, :], scalar=rstd[:, 0:1], in1=nrm[which][:, :], op0=ALU.mult, op1=ALU.mult),
                    [p, nrm[which]], [cbf_], scalars=[rstd])
                transpose_to_fm(cx, cm, cbf_, cT[which], b * 128, 4, sc2)
        slab = ws.slab()
        sv = load_wcols(cx, slab, KC, 128, w_in, [(0, 1024, 64), (64, 1024 + 32, 32), (96, 1024, 32)])
        for tt in range(T // 512):
            i = ws.pi % 2
            fm_mm(cx, ws.pa[i], 64, sv, 0, slab, ws.hT, ws.hT, (tt * 512, tt * 512 + 512), KC)
            fm_mm(cx, ws.pb[i], 64, sv, 64, slab, ws.hT, ws.hT, (tt * 512, tt * 512 + 512), KC)
            rope_out(cx, ws, ws.pa[i], ws.pb[i], 64, C, S, t0 + tt * 512, outs["krT"][0:64, t0 + tt * 512:t0 + tt * 512 + 512])
            ws.pi += 1
        for h in range(16):
            c0 = h * 192
            slab = ws.slab()
            sv = load_wcols(cx, slab, 4, 256, w_uq, [(0, c0, 128), (128, c0 + 128, 64), (192, c0 + 160, 32), (224, c0 + 128, 32)])
            for tt in range(T // 512):
                cols = (tt * 512, tt * 512 + 512)
                i = ws.pi % 2
                ws.pi += 1
                fm_mm(cx, ws.pa[i], 128, sv, 0, slab, cT[0], cT[0], cols, 4)
                plain_out(cx, ws, ws.pa[i], 128, 512, outs["qnT"][h * 128:(h + 1) * 128, t0 + cols[0]:t0 + cols[1]])
                i = ws.pi % 2
                fm_mm(cx, ws.pa[i], 64, sv, 128, slab, cT[0], cT[0], cols, 4)
                fm_mm(cx, ws.pb[i], 64, sv, 192, slab, cT[0], cT[0], cols, 4)
                rope_out(cx, ws, ws.pa[i], ws.pb[i], 64, C, S, t0 + cols[0], outs["qrT"][h * 64:(h + 1) * 64, t0 + cols[0]:t0 + cols[1]])
                ws.pi += 1
        for h in range(16):
            slab = ws.slab()
            sv = load_wcols(cx, slab, 4, 128, w_ukv, [(0, h * 256, 128)])
            for tt in range(T // 512):
                cols = (tt * 512, tt * 512 + 512)
                i = ws.pi % 2
                ws.pi += 1
                fm_mm(cx, ws.pa[i], 128, sv, 0, slab, cT[1], cT[1], cols, 4)
                plain_out(cx, ws, ws.pa[i], 128, 512, outs["knT"][h * 128:(h + 1) * 128, t0 + cols[0]:t0 + cols[1]])
        for hq in range(4):
            slab = ws.slab()
            sv = load_wcols(cx, slab, 4, 512, w_ukv, [(j * 128, (hq * 4 + j) * 256 + 128, 128) for j in range(4)])
            for b in range(T // 128):
                i = ws.pi % 2
                ws.pi += 1
                tm_mm(cx, ws.pa[i], 512, sv, 0, slab, cT[1], cT[1], b * 128, 4)
                plain_out(cx, ws, ws.pa[i], 128, 512, outs["v"](t0 + b * 128, hq), src3=True)


def mla_attn_phase(cx, cm, io, S, NH=4, NTC=4096):
    NQT = S // 512
    NKC = S // 128
    scale = 192.0 ** -0.5
    W = AttnWS(cx)
    tri = cx.sbuf("mtri", [128, 128], BF16, dma=True)
    cx.dma("sp", lambda e: [e.dma_start(out=tri[:, :], in_=io["tri"][:, :])], None, tri)
    krT = cx.sbuf("krT", [64, S], BF16, dma=True)
    NR = S // NTC
    CPR = NTC // 128
    cx.dma("sp", lambda e: [e.dma_start(out=krT[:, rs * NTC:(rs + 1) * NTC], in_=io["kr"](rs)) for rs in range(NR)],
           None, krT, n=NR)
    kns = [cx.sbuf("knT", [128, S], BF16, dma=True) for _ in range(2)]
    vss = [cx.sbuf("mv", [128, NKC * 128], BF16, dma=True) for _ in range(2)]
    qns = [cx.sbuf("mqn", [128, 512], BF16, dma=True) for _ in range(2)]
    qrs = [cx.sbuf("mqr", [64, 512], BF16, dma=True) for _ in range(2)]
    obs = [cx.sbuf("mob", [128, 512], BF16, dma=True) for _ in range(2)]
    it = 0
    pipe = AttnPipe(cx, W)
    for h in range(NH):
        kn, vv = kns[h % 2], vss[h % 2]
        cx.dma("sp", lambda e, kn=kn, h=h: [e.dma_start(out=kn[:, rs * NTC:(rs + 1) * NTC], in_=io["kn"](h, rs))
                                            for rs in range(NR)], None, kn, n=NR)
        cx.dma("sp", lambda e, vv=vv, h=h: [e.dma_start(out=vv[:, rs * NTC:(rs + 1) * NTC], in_=io["v"](h, rs))
                                            for rs in range(NR)], None, vv, n=NR)
        for qt in range(NQT):
            q0 = qt * 512
            qn, qr, ob = qns[it % 2], qrs[it % 2], obs[it % 2]
            ol = it % 2
            it += 1
            cx.dma("sp", lambda e, qn=qn, h=h, q0=q0: [e.dma_start(out=qn[:, :], in_=io["qn"](h, q0))], None, qn)
            cx.dma("sp", lambda e, qr=qr, h=h, q0=q0: [e.dma_start(out=qr[:, :], in_=io["qr"](h, q0))], None, qr)
            nk = 4 * (qt + 1)
            for kc in range(nk):
                m = kc - 4 * qt
                clo = 128 * max(m, 0)
                bm = None if m < 0 else (tri, tri[:, :], clo)
                def fin(ol=ol, ob=ob, h=h, q0=q0):
                    rl = attn_rl(cx, W, ol)
                    O = W.O[ol]
                    cx.op("dve", lambda e: e.tensor_tensor(out=ob[:, :], in0=O[:, :], in1=rl[:, :], op=ALU.mult), [O, rl], [ob])
                    cx.dma("sp", lambda e: [e.dma_start(out=io["o"](h, q0), in_=ob[:, :])], ob, None)
                    if "after_store" in io and (q0 // 512 + 1) % (NTC // 512) == 0:
                        io["after_store"](q0 // NTC, [h], obs)

                pipe.push(ol, [(qn, qn[:, :], kn, kn[:, kc * 128:(kc + 1) * 128]),
                               (qr, qr[0:64, :], krT, krT[0:64, kc * 128:(kc + 1) * 128])],
                          (vv, vv[:, kc * 128:(kc + 1) * 128]), clo, 512, kc == 0, kc == nk - 1, scale,
                          after=(fin if kc == nk - 1 else None), blkmask=bm)
    pipe.flush()


NCORE = 8
DM, DFF, SEQ_, NTC = 2048, 5632, 16384, 4096
GROUPS = [[0, 1, 2, 3], [4, 5, 6, 7]]
UA, UB, UC = 41, 16, 57


def rep128(v):
    return np.ascontiguousarray(np.tile(np.asarray(v, np.float32)[None, :], (128, 1)))


def exchange(cx, src2d, parts, copies):
    with cx.scope():
        bds = []
        bs = cx.dram_buf(src2d, "xs")
        for (u0, n, dst2d) in parts:
            bd = cx.dram_buf(dst2d, "xd")
            for j in range(n):
                u = u0 + j
                cx.cc(lambda e, u=u, j=j, dst2d=dst2d: e.collective_compute(
                    "AllGather", ALU.bypass, replica_groups=GROUPS, ins=[src2d[u * 128:(u + 1) * 128, :]],
                    outs=[dst2d[j * 512:(j + 1) * 512, :]]), bs, bd)
            bds.append(bd)
        mine = cx.dram_buf(None, "mine")
        for (out_ap, in_fn, pi_) in copies:
            cx.wait_all("sp", [bds[pi_]])
            cx.dma("sp", lambda e, out_ap=out_ap, in_fn=in_fn: [e.dma_start(out=out_ap, in_=in_fn())], None, mine)


def out_exchange(cx, io, src2d, dst2d, mine_ap, rv, run_phase):
    with cx.scope():
        bs = cx.dram_buf(src2d, "xs")
        bd = cx.dram_buf(dst2d, "xd")

        def after_store(tb, hls, store_bufs):
            cx.wait_all("pool", store_bufs)
            for hl in hls:
                u = tb * 4 + hl
                cx.cc(lambda e, u=u: e.collective_compute(
                    "AllGather", ALU.bypass, replica_groups=GROUPS, ins=[src2d[u * 128:(u + 1) * 128, :]],
                    outs=[dst2d[u * 512:(u + 1) * 512, :]]), bs, bd)

        io["after_store"] = after_store
        with cx.scope():
            run_phase()
        cx.wait_all("sp", [bd])
        mine = cx.dram_buf(None, "mine")
        cx.dma("sp", lambda e: [e.dma_start(out=mine_ap[:, :], in_=dst2d[bass.ds(rv() * 2048, 2048), :])], None, mine)


def build_program():
    nc = bass.Bass("TRN2", target_bir_lowering=False)

    def I(name, shape, dt=F32):
        return nc.dram_tensor(name, list(shape), dt, kind="ExternalInput").ap()

    def T(name, shape, dt=F32):
        return nc.dram_tensor(name, list(shape), dt).ap()

    a_x = I("x", [NTC, DM]); a_rank = I("rank", [1, 2], I32); a_pos = I("pos", [128, NTC], I32)
    a_posc = I("pos_cmp", [128, 1024], I32); a_inv128 = I("inv128", [128, 2]); a_inv64 = I("inv64", [128, 2])
    a_c = I("consts", [128, 128]); a_cbf = I("cbf", [128, CB_W], BF16); a_ab = I("ab", [SEQ_ // 128, 128, 512])
    fw = [(I("fw_rep%d" % k, [128, DM]), I("f_w_in%d" % k, [DM, 2 * DFF]), I("f_w_out%d" % k, [DFF, DM])) for k in range(4)]
    a_mw0 = I("mw_rep0", [128, DM]); a_mw1 = I("mw_rep1", [128, DM])
    a_nw = I("nsa_w_in", [DM, 5168]); a_w1 = I("w1", [2, 4096, 256]); a_w2 = I("w2", [2, 256, 128]); a_peT = I("peT", [2, 128, 32])
    a_now = I("nsa_w_out", [2048, DM])
    a_mwi = I("mla_w_in", [DM, 1088]); a_qn = I("qn_rep", [128, 512]); a_kvn = I("kvn_rep", [128, 512])
    a_uq = I("w_uq", [512, 3072]); a_ukv = I("w_ukv", [512, 4096]); a_mow = I("mla_w_out", [2048, DM])
    a_fn = I("fn_rep", [128, DM])
    a_out = nc.dram_tensor("out", [NTC, DM], F32, kind="ExternalOutput").ap()
    xs_ = [T("xr%d" % i, [NTC, DM]) for i in range(6)]
    A_src = T("A_src", [UA * 128, NTC], BF16)
    Aq = T("Aq", [16 * 512, NTC], BF16); Af = T("Af", [16 * 512, NTC], BF16); Av = T("Av", [8 * 512, NTC], BF16); Ag = T("Ag", [512, NTC], BF16)
    B_src = T("B_src", [UB * 128, NTC], BF16); B_dst = T("B_dst", [UB * 512, NTC], BF16)
    myQ = T("myQ", [2048, NTC], BF16); myF = T("myF", [2048, NTC], BF16); myV = T("myV", [1024, NTC], BF16); myG = T("myG", [48, NTC], BF16)
    myO = T("myO", [2048, NTC], BF16); myO2 = T("myO2", [2048, NTC], BF16)
    myQn = T("myQn", [2048, NTC], BF16); myQr = T("myQr", [1024, NTC], BF16); myKn = T("myKn", [2048, NTC], BF16); myV2 = T("myV2", [2048, NTC], BF16)
    C_src = T("C_src", [UC * 128, NTC], BF16)
    Cqn = T("Cqn", [16 * 512, NTC], BF16); Cqr = T("Cqr", [8 * 512, NTC], BF16); Ckn = T("Ckn", [16 * 512, NTC], BF16)
    Ckr = T("Ckr", [512, NTC], BF16); Cv = T("Cv", [16 * 512, NTC], BF16)
    D_src = T("D_src", [UB * 128, NTC], BF16); D_dst = T("D_dst", [UB * 512, NTC], BF16)

    with ExitStack() as stack:
        cx = Ctx(nc, stack)
        cm = Common(cx, a_c)
        rv = cx.load_rank("sp", a_rank)

        def split(q0):
            return q0 // NTC, q0 % NTC

        with cx.scope():
            ffn_phase(cx, cm, a_x, xs_[0], fw[0][0], fw[0][1], fw[0][2], NTC, DM, DFF, 1e-6)
        VS = A_src[32 * 128:36 * 128, :].rearrange("(g p) c -> g p c", g=4)
        VW = A_src[36 * 128:40 * 128, :].rearrange("(g p) c -> g p c", g=4)
        outs = dict(qT=A_src[0:2048, :], kcT=A_src[2048:2560, :], vcT=A_src[2560:3072, :], ksT=A_src[3072:3584, :],
                    kwT=A_src[3584:4096, :], gT=A_src[40 * 128:40 * 128 + 48, :],
                    vs=lambda t0: VS[:, :, t0:t0 + 128].rearrange("g p d -> p g d"),
                    vw=lambda t0: VW[:, :, t0:t0 + 128].rearrange("g p d -> p g d"))
        with cx.scope():
            nsa_proj_phase(cx, cm, xs_[0], a_mw0, a_nw, a_pos, a_inv128, outs, NTC, DM)
        cpA = [(myQ[:, :], lambda: Aq[bass.ds(rv() * 2048, 2048), :], 0)]
        for i_ in range(4):
            cpA.append((myF[i_ * 512:(i_ + 1) * 512, :], lambda i_=i_: Af[bass.ds(rv() * 512 + i_ * 2048, 512), :], 1))
        for i_ in range(2):
            cpA.append((myV[i_ * 512:(i_ + 1) * 512, :], lambda i_=i_: Av[bass.ds(rv() * 512 + i_ * 2048, 512), :], 2))
        for rs_ in range(4):
            cpA.append((myG[rs_ * 12:(rs_ + 1) * 12, :], lambda rs_=rs_: Ag[bass.ds(rv() * 12 + rs_ * 128, 12), :], 3))
        exchange(cx, A_src, [(0, 16, Aq), (16, 16, Af), (32, 8, Av), (40, 1, Ag)], cpA)
        FU = {"kcT": 0, "vcT": 1, "ksT": 2, "kwT": 3}
        VU = {"vs": 0, "vw": 1}

        def a_q(rh, q0):
            rs, c0 = split(q0)
            return myQ[rh * 512 + rs * 128:rh * 512 + rs * 128 + 128, c0:c0 + 512]

        def a_g(q0):
            rs, c0 = split(q0)
            return myG[rs * 12:(rs + 1) * 12, c0:c0 + 512]

        def a_feat(nm, rs):
            return myF[FU[nm] * 512 + rs * 128:FU[nm] * 512 + rs * 128 + 128, :]

        def a_kwh(qh):
            rs, c0 = split(qh)
            return myF[3 * 512 + rs * 128:3 * 512 + rs * 128 + 128, c0:c0 + 512]

        def a_vrank(nm, rs):
            return myV[VU[nm] * 512 + rs * 128:VU[nm] * 512 + rs * 128 + 128, :]

        def a_vwh(qh):
            rs, tl0 = split(qh)
            return myV[512 + rs * 128:512 + rs * 128 + 128, tl0:tl0 + 512]

        def b_og(src):
            def f(rh, q0):
                tb, c0 = split(q0)
                return src[(tb * 4 + rh) * 128:(tb * 4 + rh + 1) * 128, c0:c0 + 512]
            return f

        def b_oT(dst):
            def f(c, t0, Tn):
                g, rh = c // 4, c % 4
                return dst[rh * 512 + g * 128:rh * 512 + g * 128 + 128, t0:t0 + Tn]
            return f

        io = dict(q=a_q, g=a_g, feat=a_feat, kwh=a_kwh, vrank=a_vrank, vwh=a_vwh, og=b_og(B_src), cbf=a_cbf, ab=a_ab,
                  w1=a_w1, w2=a_w2, peT=a_peT, pos_cmp=a_posc, invsign=a_inv128)
        out_exchange(cx, io, B_src, B_dst, myO, rv, lambda: nsa_attn_phase(cx, cm, io, SEQ_, NTC))
        with cx.scope():
            outproj_phase(cx, cm, b_oT(myO), a_now, xs_[0], xs_[1], NTC, 2048, DM, 1.0)
        with cx.scope():
            ffn_phase(cx, cm, xs_[1], xs_[2], fw[1][0], fw[1][1], fw[1][2], NTC, DM, DFF, 1e-6)
        with cx.scope():
            ffn_phase(cx, cm, xs_[2], xs_[3], fw[2][0], fw[2][1], fw[2][2], NTC, DM, DFF, 1e-6)
        VV = C_src[41 * 128:57 * 128, :].rearrange("(h p) c -> h p c", h=16)
        mo = dict(qnT=C_src[0:2048, :], qrT=C_src[2048:3072, :], knT=C_src[3072:5120, :], krT=C_src[5120:5184, :],
                  v=lambda t0, hq: VV[hq * 4:(hq + 1) * 4, :, t0:t0 + 128].rearrange("h p d -> p h d"))
        with cx.scope():
            mla_proj_phase(cx, cm, xs_[3], a_mw1, a_mwi, a_qn, a_kvn, a_uq, a_ukv, a_pos, a_inv64, mo, NTC, DM)
        exchange(cx, C_src, [(0, 16, Cqn), (16, 8, Cqr), (24, 16, Ckn), (40, 1, Ckr), (41, 16, Cv)],
                 [(myQn[:, :], lambda: Cqn[bass.ds(rv() * 2048, 2048), :], 0), (myQr[:, :], lambda: Cqr[bass.ds(rv() * 1024, 1024), :], 1),
                  (myKn[:, :], lambda: Ckn[bass.ds(rv() * 2048, 2048), :], 2), (myV2[:, :], lambda: Cv[bass.ds(rv() * 2048, 2048), :], 4)])

        def c_qn(hl, q0):
            rs, c0 = split(q0)
            return myQn[hl * 512 + rs * 128:hl * 512 + rs * 128 + 128, c0:c0 + 512]

        def c_qr(hl, q0):
            rs, c0 = split(q0)
            r0_ = (hl // 2) * 512 + rs * 128 + (hl % 2) * 64
            return myQr[r0_:r0_ + 64, c0:c0 + 512]

        def c_kn(hl, rs):
            return myKn[hl * 512 + rs * 128:hl * 512 + rs * 128 + 128, :]

        def c_kr(rs):
            return Ckr[rs * 128:rs * 128 + 64, :]

        def c_v(hl, rs):
            return myV2[hl * 512 + rs * 128:hl * 512 + rs * 128 + 128, :]

        io2 = dict(qn=c_qn, qr=c_qr, kn=c_kn, kr=c_kr, v=c_v, o=b_og(D_src), tri=a_cbf[:, CB_TRI:CB_TRI + 128])
        out_exchange(cx, io2, D_src, D_dst, myO2, rv, lambda: mla_attn_phase(cx, cm, io2, SEQ_, 4, NTC))
        with cx.scope():
            outproj_phase(cx, cm, b_oT(myO2), a_mow, xs_[3], xs_[4], NTC, 2048, DM, 1.0)
        with cx.scope():
            ffn_phase(cx, cm, xs_[4], xs_[5], fw[3][0], fw[3][1], fw[3][2], NTC, DM, DFF, 1e-6)
        with cx.scope():
            final_norm_phase(cx, cm, xs_[5], a_out, a_fn, NTC, DM)
        cx.emit()
    return nc


def kernel(x, positions, ffn_norm_w, ffn_w_in, ffn_w_out, mix_norm_w, nsa_w_in, nsa_cmp_pe, nsa_cmp_w1, nsa_cmp_w2,
           nsa_w_out, mla_w_in, mla_q_norm_w, mla_kv_norm_w, mla_w_uq, mla_w_ukv, mla_w_out, final_norm_w):
    f = lambda a: np.ascontiguousarray(np.asarray(a))
    x = f(x); positions = f(positions).astype(np.int32)
    ffn_norm_w, ffn_w_in, ffn_w_out, mix_norm_w = f(ffn_norm_w), f(ffn_w_in), f(ffn_w_out), f(mix_norm_w)
    nc = build_program()
    shared = {"inv128": invsign_array(128), "inv64": invsign_array(64), "consts": consts_array(), "cbf": nsa_consts_bf16(),
              "ab": nsa_ab_table(SEQ_), "mw_rep0": rep128(mix_norm_w[0]), "mw_rep1": rep128(mix_norm_w[1]),
              "nsa_w_in": f(nsa_w_in)[0], "w1": f(nsa_cmp_w1)[0], "w2": f(nsa_cmp_w2)[0],
              "peT": np.ascontiguousarray(f(nsa_cmp_pe)[0].transpose(0, 2, 1)), "nsa_w_out": f(nsa_w_out)[0],
              "mla_w_in": f(mla_w_in)[0], "qn_rep": rep128(f(mla_q_norm_w)[0]), "kvn_rep": rep128(f(mla_kv_norm_w)[0]),
              "w_uq": f(mla_w_uq)[0], "w_ukv": f(mla_w_ukv)[0], "mla_w_out": f(mla_w_out)[0], "fn_rep": rep128(f(final_norm_w))}
    for k, (i, j) in enumerate([(0, 0), (0, 1), (1, 0), (1, 1)]):
        shared["fw_rep%d" % k] = rep128(ffn_norm_w[i, j])
        shared["f_w_in%d" % k] = ffn_w_in[i, j]
        shared["f_w_out%d" % k] = ffn_w_out[i, j]
    ncmp = (SEQ_ - 32) // 16 + 1
    maps = []
    for c in range(NCORE):
        b, r = c // 4, c % 4
        posc = np.zeros(1024, np.int32)
        posc[:ncmp] = positions[b, 16 * np.arange(ncmp) + 31]
        m = dict(shared)
        m["x"] = np.ascontiguousarray(x[b, r * NTC:(r + 1) * NTC])
        m["rank"] = np.array([[r, 0]], np.int32)
        m["pos"] = np.ascontiguousarray(np.tile(positions[b, r * NTC:(r + 1) * NTC][None, :], (128, 1)))
        m["pos_cmp"] = np.ascontiguousarray(np.tile(posc[None], (128, 1)))
        maps.append(m)
    res = run_bass_kernel_spmd(nc, maps, core_ids=list(range(NCORE)))
    out = np.empty((2, SEQ_, DM), np.float32)
    for c in range(NCORE):
        b, r = c // 4, c % 4
        out[b, r * NTC:(r + 1) * NTC] = np.asarray(res.results[c]["out"])
    return out
```
